# Optimizing a Trainium2 kernel written in Bass

```python
import math
import jax, jax.numpy as jnp
from jax import lax
import numpy as np

D_MODEL = 1024
BATCH = 32
SEQ = 256
DEPTH = 2
DEC_BATCH = 4
DEC_SEQ = 4096
PAST_LEN = 256

GRID_W = 64
N_EVEN = (DEPTH + 1) // 2
N_ODD = DEPTH // 2
N_MOD = 9
EPS = 1e-6
D_FF = 2816
ROPE_BASE = 10000.0
RET_HEADS = 4
RET_DK = 64
RET_DV = 128
RET_CHUNK = 128
SGU_GROUPS = 4
SGU_CH = 128
SGU_CHUNK = 128
DIFF_HEADS = 8
DIFF_DH = 64
Q_BLOCK = 128

EVEN_IN = 2 * RET_HEADS * RET_DK + 2 * RET_HEADS * RET_DV + 2 * SGU_GROUPS * SGU_CH
EVEN_MIX = RET_HEADS * RET_DV + SGU_GROUPS * SGU_CH
ODD_IN = 3 * DIFF_HEADS * 2 * DIFF_DH
ODD_MIX = DIFF_HEADS * 2 * DIFF_DH

kernel_name = "hybrid_diffusion_retention_sgu_diffattn_step"


def rms_norm(x, gain=None):
    xf = x.astype(jnp.float32)
    y = xf * lax.rsqrt(jnp.mean(xf * xf, axis=-1, keepdims=True) + EPS)
    if gain is not None:
        y = y * gain.astype(jnp.float32)
    return y.astype(x.dtype)


def modulate(h, shift, scale):
    return h * (1.0 + scale) + shift


def swiglu(h, w_in, w_out):
    g, u = jnp.split(h @ w_in, 2, axis=-1)
    return (jax.nn.silu(g) * u) @ w_out


def axial_rope(x):
    L, d = x.shape[1], x.shape[-1]
    n_rows = L // GRID_W
    row = jnp.repeat(jnp.arange(n_rows, dtype=jnp.float32), GRID_W)
    col = jnp.tile(jnp.arange(GRID_W, dtype=jnp.float32), n_rows)
    half = d // 2
    quarter = half // 2
    freq = ROPE_BASE ** (-jnp.arange(quarter, dtype=jnp.float32) / quarter)
    bshape = (1, L) + (1,) * (x.ndim - 3) + (quarter,)

    def rotate(xa, pos):
        ang = pos[:, None] * freq
        cos = jnp.cos(ang).reshape(bshape).astype(x.dtype)
        sin = jnp.sin(ang).reshape(bshape).astype(x.dtype)
        x1, x2 = xa[..., :quarter], xa[..., quarter:]
        return jnp.concatenate([x1 * cos - x2 * sin, x2 * cos + x1 * sin], axis=-1)

    return jnp.concatenate([rotate(x[..., :half], row), rotate(x[..., half:], col)], axis=-1)


def retention_dir(q, k, v, log_gamma, s0):
    B, L, H, DK = q.shape
    DV = v.shape[-1]
    C = RET_CHUNK
    n = L // C
    qc = q.astype(jnp.float32).reshape(B, n, C, H, DK)
    kc = k.astype(jnp.float32).reshape(B, n, C, H, DK)
    vc = v.astype(jnp.float32).reshape(B, n, C, H, DV)
    pos = jnp.arange(C, dtype=jnp.float32)
    lg = log_gamma[:, None]
    decay_q = jnp.exp(lg * (pos + 1.0))
    decay_k = jnp.exp(lg * (C - 1.0 - pos))
    diff = pos[:, None] - pos[None, :]
    dmat = jnp.where(diff >= 0, jnp.exp(lg[:, :, None] * jnp.maximum(diff, 0.0)), 0.0)
    chunk_decay = jnp.exp(log_gamma * C)[None, :, None, None]
    scores = jnp.einsum('bnqhd,bnkhd->bnhqk', qc, kc) * dmat
    o_intra = jnp.einsum('bnhqk,bnkhe->bnqhe', scores, vc)
    upd = jnp.einsum('bnkhd,hk,bnkhe->bnhde', kc, decay_k, vc)

    def step(S, u):
        return chunk_decay * S + u, S

    s_final, s_prev = lax.scan(step, s0.astype(jnp.float32), jnp.moveaxis(upd, 1, 0))
    s_prev = jnp.moveaxis(s_prev, 0, 1)
    o_inter = jnp.einsum('bnqhd,hq,bnhde->bnqhe', qc, decay_q, s_prev)
    return (o_intra + o_inter).reshape(B, L, H, DV), s_final


def sgu_mixer(u, v, w_s, b_s):
    B, L, _ = u.shape
    n = L // SGU_CHUNK
    vn = rms_norm(v.reshape(B, n, SGU_CHUNK, SGU_GROUPS, SGU_CH))
    mixed = jnp.einsum('gpm,bkmgc->bkpgc', w_s, vn) + b_s.T[None, None, :, :, None]
    return u * mixed.reshape(B, L, SGU_GROUPS * SGU_CH)


def even_mixer(h, w_in, w_out, decay_fwd, decay_bwd, w_s, b_s, s0_f, s0_b, latent):
    B, L, _ = h.shape
    qk_w = RET_HEADS * RET_DK
    v_w = RET_HEADS * RET_DV
    g_w = SGU_GROUPS * SGU_CH
    splits = [qk_w, 2 * qk_w, 2 * qk_w + v_w, 2 * qk_w + 2 * v_w, 2 * qk_w + 2 * v_w + g_w]
    q, k, v, g, u, vs = jnp.split(h @ w_in, splits, axis=-1)
    q = q.reshape(B, L, RET_HEADS, RET_DK) * (RET_DK ** -0.5)
    k = k.reshape(B, L, RET_HEADS, RET_DK)
    v = v.reshape(B, L, RET_HEADS, RET_DV)
    if latent:
        q, k = axial_rope(q), axial_rope(k)
    lg_f = jax.nn.log_sigmoid(decay_fwd.astype(jnp.float32))
    lg_b = jax.nn.log_sigmoid(decay_bwd.astype(jnp.float32))
    o_f, s_f = retention_dir(q, k, v, lg_f, s0_f)
    o_b, s_b = retention_dir(q[:, ::-1], k[:, ::-1], v[:, ::-1], lg_b, s0_b)
    ret = rms_norm(o_f + o_b[:, ::-1]).astype(h.dtype).reshape(B, L, v_w)
    ret = jax.nn.silu(g) * ret
    sgu = sgu_mixer(jax.nn.gelu(u), jax.nn.gelu(vs), w_s, b_s)
    out = jnp.concatenate([ret, sgu], axis=-1) @ w_out
    return out, s_f, s_b


def diff_attention(q, k, v, lam):
    B, L, H = q.shape[0], q.shape[1], q.shape[2]
    nb = L // Q_BLOCK
    scale = DIFF_DH ** -0.5
    qb = jnp.moveaxis(q.reshape(B, nb, Q_BLOCK, H, 2, DIFF_DH), 1, 0)

    def block(qblk):
        s = jnp.einsum('bqhjd,bkhjd->bhjqk', qblk, k).astype(jnp.float32) * scale
        p = jax.nn.softmax(s, axis=-1)
        a = p[:, :, 0] - lam * p[:, :, 1]
        return jnp.einsum('bhqk,bkhe->bqhe', a.astype(v.dtype), v)

    o = lax.map(block, qb)
    return jnp.moveaxis(o, 0, 1).reshape(B, L, H, 2 * DIFF_DH)


def odd_mixer(h, w_in, w_out, lam_p, subln, lambda_init, k_ctx, v_ctx, latent):
    B, L, _ = h.shape
    qk_w = DIFF_HEADS * 2 * DIFF_DH
    q, k, v = jnp.split(h @ w_in, [qk_w, 2 * qk_w], axis=-1)
    q = q.reshape(B, L, DIFF_HEADS, 2, DIFF_DH)
    k = k.reshape(B, L, DIFF_HEADS, 2, DIFF_DH)
    v = v.reshape(B, L, DIFF_HEADS, 2 * DIFF_DH)
    lp = lam_p.astype(jnp.float32)
    lam = jnp.exp(jnp.sum(lp[0] * lp[1])) - jnp.exp(jnp.sum(lp[2] * lp[3])) + lambda_init
    if latent:
        q, k_lat = axial_rope(q), axial_rope(k)
        P = k_ctx.shape[1]
        k_all = jnp.concatenate([k_ctx.reshape(B, P, DIFF_HEADS, 2, DIFF_DH).astype(h.dtype), k_lat], axis=1)
        v_all = jnp.concatenate([v_ctx.astype(h.dtype), v], axis=1)
    else:
        k_all, v_all = k, v
    o = diff_attention(q, k_all, v_all, lam)
    o = rms_norm(o, subln) * (1.0 - lambda_init)
    out = o.reshape(B, L, ODD_MIX) @ w_out
    return out, k.reshape(B, L, DIFF_HEADS, 2 * DIFF_DH), v


def run_trunk(x, cond, latent, s_fwd, s_bwd, k_cache, v_cache, weights):
    (ada_w, ada_b, norm_g, ffn_w_in, ffn_w_out, even_w_in, even_w_out, ret_decay_fwd,
     ret_decay_bwd, sgu_w, sgu_b, odd_w_in, odd_w_out, diff_lambda, diff_subln, final_norm) = weights
    B = x.shape[0]
    new_f, new_b, new_k, new_v = [], [], [], []
    for i in range(DEPTH):
        mod = (jax.nn.silu(cond) @ ada_w[i] + ada_b[i]).reshape(-1, N_MOD, D_MODEL)[:, :, None, :]
        h = modulate(rms_norm(x, norm_g[i, 0]), mod[:, 0], mod[:, 1])
        x = x + 0.5 * mod[:, 2] * swiglu(h, ffn_w_in[i, 0], ffn_w_out[i, 0])
        h = modulate(rms_norm(x, norm_g[i, 1]), mod[:, 3], mod[:, 4])
        if i % 2 == 0:
            e = i // 2
            if latent:
                s0_f, s0_b = s_fwd[:, e], s_bwd[:, e]
            else:
                s0_f = jnp.zeros((B, RET_HEADS, RET_DK, RET_DV), jnp.float32)
                s0_b = jnp.zeros((B, RET_HEADS, RET_DK, RET_DV), jnp.float32)
            out, sf, sb = even_mixer(h, even_w_in[e], even_w_out[e], ret_decay_fwd[e], ret_decay_bwd[e],
                                     sgu_w[e], sgu_b[e], s0_f, s0_b, latent)
            if not latent:
                new_f.append(sf)
                new_b.append(sb)
        else:
            o_idx = i // 2
            lambda_init = 0.8 - 0.6 * math.exp(-0.3 * i)
            kc = k_cache[:, o_idx] if latent else None
            vc = v_cache[:, o_idx] if latent else None
            out, kk, vv = odd_mixer(h, odd_w_in[o_idx], odd_w_out[o_idx], diff_lambda[o_idx],
                                    diff_subln[o_idx], lambda_init, kc, vc, latent)
            if not latent:
                new_k.append(kk)
                new_v.append(vv)
        x = x + mod[:, 5] * out
        h = modulate(rms_norm(x, norm_g[i, 2]), mod[:, 6], mod[:, 7])
        x = x + 0.5 * mod[:, 8] * swiglu(h, ffn_w_in[i, 1], ffn_w_out[i, 1])
    return rms_norm(x, final_norm), new_f, new_b, new_k, new_v


def setup_inputs(seed: int = 0) -> dict:
    key = jax.random.key(seed)
    ks = jax.random.split(key, 24)

    def nrm(k, shape, s):
        return jax.random.normal(k, shape, jnp.float32) * s

    p = 1.0 - 2.0 ** (-5.0 - jnp.arange(RET_HEADS, dtype=jnp.float32))
    base_logit = jnp.log(p) - jnp.log1p(-p)
    return {
        'x_prompt': nrm(ks[0], (BATCH, SEQ, D_MODEL), 1.0),
        'x_sample': nrm(ks[1], (DEC_BATCH, DEC_SEQ, D_MODEL), 1.0),
        'state_ret_fwd': nrm(ks[2], (DEC_BATCH, N_EVEN, RET_HEADS, RET_DK, RET_DV), 0.5),
        'state_ret_bwd': nrm(ks[3], (DEC_BATCH, N_EVEN, RET_HEADS, RET_DK, RET_DV), 0.5),
        'cache_k': nrm(ks[4], (DEC_BATCH, N_ODD, PAST_LEN, DIFF_HEADS, 2 * DIFF_DH), 1.0),
        'cache_v': nrm(ks[5], (DEC_BATCH, N_ODD, PAST_LEN, DIFF_HEADS, 2 * DIFF_DH), 1.0),
        'c': nrm(ks[6], (DEC_BATCH, D_MODEL), 1.0),
        'c_ctx': nrm(ks[7], (D_MODEL,), 1.0),
        'ada_w': nrm(ks[8], (DEPTH, D_MODEL, N_MOD * D_MODEL), 0.5 * D_MODEL ** -0.5),
        'ada_b': nrm(ks[9], (DEPTH, N_MOD * D_MODEL), 0.02),
        'norm_g': 1.0 + nrm(ks[10], (DEPTH, 3, D_MODEL), 0.02),
        'ffn_w_in': nrm(ks[11], (DEPTH, 2, D_MODEL, 2 * D_FF), D_MODEL ** -0.5),
        'ffn_w_out': nrm(ks[12], (DEPTH, 2, D_FF, D_MODEL), D_FF ** -0.5),
        'even_w_in': nrm(ks[13], (N_EVEN, D_MODEL, EVEN_IN), D_MODEL ** -0.5),
        'even_w_out': nrm(ks[14], (N_EVEN, EVEN_MIX, D_MODEL), EVEN_MIX ** -0.5),
        'ret_decay_fwd': base_logit + nrm(ks[15], (N_EVEN, RET_HEADS), 0.1),
        'ret_decay_bwd': base_logit + nrm(ks[16], (N_EVEN, RET_HEADS), 0.1),
        'sgu_w': nrm(ks[17], (N_EVEN, SGU_GROUPS, SGU_CHUNK, SGU_CHUNK), SGU_CHUNK ** -0.5),
        'sgu_b': 1.0 + nrm(ks[18], (N_EVEN, SGU_GROUPS, SGU_CHUNK), 0.1),
        'odd_w_in': nrm(ks[19], (N_ODD, D_MODEL, ODD_IN), D_MODEL ** -0.5),
        'odd_w_out': nrm(ks[20], (N_ODD, ODD_MIX, D_MODEL), ODD_MIX ** -0.5),
        'diff_lambda': nrm(ks[21], (N_ODD, 4, DIFF_DH), 0.1),
        'diff_subln': 1.0 + nrm(ks[22], (N_ODD, 2 * DIFF_DH), 0.02),
        'final_norm': 1.0 + nrm(ks[23], (D_MODEL,), 0.02),
    }


def reference(x_prompt, x_sample, state_ret_fwd, state_ret_bwd, cache_k, cache_v, c, c_ctx,
              ada_w, ada_b, norm_g, ffn_w_in, ffn_w_out, even_w_in, even_w_out, ret_decay_fwd,
              ret_decay_bwd, sgu_w, sgu_b, odd_w_in, odd_w_out, diff_lambda, diff_subln, final_norm):
    weights = (ada_w, ada_b, norm_g, ffn_w_in, ffn_w_out, even_w_in, even_w_out, ret_decay_fwd,
               ret_decay_bwd, sgu_w, sgu_b, odd_w_in, odd_w_out, diff_lambda, diff_subln, final_norm)
    y_prompt, pf, pb, pk, pv = run_trunk(x_prompt, c_ctx[None, :], False, None, None, None, None, weights)
    y_sample, _, _, _, _ = run_trunk(x_sample, c, True, state_ret_fwd, state_ret_bwd, cache_k, cache_v, weights)
    new_ret_fwd = jnp.stack(pf, axis=1)
    new_ret_bwd = jnp.stack(pb, axis=1)
    new_cache_k = jnp.stack(pk, axis=1)
    new_cache_v = jnp.stack(pv, axis=1)
    return (y_prompt, y_sample, new_ret_fwd, new_ret_bwd, new_cache_k, new_cache_v)
```

```python
import math
import contextlib
import numpy as np
from concourse.bass_utils import run_bass_kernel_spmd
import concourse.bass as bass
import concourse.mybir as mybir

F32 = mybir.dt.float32
BF16 = mybir.dt.bfloat16
AF = mybir.ActivationFunctionType
ALU = mybir.AluOpType
AX = mybir.AxisListType

EPOCH = 30000
COMPUTE = ("pe", "act", "dve", "pool")
NSLOT = {"sp": 12, "pool": 12, "act": 4}


class T:
    __slots__ = ("ap", "wr", "rd", "name")

    def __init__(self, ap, name=""):
        self.ap = ap
        self.wr = {}
        self.rd = {}
        self.name = name

    def inherit(self, *others):
        for o in others:
            for k, v in o.wr.items():
                if k not in self.rd or self.rd[k].seq < v.seq:
                    self.rd[k] = v
            for k, v in o.rd.items():
                if k not in self.rd or self.rd[k].seq < v.seq:
                    self.rd[k] = v
        return self


class Op:
    __slots__ = ("eng", "fn", "deps", "signal", "val", "stream", "seq", "is_dma", "slot", "inc")

    def __init__(self, eng, fn, stream, is_dma, slot=None):
        self.eng = eng
        self.fn = fn
        self.deps = []
        self.signal = False
        self.val = None
        self.stream = stream
        self.is_dma = is_dma
        self.slot = slot
        self.seq = 0
        self.inc = 16


class Prog:
    def __init__(self, nc):
        self.nc = nc
        self.ops = {e: [] for e in ("pe", "act", "dve", "pool", "sp")}
        self.seq = 0
        self.slot_rr = {q: 0 for q in NSLOT}
        self.slot_last = {}
        self.all_ops = []

    def _deps(self, op, reads, writes):
        deps = {}
        for t in reads:
            for k, v in t.wr.items():
                deps[id(v)] = v
        for t in writes:
            for k, v in t.wr.items():
                deps[id(v)] = v
            for k, v in t.rd.items():
                deps[id(v)] = v
        for v in deps.values():
            if v is op:
                continue
            if (not v.is_dma) and (not op.is_dma) and v.stream == op.stream and op.eng == "pe":
                continue
            op.deps.append(v)
            v.signal = True
        for t in reads:
            t.rd[op.stream] = op
        for t in writes:
            t.wr = {op.stream: op}
            t.rd = {}

    def op(self, eng, fn, reads=(), writes=()):
        o = Op(eng, fn, eng, False)
        self.seq += 1
        o.seq = self.seq
        self._deps(o, reads, writes)
        self.ops[eng].append(o)
        self.all_ops.append(o)
        return o

    def dma(self, q, out_ap, in_ap, reads=(), writes=(), **kw):
        slot = self.slot_rr[q]
        self.slot_rr[q] = (slot + 1) % NSLOT[q]
        stream = ("dma", q, slot)

        def fn(e, out_ap=out_ap, in_ap=in_ap, kw=kw):
            return e.dma_start(out=out_ap, in_=in_ap, allow_slow_non_contiguous=True, **kw)

        o = Op(q, fn, stream, True, slot)
        o.signal = True
        self.seq += 1
        o.seq = self.seq
        prev = self.slot_last.get(stream)
        if prev is not None:
            o.deps.append(prev)
        self.slot_last[stream] = o
        self._deps(o, reads, writes)
        self.ops[q].append(o)
        self.all_ops.append(o)
        return o

    def cc(self, kind, op, groups, in_ap, out_ap, reads=(), writes=()):
        stream = ("dma", "cc", 0)

        def fn(e):
            return e.collective_compute(kind, op, replica_groups=groups, ins=[in_ap], outs=[out_ap])

        o = Op("pool", fn, stream, True, 0)
        o.inc = 1
        o.signal = True
        self.seq += 1
        o.seq = self.seq
        prev = self.slot_last.get(stream)
        if prev is not None:
            o.deps.append(prev)
        self.slot_last[stream] = o
        self._deps(o, reads, writes)
        self.ops["pool"].append(o)
        self.all_ops.append(o)
        return o

    def fence(self, eng, ts):
        def fn(e):
            return None
        o = Op(eng, fn, eng, False)
        self.seq += 1
        o.seq = self.seq
        seen = {}
        for t in ts:
            for v in list(t.wr.values()) + list(t.rd.values()):
                seen[id(v)] = v
        for v in seen.values():
            o.deps.append(v)
            v.signal = True
        self.ops[eng].append(o)
        self.all_ops.append(o)
        return o

    def emit(self):
        import contextlib
        nc = self.nc
        cnt = {e: 0 for e in COMPUTE}
        dcnt = {}
        for o in self.all_ops:
            if o.is_dma:
                dcnt[o.stream] = dcnt.get(o.stream, 0) + 1
                o.val = dcnt[o.stream]
            elif o.signal:
                cnt[o.eng] += 1
                o.val = cnt[o.eng]
        DEP = EPOCH // 16
        stack = contextlib.ExitStack()
        sems = {}
        for e in COMPUTE:
            for ep in range((max(cnt[e], 1) - 1) // EPOCH + 1):
                sems[(e, ep)] = stack.enter_context(nc.semaphore(f"s_{e}_{ep}"))
        for st in dcnt:
            for ep in range((dcnt[st] - 1) // DEP + 1):
                sems[(st, ep)] = stack.enter_context(nc.semaphore(f"d_{st[1]}_{st[2]}_{ep}"))
        self.stats = {e: len(self.ops[e]) for e in self.ops}
        self.stats["sems"] = len(sems)
        self.stats["signals"] = dict(cnt)

        def semof(o):
            n = o.val - 1
            if o.is_dma:
                ep = n // DEP
                return sems[(o.stream, ep)], (n % DEP + 1) * o.inc
            ep = n // EPOCH
            return sems[(o.stream, ep)], n % EPOCH + 1

        block = stack.enter_context(nc.Block())
        ops = self.ops

        def run(engname):
            def body(e):
                waited = {}
                nwait = 0
                for o in ops[engname]:
                    for d in o.deps:
                        key = d.stream
                        if waited.get(key, 0) >= d.val:
                            continue
                        waited[key] = d.val
                        s, v = semof(d)
                        e.wait_ge(s, v)
                        nwait += 1
                    ins = o.fn(e)
                    if ins is None:
                        continue
                    if o.is_dma:
                        s, v = semof(o)
                        ins.then_inc(s, o.inc)
                    elif o.signal:
                        s, v = semof(o)
                        ins.then_inc(s, 1)
                self.stats["waits_" + engname] = nwait
            return body

        block.tensor(run("pe"))
        block.scalar(run("act"))
        block.vector(run("dve"))
        block.gpsimd(run("pool"))
        block.sync(run("sp"))
        stack.close()


C = 128
EPS = 1e-6
D = 1024
KC = 8
DFF = 2816
HC = 22
ARENA_B = 47104
LAMBDA_INIT = 0.8 - 0.6 * math.exp(-0.3 * 1)


class StopBuild(Exception):
    pass


def build(NPS, LS, PAST, stop_after=None, ck_stop=None):
    def ck(i):
        if ck_stop is not None and i == ck_stop:
            raise StopBuild()

    nc = bass.Bass("TRN2", target_bir_lowering=False)
    st = contextlib.ExitStack()
    P = Prog(nc)
    NPT = NPS * 256

    def din(name, shape, dt=F32):
        return nc.dram_tensor(name, list(shape), dt, kind="ExternalInput").ap()

    def dout(name, shape, dt=F32):
        return nc.dram_tensor(name, list(shape), dt, kind="ExternalOutput").ap()

    def dscr(name, shape, dt):
        return nc.dram_tensor(name, list(shape), dt).ap()

    def sb(name, shape, dt=F32):
        return st.enter_context(nc.sbuf_tensor(name, list(shape), dt))

    xp = din("x_prompt", [NPT, D])
    xs_ = din("x_sample", [LS, D])
    PAIRS = [[0, 1], [2, 3], [4, 5], [6, 7]]
    s0f = din("state_f", [4, 64, 128])
    s0b = din("state_b", [4, 64, 128])
    cache_k = din("cache_k", [PAST, D])
    cache_v = din("cache_v", [PAST, D])
    cond = din("cond", [2, D])
    ada_w = din("ada_w", [2, D, 9 * D])
    ada_b = din("ada_b", [2, 9 * D])
    norm_g = din("norm_g", [6, D])
    ffn_w_in = din("ffn_w_in", [2, 2, D, 2 * DFF])
    ffn_w_out = din("ffn_w_out", [2, 2, DFF, D])
    even_w_in = din("even_w_in", [D, 2560])
    even_w_out = din("even_w_out", [D, D])
    rdf = din("ret_decay_fwd", [1, 4])
    rdb = din("ret_decay_bwd", [1, 4])
    sgu_w = din("sgu_w", [4, C, C])
    sgu_b = din("sgu_b", [4, C])
    odd_w_in = din("odd_w_in", [D, 3072])
    odd_w_out = din("odd_w_out", [D, D])
    diff_lambda = din("diff_lambda", [4, 64])
    diff_subln = din("diff_subln", [1, C])
    final_norm = din("final_norm", [1, D])
    c_ident = din("c_ident", [C, C])
    c_prot = din("c_prot", [C, C])
    c_cos = din("c_cos", [C, LS])
    c_sin = din("c_sin", [C, LS])
    c_tabs = din("c_tabs", [C, 5, C])
    c_mcol = din("c_mcol", [C, 4])
    c_flag = din("c_flag", [C, 4])

    y_p = dout("y_prompt", [NPT, D])
    y_s = dout("y_sample", [LS, D])
    o_rf = dout("new_ret_f", [NPS, 4, 64, 128])
    o_rb = dout("new_ret_b", [NPS, 4, 64, 128])
    o_k = dout("new_k", [NPT, D])
    o_v = dout("new_v", [NPT, D])

    groups = []
    for gi, (nm, latent, ntok, seqlen, xin, yout) in enumerate(
            [("p", False, NPT, 256, xp, y_p), ("s", True, LS, LS, xs_, y_s)]):
        g = dict(name=nm, latent=latent, ntok=ntok, seqlen=seqlen, xin=xin, yout=yout, ci=gi)
        tiles = []
        t0 = 0
        while t0 < ntok:
            n = min(1024, ntok - t0)
            tiles.append((t0, n // 512))
            t0 += n
        g["tiles"] = tiles
        nt = len(tiles)
        nch = ntok // C
        g["nch"] = nch
        nk = ntok
        g["nk"] = nk
        g["X1"] = dscr(f"X1_{nm}", [nt, C, KC, 1024], F32)
        g["X4"] = dscr(f"X4_{nm}", [nt, C, KC, 1024], F32)
        g["QFM"] = dscr(f"QFM_{nm}", [nt, C, KC, 2, 1024], BF16)
        g["KFM"] = dscr(f"KFM_{nm}", [8, C, nk], BF16)
        g["VTM"] = dscr(f"VTM_{nm}", [nk // C, C, D], BF16)
        g["SPF"] = dscr(f"SPF_{nm}", [nch, 64, 512], BF16)
        g["SPB"] = dscr(f"SPB_{nm}", [nch, 64, 512], BF16)
        g["UPDB"] = dscr(f"UPDB_{nm}", [nch, 64, 512], F32)
        g["UPDF"] = dscr(f"UPDF_{nm}", [nch, 64, 512], F32)
        g["tUPDF"] = [T(None) for _ in range(nch)]
        g["tX1"] = [T(None) for _ in range(nt)]
        g["tX4"] = [T(None) for _ in range(nt)]
        g["tQ"] = [T(None) for _ in range(nt)]
        g["tK"] = [[T(None) for _ in range(nt)] for _ in range(8)]
        g["tV"] = [T(None) for _ in range(nk // C)]
        g["tSPF"] = [T(None) for _ in range(nch)]
        g["tSPB"] = [T(None) for _ in range(nch)]
        g["tUPDB"] = [T(None) for _ in range(nch)]
        groups.append(g)

    LH = LS
    KCTX = dscr("KCTX", [8, C, PAST], BF16); tKCTX = T(None)
    VCTX = dscr("VCTX", [PAST // C, C, D], BF16); tVCTX = T(None)
    NSPL = max(1, (8 * C * LH * 2) // (2 << 20))
    HPS = 8 // NSPL
    CPS = (LH // C) // NSPL
    KALL = [dscr(f"KALL{i}", [2 * HPS * C, LH], BF16) for i in range(NSPL)]; tKALL = T(None)
    VALL = [dscr(f"VALL{i}", [2 * CPS * C, D], BF16) for i in range(NSPL)]; tVALL = T(None)
    CSEND = dscr("CSEND", [128, 512], F32); tCSEND = T(None)
    CALL = dscr("CALL", [256, 512], F32); tCALL = T(None)

    X = sb("X", [C, KC, 1024], F32)
    H = sb("H", [C, KC, 1024], BF16)
    ARENA = sb("ARENA", [C, ARENA_B], mybir.dt.uint8)
    xT = [T(None, "x0"), T(None, "x1")]
    hT = [T(None, "h0"), T(None, "h1")]
    WB = [sb(f"WB{i}", [C, 8192], BF16) for i in range(3)]
    wbT = [T(None, f"wb{i}") for i in range(3)]
    PS = [st.enter_context(nc.psum_tensor(f"PS{i}", [C, 512], F32)) for i in range(8)]
    psT = [T(None, f"ps{i}") for i in range(8)]
    ps_rr = [0]

    ps_res = set()

    def ps():
        while True:
            i = ps_rr[0]
            ps_rr[0] = (i + 1) % 8
            if i not in ps_res:
                return PS[i], psT[i]

    def ps_reserve(n):
        out = []
        for _ in range(n):
            while True:
                i = ps_rr[0]
                ps_rr[0] = (i + 1) % 8
                if i not in ps_res:
                    break
            ps_res.add(i)
            out.append((PS[i], psT[i], i))
        return out

    def ps_release(lst):
        for _, _, i in lst:
            ps_res.discard(i)

    XST = [sb(f"XST{i}", [C, D], F32) for i in range(2)]; tXST = [T(None), T(None)]
    ONES = sb("ONES", [C, C], BF16); tONES = T(None)
    IDENT = sb("IDENT", [C, C], F32); tIDENT = T(None)
    PROT = sb("PROT", [C, C], F32); tPROT = T(None)
    TABS = sb("TABS", [C, 5, C], F32); tTABS = T(None)
    MCOL = sb("MCOL", [C, 4], F32); tMCOL = T(None)
    P.op("dve", lambda e: e.memset(ONES[:], 1.0), [], [tONES])
    P.dma("sp", IDENT[:], c_ident, writes=[tIDENT])
    P.dma("sp", PROT[:], c_prot, writes=[tPROT])
    P.dma("sp", TABS[:], c_tabs, writes=[tTABS])
    P.dma("sp", MCOL[:], c_mcol, writes=[tMCOL])
    FLAG = sb("FLAG", [C, 4], F32)
    P.dma("sp", FLAG[:], c_flag, writes=[tMCOL])

    CST = sb("CST", [C, 4], F32); tCST = T(None)
    P.op("dve", lambda e: e.memset(CST[:, 0:1], float(D * EPS)), [], [tCST])
    P.op("dve", lambda e: e.memset(CST[:, 1:2], float(C * EPS)), [], [tCST])
    P.op("dve", lambda e: e.memset(CST[:, 2:3], 1.0), [], [tCST])
    P.op("dve", lambda e: e.memset(CST[:, 3:4], float(EPS)), [], [tCST])

    def mm(out, lhsT, rhs, start, stop, reads, writes):
        return P.op("pe", lambda e: e.matmul(out, lhsT=lhsT, rhs=rhs, start=start, stop=stop), reads, writes)

    def tr(out, in_, reads, writes):
        return P.op("pe", lambda e: e.transpose(out, in_, IDENT[:]), list(reads) + [tIDENT], writes)

    def act(out, in_, func, reads, writes, bias=None, scale=None):
        kw = {}
        if bias is not None:
            kw["bias"] = bias
        if scale is not None:
            kw["scale"] = scale
        return P.op("act", lambda e: e.activation(out=out, in_=in_, func=func, **kw), reads, writes)

    def tt(out, in0, in1, op, reads, writes):
        return P.op("dve", lambda e: e.tensor_tensor(out=out, in0=in0, in1=in1, op=op), reads, writes)

    def ts(out, in0, s1, s2, op0, op1, reads, writes):
        if s2 is None:
            return P.op("dve", lambda e: e.tensor_scalar(out=out, in0=in0, scalar1=s1, scalar2=None, op0=op0), reads, writes)
        return P.op("dve", lambda e: e.tensor_scalar(out=out, in0=in0, scalar1=s1, scalar2=s2, op0=op0, op1=op1), reads, writes)

    def stt(out, in0, scalar, in1, op0, op1, reads, writes):
        return P.op("dve", lambda e: e.scalar_tensor_tensor(out=out, in0=in0, scalar=scalar, in1=in1, op0=op0, op1=op1), reads, writes)

    def recip(out, in_, reads, writes):
        return P.op("dve", lambda e: e.reciprocal(out=out, in_=in_), reads, writes)

    def cp(eng, out, in_, reads, writes):
        if eng == "act":
            return P.op("act", lambda e: e.copy(out=out, in_=in_), reads, writes)
        return P.op("dve", lambda e: e.tensor_copy(out=out, in_=in_), reads, writes)

    cp_rr = [0]

    def cpa(out, in_, reads, writes):
        cp_rr[0] ^= 1
        return cp("act" if cp_rr[0] else "dve", out, in_, reads, writes)

    ncdma = lambda: nc.allow_non_contiguous_dma(reason="small param layout")

    CONDT = sb("CONDT", [C, KC, 2], F32); tCOND = T(None)
    with ncdma():
        for c in range(2):
            P.dma("sp", CONDT[:, :, c], cond[c, :].rearrange("(k p) -> p k", p=C), writes=[tCOND])
    act(CONDT[:], CONDT[:], AF.Silu, [tCOND], [tCOND])
    MODT = sb("MODT", [C, 2, 72, 2], F32); tMOD = T(None)
    ADAB = sb("ADAB", [C, 2, 72], F32); tADAB = T(None)
    GT = sb("GT", [C, 6, KC], F32); tGT = T(None)
    FNG = sb("FNG", [C, KC], F32); tFNG = T(None)
    with ncdma():
        for l in range(2):
            P.dma("sp", ADAB[:, l, :], ada_b[l, :].rearrange("(o p) -> p o", p=C), writes=[tADAB])
        for i in range(6):
            P.dma("sp", GT[:, i, :], norm_g[i, :].rearrange("(k p) -> p k", p=C), writes=[tGT])
        P.dma("sp", FNG[:], final_norm[0, :].rearrange("(k p) -> p k", p=C), writes=[tFNG])
    AW = [ARENA[:, i * 16384:(i + 1) * 16384].bitcast(F32).rearrange("p (k m) -> p k m", k=KC) for i in range(2)]
    tAW = [T(None), T(None)]
    AV = sb("AV", [C, 2, 2, 3, KC], F32)
    GV = sb("GV", [C, 2, 2, 3, KC], F32)
    tAV = T(None)
    ada_blk = [0]

    def ada_layer(l):
        for b in range(18):
            blk = ada_blk[0]
            ada_blk[0] += 1
            buf, tb = AW[blk % 2], tAW[blk % 2]
            P.dma("sp", buf, ada_w[l, :, b * 512:(b + 1) * 512].rearrange("(k p) m -> p k m", p=C), writes=[tb])
            pt, tp = ps()
            for oc in range(4):
                for kc in range(KC):
                    mm(pt[:, oc * 2:oc * 2 + 2], buf[:, kc, oc * 128:(oc + 1) * 128], CONDT[:, kc, :],
                       kc == 0, kc == KC - 1, [tb, tCOND], [tp])
            tt(MODT[:, l, b * 4:(b + 1) * 4, :], pt[:, 0:8].rearrange("p (a c) -> p a c", c=2),
               ADAB[:, l, b * 4:(b + 1) * 4].unsqueeze(2).to_broadcast([C, 4, 2]), ALU.add, [tp, tADAB], [tMOD])
        for c in range(2):
            for n in range(3):
                ts(AV[:, l, c, n, :], MODT[:, l, (3 * n + 1) * 8:(3 * n + 2) * 8, c], 1.0, 32.0, ALU.add, ALU.mult, [tMOD], [tAV])
                tt(AV[:, l, c, n, :], AV[:, l, c, n, :], GT[:, l * 3 + n, :], ALU.mult, [tAV, tGT], [tAV])
                ts(GV[:, l, c, n, :], MODT[:, l, (3 * n + 2) * 8:(3 * n + 3) * 8, c], 0.5 if n != 1 else 1.0, None, ALU.mult, None, [tMOD], [tAV])

    ada_layer(0)
    FNA = sb("FNA", [C, KC], F32)
    ts(FNA[:], FNG[:], 32.0, None, ALU.mult, None, [tFNG], [tAV])

    def Avec(l, c, n, kc):
        return AV[:, l, c, n, kc:kc + 1]

    def Bvec(l, c, n, kc):
        return MODT[:, l, (3 * n) * 8 + kc, c:c + 1]

    def Gvec(l, c, n, kc):
        return GV[:, l, c, n, kc:kc + 1]

    LGR = sb("LGR", [C, 2, 4], F32); tLG = T(None)
    LGQ = sb("LGQ", [C, 2, 2], F32)
    with ncdma():
        for d_, src in enumerate((rdf, rdb)):
            P.dma("sp", LGR[:, d_, :], src[0, :].partition_broadcast(C), writes=[tLG])
            for c in range(2):
                for hh in range(2):
                    P.dma("sp", LGQ[hh * 64:(hh + 1) * 64, d_, c:c + 1],
                          src[0, 2 * c + hh:2 * c + hh + 1].partition_broadcast(64), writes=[tLG])
    for tbuf in (LGR, LGQ):
        act(tbuf[:], tbuf[:], AF.Exp, [tLG], [tLG], scale=-1.0)
        act(tbuf[:], tbuf[:], AF.Ln, [tLG, tCST], [tLG], bias=CST[:, 2:3])
        ts(tbuf[:], tbuf[:], -1.0, None, ALU.mult, None, [tLG], [tLG])
    DT = sb("DT", [C, 4, C], BF16); tDT = T(None)
    DTMP = sb("DTMP", [C, C], F32)
    for hh in range(4):
        ts(DTMP[:], TABS[:, 0, :], LGR[:, 0, hh:hh + 1], None, ALU.mult, None, [tTABS, tLG], [tDT])
        stt(DTMP[:], TABS[:, 1, :], LGR[:, 1, hh:hh + 1], DTMP[:], ALU.mult, ALU.add, [tTABS, tLG, tDT], [tDT])
        act(DTMP[:], DTMP[:], AF.Exp, [tDT], [tDT])
        tt(DT[:, hh, :], DTMP[:], TABS[:, 2, :], ALU.mult, [tDT, tTABS], [tDT])
    DQ = sb("DQ", [C, 2, 2, C], F32)
    for d_ in range(2):
        for c in range(2):
            act(DQ[:, d_, c, :], TABS[:, 3 + d_, :], AF.Exp, [tTABS, tLG, tDT], [tDT], scale=LGQ[:, d_, c:c + 1])
    ts(DQ[:], DQ[:], 0.125, None, ALU.mult, None, [tDT], [tDT])
    DK = sb("DK", [C, 2, 4], F32)
    CD = sb("CD", [C, 2, 4], F32)
    for d_ in range(2):
        act(DK[:, d_, :], LGR[:, d_, :], AF.Exp, [tLG, tMCOL, tDT], [tDT], scale=MCOL[:, d_:d_ + 1])
        act(CD[:, d_, :], LGR[:, d_, :], AF.Exp, [tLG, tDT], [tDT], scale=float(C))
    WST = sb("WST", [C, 4, C], BF16); tWST = T(None)
    SGB = sb("SGB", [C, 4, C], F32)
    WSL = XST[0][:, 0:512].rearrange("p (g m) -> p g m", g=4)
    P.dma("sp", WSL, sgu_w.rearrange("g p m -> p g m"), writes=[tXST[0]])
    with ncdma():
        for g_ in range(4):
            P.dma("sp", SGB[:, g_, :], sgu_b[g_, :].partition_broadcast(C), writes=[tWST])
    pt, tp = ps()
    for g_ in range(4):
        tr(pt[:, g_ * C:(g_ + 1) * C], WSL[:, g_, :], [tXST[0]], [tp])
    cp("dve", WST[:], pt[:].rearrange("p (g m) -> p g m", g=4), [tp], [tWST])
    LAM = sb("LAM", [C, 4, 64], F32); tLAM = T(None)
    LAMV = sb("LAMV", [C, 4], F32)
    SUBG = sb("SUBG", [C, 1], F32)
    with ncdma():
        P.dma("sp", LAM[:].rearrange("p a b -> p (a b)"), diff_lambda.rearrange("a b -> (a b)").partition_broadcast(C), writes=[tLAM])
        P.dma("sp", SUBG[:], diff_subln[0, :].rearrange("(p o) -> p o", o=1), writes=[tLAM])
    tt(LAM[:, 0, :], LAM[:, 0, :], LAM[:, 1, :], ALU.mult, [tLAM], [tLAM])
    tt(LAM[:, 2, :], LAM[:, 2, :], LAM[:, 3, :], ALU.mult, [tLAM], [tLAM])
    P.op("dve", lambda e: e.reduce_sum(out=LAMV[:, 0:1], in_=LAM[:, 0, :], axis=AX.X), [tLAM], [tLAM])
    P.op("dve", lambda e: e.reduce_sum(out=LAMV[:, 1:2], in_=LAM[:, 2, :], axis=AX.X), [tLAM], [tLAM])
    act(LAMV[:, 0:2], LAMV[:, 0:2], AF.Exp, [tLAM], [tLAM])
    tt(LAMV[:, 2:3], LAMV[:, 1:2], LAMV[:, 0:1], ALU.subtract, [tLAM], [tLAM])
    ts(LAMV[:, 2:3], LAMV[:, 2:3], -LAMBDA_INIT, None, ALU.add, None, [tLAM], [tLAM])
    ts(SUBG[:], SUBG[:], (1.0 - LAMBDA_INIT) * math.sqrt(128.0), None, ALU.mult, None, [tLAM], [tLAM])
    NEGLAM = LAMV[:, 2:3]

    def wspec_ffn_in(l, f):
        W = ffn_w_in[l, f]
        out = []
        for i0 in range(0, HC, 4):
            n = min(4, HC - i0)
            out.append(("ffi", l, f, i0, n, KC, 2 * n * C,
                        [(W[:, i0 * C:(i0 + n) * C], 0), (W[:, DFF + i0 * C:DFF + (i0 + n) * C], n * C)]))
        return out

    def wspec_ffn_out(l, f):
        W = ffn_w_out[l, f]
        return [("ffo", l, f, b, 0, HC, 256, [(W[:, b * 256:(b + 1) * 256], 0)]) for b in range(4)]

    def wspec_cols(tag, W, c0, n):
        return [(tag, c0, n, 0, 0, KC, n, [(W[:, c0:c0 + n], 0)])]

    planA = wspec_ffn_in(0, 0) + wspec_ffn_out(0, 0) + wspec_cols("evkv", even_w_in, 256, 768)
    planB = (wspec_cols("ev1", even_w_in, 0, 1024) + wspec_cols("ev2", even_w_in, 1024, 1024)
             + wspec_cols("ev3", even_w_in, 2048, 512) + wspec_cols("evo", even_w_out, 0, 1024)
             + wspec_ffn_in(0, 1) + wspec_ffn_out(0, 1) + wspec_ffn_in(1, 0) + wspec_ffn_out(1, 0)
             + wspec_cols("od1", odd_w_in, 0, 1024) + wspec_cols("od2", odd_w_in, 1024, 1024)
             + wspec_cols("od3", odd_w_in, 2048, 1024))
    planC = wspec_cols("odo", odd_w_out, 0, 1024) + wspec_ffn_in(1, 1) + wspec_ffn_out(1, 1)
    all_tiles = [(g, ti) for g in groups for ti in range(len(g["tiles"]))]
    passes = "ABC" if stop_after is None else "ABC"[:"ABC".index(stop_after) + 1]
    plan = []
    for ps_ in passes:
        for _ in all_tiles:
            plan += {"A": planA, "B": planB, "C": planC}[ps_]
    wstate = dict(issued=0, used=0)

    def w_issue():
        i = wstate["issued"]
        if i >= len(plan):
            return
        spec = plan[i]
        kcn, ncols = spec[5], spec[6]
        buf = WB[i % 3][:, 0:kcn * ncols].rearrange("p (k m) -> p k m", k=kcn)
        for src, off in spec[7]:
            n = src.shape[1]
            P.dma("pool", buf[:, :, off:off + n], src.rearrange("(k p) m -> p k m", p=C), writes=[wbT[i % 3]])
        wstate["issued"] = i + 1

    def w_prefetch():
        while wstate["issued"] < min(wstate["used"] + 3, len(plan)):
            w_issue()

    def w_next(tag, ahead=2):
        i = wstate["used"]
        spec = plan[i]
        assert spec[0] == tag, (spec[0], tag)
        while wstate["issued"] < min(i + 1 + ahead, len(plan)):
            w_issue()
        wstate["used"] = i + 1
        kcn, ncols = spec[5], spec[6]
        return WB[i % 3][:, 0:kcn * ncols].rearrange("p (k m) -> p k m", k=kcn), wbT[i % 3], spec

    arena_hist = [T(None)]

    class Carver:
        def __init__(self):
            self.off = 0
            self.ts = []

        def take(self, shape, dt):
            nbytes = int(np.prod(shape[1:])) * (2 if dt == BF16 else 4)
            a = ARENA[:, self.off:self.off + nbytes].bitcast(dt)
            self.off += (nbytes + 63) // 64 * 64
            assert self.off <= ARENA_B, self.off
            if len(shape) > 2:
                names = "abcde"[:len(shape) - 1]
                kw = {names[i]: shape[1 + i] for i in range(len(shape) - 2)}
                a = a.rearrange("p (" + " ".join(names) + ") -> p " + " ".join(names), **kw)
            t = T(None).inherit(arena_hist[0])
            self.ts.append(t)
            return a, t

        def done(self):
            arena_hist[0] = T(None).inherit(arena_hist[0], *self.ts)

    arena_hist[0].inherit(*tAW)

    def ada_layer_late(l):
        for t_ in tAW:
            t_.inherit(arena_hist[0])
        ada_layer(l)
        arena_hist[0] = T(None).inherit(arena_hist[0], *tAW)

    SQ = sb("SQ", [C, KC, 512], BF16); tSQ = T(None)
    RS = [sb(f"RS{i}", [C, 512], F32) for i in range(2)]; tRS = [T(None), T(None)]
    TMP = [sb(f"TMP{i}", [C, 512], F32) for i in range(4)]; tTMP = [T(None) for _ in range(4)]
    tmp_rr = [0]

    def tmp():
        i = tmp_rr[0]
        tmp_rr[0] = (i + 1) % 4
        return TMP[i], tTMP[i]

    rs_rr = [0]

    def rstd_of(src_ap, n, width, nfeat, reads):
        i = rs_rr[0]
        rs_rr[0] ^= 1
        act(SQ[:, 0:n, 0:width], src_ap, AF.Square, reads, [tSQ])
        pt, tp = ps()
        for kc in range(n):
            mm(pt[:, 0:width], ONES[:], SQ[:, kc, 0:width], kc == 0, kc == n - 1, [tONES, tSQ], [tp])
        act(RS[i][:, 0:width], pt[:, 0:width], AF.Ln, [tp, tCST], [tRS[i]], bias=CST[:, 0:1] if nfeat == D else CST[:, 1:2])
        act(RS[i][:, 0:width], RS[i][:, 0:width], AF.Exp, [tRS[i]], [tRS[i]], scale=-0.5)
        return RS[i], tRS[i]

    def norm_tile(l, c, n, nsub):
        pts = []
        for s in range(nsub):
            cols = slice(s * 512, (s + 1) * 512)
            act(H[:, :, cols], X[:, :, cols], AF.Square, [xT[s]], [hT[s]])
        for s in range(nsub):
            cols = slice(s * 512, (s + 1) * 512)
            pt, tp = ps()
            for kc in range(KC):
                mm(pt[:], ONES[:], H[:, kc, cols], kc == 0, kc == KC - 1, [tONES, hT[s]], [tp])
            pts.append((pt, tp))
        for s in range(nsub):
            pt, tp = pts[s]
            act(RS[s][:], pt[:], AF.Ln, [tp, tCST], [tRS[s]], bias=CST[:, 0:1])
            act(RS[s][:], RS[s][:], AF.Exp, [tRS[s]], [tRS[s]], scale=-0.5)
        for s in range(nsub):
            cols = slice(s * 512, (s + 1) * 512)
            for kc in range(KC):
                tm, ttm = tmp()
                stt(tm[:], X[:, kc, cols], Avec(l, c, n, kc), RS[s][:], ALU.mult, ALU.mult, [xT[s], tRS[s], tAV], [ttm])
                act(H[:, kc, cols], tm[:], AF.Identity, [ttm, tMOD], [hT[s]], bias=Bvec(l, c, n, kc))

    def norm_mod(l, c, n, s, dst, tdst):
        cols = slice(s * 512, (s + 1) * 512)
        rs, trs = rstd_of(X[:, :, cols], KC, 512, D, [xT[s]])
        for kc in range(KC):
            tm, ttm = tmp()
            stt(tm[:], X[:, kc, cols], Avec(l, c, n, kc), rs[:], ALU.mult, ALU.mult, [xT[s], trs, tAV], [ttm])
            act(dst[:, kc, cols], tm[:], AF.Identity, [ttm, tMOD], [tdst], bias=Bvec(l, c, n, kc))

    def ffn(l, f, c, nsub):
        cv = Carver()
        HID, _ = cv.take([C, HC, 1024], BF16)
        tHID = [T(None).inherit(arena_hist[0]) for _ in range(2)]
        cv.ts += tHID
        for (i0_) in range(0, HC, 4):
            wb, twb, spec = w_next("ffi")
            n = spec[4]
            for s in range(nsub):
                for j in range(n):
                    cols = slice(s * 512, (s + 1) * 512)
                    pg, tpg = ps()
                    pu, tpu = ps()
                    for kc in range(KC):
                        mm(pg[:], wb[:, kc, j * C:(j + 1) * C], H[:, kc, cols], kc == 0, kc == KC - 1, [twb, hT[s]], [tpg])
                    for kc in range(KC):
                        mm(pu[:], wb[:, kc, (n + j) * C:(n + j + 1) * C], H[:, kc, cols], kc == 0, kc == KC - 1, [twb, hT[s]], [tpu])
                    tm, ttm = tmp()
                    act(tm[:], pg[:], AF.Silu, [tpg], [ttm])
                    tt(HID[:, i0_ + j, cols], tm[:], pu[:], ALU.mult, [ttm, tpu], [tHID[s]])
        for b in range(4):
            wb, twb, spec = w_next("ffo")
            for o2 in range(2):
                oc = 2 * b + o2
                for s in range(nsub):
                    cols = slice(s * 512, (s + 1) * 512)
                    po, tpo = ps()
                    for i in range(HC):
                        mm(po[:], wb[:, i, o2 * C:(o2 + 1) * C], HID[:, i, cols], i == 0, i == HC - 1, [twb, tHID[s]], [tpo])
                    stt(X[:, oc, cols], po[:], Gvec(l, c, 0 if f == 0 else 2, oc), X[:, oc, cols], ALU.mult, ALU.add,
                        [tpo, tAV, xT[s]], [xT[s]])
        cv.done()

    xst_rr = [0]

    def load_x_tm(g, t0, nsub):
        for ch in range(nsub * 4):
            i = xst_rr[0]
            xst_rr[0] ^= 1
            s = ch // 4
            P.dma("sp", XST[i][:], g["xin"][t0 + ch * C:t0 + (ch + 1) * C, :], writes=[tXST[i]])
            for half in range(2):
                pt, tp = ps()
                for q in range(4):
                    kc = half * 4 + q
                    tr(pt[:, q * C:(q + 1) * C], XST[i][:, kc * C:(kc + 1) * C], [tXST[i]], [tp])
                cpa(X[:, half * 4:(half + 1) * 4, ch * C:(ch + 1) * C], pt[:].rearrange("p (a b) -> p a b", a=4), [tp], [xT[s]])

    rope_pending = []

    def rope_flush():
        while rope_pending:
            rope_pending.pop(0)()

    def rope(src_ps, tsrc, s_tok0, dst32, tdst, g, post=None):
        if not g["latent"]:
            cpa(dst32, src_ps, [tsrc], [tdst])
            if post is not None:
                post()
            return
        tm, ttm = tmp()
        cp("act", tm[:], src_ps, [tsrc], [ttm])

        def stage2():
            pr, tpr = ps()
            mm(pr[:], PROT[:], tm[:], True, True, [tPROT, ttm], [tpr])
            tt(dst32, tm[:], ROPE[:, 0, :], ALU.mult, [ttm, tROPE], [tdst])
            tm2, ttm2 = tmp()
            tt(tm2[:], pr[:], ROPE[:, 1, :], ALU.mult, [tpr, tROPE], [ttm2])
            tt(dst32, dst32, tm2[:], ALU.add, [ttm2, tdst], [tdst])
            if post is not None:
                post()

        rope_flush()
        rope_pending.append(stage2)

    ROPE = sb("ROPE", [C, 2, 512], F32); tROPE = T(None)

    def load_rope(g, tok0):
        if g["latent"]:
            P.dma("sp", ROPE[:, 0, :], c_cos[:, tok0:tok0 + 512], writes=[tROPE])
            P.dma("sp", ROPE[:, 1, :], c_sin[:, tok0:tok0 + 512], writes=[tROPE])

    SF = sb("SF", [64, 4, C], F32); tSF = T(None)
    SBK = sb("SBK", [64, 4, C], F32); tSBK = T(None)
    STG16 = [sb(f"STG16_{i}", [64, 512], BF16) for i in range(2)]; tSTG16 = [T(None), T(None)]
    STG32 = [sb(f"STG32_{i}", [64, 512], F32) for i in range(2)]; tSTG32 = [T(None), T(None)]
    stg_rr = [0, 0]

    def pass_a(g, ti):
        t0, nsub = g["tiles"][ti]
        ci = g["ci"]
        load_x_tm(g, t0, nsub)
        norm_tile(0, ci, 0, nsub)
        ffn(0, 0, ci, nsub)
        for s in range(nsub):
            P.dma("sp", g["X1"][ti, :, :, s * 512:(s + 1) * 512], X[:, :, s * 512:(s + 1) * 512], reads=[xT[s]], writes=[g["tX1"][ti]])
        norm_tile(0, ci, 1, nsub)
        wb, twb, spec = w_next("evkv")
        cv = Carver()
        KR, tKR = cv.take([C, 2, 512], F32)
        KF, tKF = cv.take([C, 2, 4, 64], BF16)
        VT, tVT = cv.take([C, 512], BF16)
        for s in range(nsub):
            cols = slice(s * 512, (s + 1) * 512)
            load_rope(g, t0 + s * 512)
            for c in range(2):
                pk, tpk = ps()
                for kc in range(KC):
                    mm(pk[:], wb[:, kc, c * C:(c + 1) * C], H[:, kc, cols], kc == 0, kc == KC - 1, [twb, hT[s]], [tpk])
                rope(pk[:], tpk, t0 + s * 512, KR[:, c, :], tKR, g)
            rope_flush()
            for ch in range(4):
                n = (t0 + s * 512) // C + ch
                seq_chunks = g["seqlen"] // C
                first = (n % seq_chunks == 0)
                last = (n % seq_chunks == seq_chunks - 1)
                tcols = slice(ch * C, (ch + 1) * C)
                pkt, tpkt = ps()
                for c in range(2):
                    tr(pkt[:, c * C:(c + 1) * C], KR[:, c, tcols], [tKR], [tpkt])
                for d_ in range(2):
                    tt(KF[:, d_, :, :], pkt[:, 0:256].rearrange("p (h d) -> p h d", h=4),
                       DK[:, d_, :].unsqueeze(2).to_broadcast([C, 4, 64]), ALU.mult, [tpkt, tDT], [tKF])
                pv, tpv = ps()
                for kc in range(KC):
                    mm(pv[:], H[:, kc, s * 512 + ch * C:s * 512 + (ch + 1) * C], wb[:, kc, 256:768], kc == 0, kc == KC - 1, [twb, hT[s]], [tpv])
                cp("act", VT[:], pv[:], [tpv], [tVT])
                pu_ = []
                for d_ in range(2):
                    pu, tpu = ps()
                    for hh in range(4):
                        mm(pu[0:64, hh * C:(hh + 1) * C], KF[:, d_, hh, :], VT[:, hh * C:(hh + 1) * C], True, True, [tKF, tVT], [tpu])
                    pu_.append((pu, tpu))
                for d_, (UPD, tUPD) in enumerate(((g["UPDF"], g["tUPDF"]), (g["UPDB"], g["tUPDB"]))):
                    i32 = stg_rr[1]; stg_rr[1] ^= 1
                    cp("dve", STG32[i32][:], pu_[d_][0][0:64, :], [pu_[d_][1]], [tSTG32[i32]])
                    P.dma("sp", UPD[n], STG32[i32][:], reads=[tSTG32[i32]], writes=[tUPD[n]])
        cv.done()

    tOUT = T(None)

    def scan(g, d_, init, store, final):
        S, tS = (SF, tSF) if d_ == 0 else (SBK, tSBK)
        UPD, tUPD = (g["UPDF"], g["tUPDF"]) if d_ == 0 else (g["UPDB"], g["tUPDB"])
        SP, tSP = (g["SPF"], g["tSPF"]) if d_ == 0 else (g["SPB"], g["tSPB"])
        seq_chunks = g["seqlen"] // C
        order = range(g["nch"]) if d_ == 0 else range(g["nch"] - 1, -1, -1)
        Sf = S[:].rearrange("d h e -> d (h e)")
        for n in order:
            pos = n % seq_chunks
            first = (pos == 0) if d_ == 0 else (pos == seq_chunks - 1)
            last = (pos == seq_chunks - 1) if d_ == 0 else (pos == 0)
            if first:
                init(S, tS, d_)
            if store:
                i16 = stg_rr[0]; stg_rr[0] ^= 1
                cp("act", STG16[i16][:], Sf, [tS], [tSTG16[i16]])
                P.dma("sp", SP[n], STG16[i16][:], reads=[tSTG16[i16]], writes=[tSP[n]])
            ub, tub = tmp()
            P.dma("pool", ub[0:64, :], UPD[n], reads=[tUPD[n]], writes=[tub])
            tt(S[:], S[:], CD[0:64, d_, :].unsqueeze(2).to_broadcast([64, 4, C]), ALU.mult, [tS, tDT], [tS])
            tt(Sf, Sf, ub[0:64, :], ALU.add, [tS, tub], [tS])
            if last and final is not None:
                final(n // seq_chunks, S, tS, d_)

    def init_zero(S, tS, d_):
        P.op("dve", lambda e: e.memset(S[:], 0.0), [], [tS])

    def init_flag(S, tS, d_):
        P.dma("sp", S[:], (s0f, s0b)[d_].rearrange("h d e -> d h e"), writes=[tS])
        ts(S[:], S[:], FLAG[0:64, d_:d_ + 1], None, ALU.mult, None, [tS, tMCOL], [tS])

    def init_mix(S, tS, d_):
        init_flag(S, tS, d_)
        src = CALL[0:64, :] if d_ == 0 else CALL[192:256, :]
        CARRY, tCARRY = tmp()
        P.dma("sp", CARRY[0:64, :], src, reads=[tCALL], writes=[tCARRY])
        Sf = S[:].rearrange("d h e -> d (h e)")
        stt(Sf, CARRY[0:64, :], FLAG[0:64, 2 + d_:3 + d_], Sf, ALU.mult, ALU.add, [tCARRY, tS, tMCOL], [tS])

    def final_out(q, S, tS, d_):
        P.dma("sp", (o_rf, o_rb)[d_][q].rearrange("h d e -> d h e"), S[:], reads=[tS], writes=[tOUT])

    def final_carry(q, S, tS, d_):
        P.dma("sp", CSEND[d_ * 64:(d_ + 1) * 64, :], S[:].rearrange("d h e -> d (h e)"), reads=[tS], writes=[tCSEND])

    for g, ti in all_tiles:
        pass_a(g, ti)
    gp, gs = groups
    scan(gp, 0, init_zero, True, final_out)
    scan(gp, 1, init_zero, True, final_out)
    scan(gs, 0, init_flag, False, final_carry)
    scan(gs, 1, init_flag, False, final_carry)
    if "B" in passes:
        w_prefetch()
    P.cc("AllGather", ALU.bypass, PAIRS, CSEND, CALL, reads=[tCSEND], writes=[tCALL])
    ada_layer_late(1)

    def sample_scans():
        scan(gs, 0, init_mix, True, None)
        scan(gs, 1, init_mix, True, None)

    def gelu_to(dst, tdst, src_ps, tsrc, width=512):
        act(dst, src_ps, AF.Gelu_apprx_tanh, [tsrc], [tdst])

    SS4 = sb("SS4", [C, 8], F32); tSS4 = T(None)
    OST = XST; tOST = tXST
    VST = [sb(f"VST{i}", [C, D], BF16) for i in range(2)]; tVST = [T(None), T(None)]
    ost_rr = [0, 0]
    ost_rr = xst_rr + [0]

    def pass_b(g, ti):
        t0, nsub = g["tiles"][ti]
        ci = g["ci"]
        lat = g["latent"]
        for s in range(nsub):
            cols = slice(s * 512, (s + 1) * 512)
            P.dma("sp", X[:, :, cols], g["X1"][ti, :, :, cols], reads=[g["tX1"][ti]], writes=[xT[s]])
        norm_tile(0, ci, 1, nsub)
        w1, tw1, _ = w_next("ev1", 2)
        w2, tw2, _ = w_next("ev2", 0)
        w3, tw3, _ = w_next("ev3", 0)
        cv = Carver()
        QR, tQR = cv.take([C, 2, 512], F32)
        KR, tKR = cv.take([C, 2, 512], F32)
        QS, tQS = cv.take([C, 3, 2, 512], BF16)
        KM, tKB = cv.take([C, 4, 512], BF16)
        VT, tVT = cv.take([C, 4, 512], BF16)
        SG, tSG = cv.take([C, 4, 512], BF16)
        GU, tGU = cv.take([C, 4, 512], BF16)
        VN, tVN = cv.take([C, 4, 512], BF16)
        PM, tPM0 = cv.take([C, 2, 4, C], BF16)
        tPM = [tPM0, T(None).inherit(arena_hist[0])]
        cv.ts.append(tPM[1])
        SPL, tSPL = cv.take([C, 4, 2, 4, C], BF16)
        P.op("dve", lambda e: e.memset(SPL[:], 0.0), [], [tSPL])
        for s in range(nsub):
            cols = slice(s * 512, (s + 1) * 512)
            tok0 = t0 + s * 512
            load_rope(g, tok0)
            for ch in range(4):
                n = tok0 // C + ch
                for d_, (SP, tSP) in enumerate(((g["SPF"], g["tSPF"]), (g["SPB"], g["tSPB"]))):
                    for r in range(2):
                        P.dma("sp", SPL[r * 64:(r + 1) * 64, ch, d_, :, :].rearrange("p (c r) e -> p c r e", r=2)[:, :, r, :],
                              SP[n].rearrange("d (c r e) -> d c r e", c=2, r=2)[:, :, r, :], reads=[tSP[n]], writes=[tSPL])
            ck(1)
            for c in range(2):
                pq, tpq = ps()
                for kc in range(KC):
                    mm(pq[:], w1[:, kc, c * C:(c + 1) * C], H[:, kc, cols], kc == 0, kc == KC - 1, [tw1, hT[s]], [tpq])
                rope(pq[:], tpq, tok0, QR[:, c, :], tQR, g)
                pk, tpk = ps()
                for kc in range(KC):
                    mm(pk[:], w1[:, kc, 256 + c * C:256 + (c + 1) * C], H[:, kc, cols], kc == 0, kc == KC - 1, [tw1, hT[s]], [tpk])
                rope(pk[:], tpk, tok0, KR[:, c, :], tKR, g)
            rope_flush()
            for hh in range(4):
                ts(KM[:, hh, :], KR[:, hh // 2, :], MCOL[:, 2 + hh % 2:3 + hh % 2], None, ALU.mult, None, [tKR, tMCOL], [tKB])
            ts(QS[:, 0, :, :], QR[:], 0.125, None, ALU.mult, None, [tQR], [tQS])
            for d_ in range(2):
                for c in range(2):
                    tt(QS[:, 1 + d_, c, :].rearrange("p (a j) -> p a j", a=4), QR[:, c, :].rearrange("p (a j) -> p a j", a=4),
                       DQ[:, d_, c, :].unsqueeze(1).to_broadcast([C, 4, C]), ALU.mult, [tQR, tDT], [tQS])
            ck(2)
            for ch in range(4):
                tcols = slice(s * 512 + ch * C, s * 512 + (ch + 1) * C)
                pv, tpv = ps()
                for kc in range(KC):
                    mm(pv[:], H[:, kc, tcols], w1[:, kc, 512:1024], kc == 0, kc == KC - 1, [tw1, hT[s]], [tpv])
                cp("act", VT[:, ch, :], pv[:], [tpv], [tVT])
                pvs, tpvs = ps()
                for kc in range(KC):
                    mm(pvs[:], H[:, kc, tcols], w3[:, kc, 0:512], kc == 0, kc == KC - 1, [tw3, hT[s]], [tpvs])
                gv, tgv = tmp()
                gelu_to(gv[:], tgv, pvs[:], tpvs)
                sq, tsq = tmp()
                tt(sq[:], gv[:], gv[:], ALU.mult, [tgv], [tsq])
                P.op("dve", lambda e, sq=sq: e.reduce_sum(out=SS4[:, 0:4], in_=sq[:].rearrange("p (g c) -> p g c", g=4), axis=AX.X), [tsq], [tSS4])
                act(SS4[:, 0:4], SS4[:, 0:4], AF.Sqrt, [tSS4, tCST], [tSS4], bias=CST[:, 3:4], scale=1.0 / C)
                recip(SS4[:, 0:4], SS4[:, 0:4], [tSS4], [tSS4])
                tt(VN[:, ch, :].rearrange("p (g c) -> p g c", g=4), gv[:].rearrange("p (g c) -> p g c", g=4),
                   SS4[:, 0:4].unsqueeze(2).to_broadcast([C, 4, C]), ALU.mult, [tgv, tSS4], [tVN])
            ck(3)
            for c4 in range(4):
                pg, tpg = ps()
                for kc in range(KC):
                    mm(pg[:], w2[:, kc, c4 * C:(c4 + 1) * C], H[:, kc, cols], kc == 0, kc == KC - 1, [tw2, hT[s]], [tpg])
                act(SG[:, c4, :], pg[:], AF.Silu, [tpg], [tSG])
                pu, tpu = ps()
                for kc in range(KC):
                    mm(pu[:], w2[:, kc, 512 + c4 * C:512 + (c4 + 1) * C], H[:, kc, cols], kc == 0, kc == KC - 1, [tw2, hT[s]], [tpu])
                gelu_to(GU[:, c4, :], tGU, pu[:], tpu)
            ck(4)
            po = ps_reserve(4)
            for ch in range(4):
                chc = slice(ch * C, (ch + 1) * C)
                sc, tsc = ps()
                for hh in range(4):
                    c, base = hh // 2, (hh % 2) * 64
                    mm(sc[:, hh * C:(hh + 1) * C], KM[:, hh, chc], QS[:, 0, c, chc], True, True, [tKB, tQS], [tsc])
                pmi = ch % 2
                tt(PM[:, pmi, :, :], sc[:].rearrange("p (h j) -> p h j", h=4), DT[:], ALU.mult, [tsc, tDT], [tPM[pmi]])
                ck(41)
                for hh in range(4):
                    c, base = hh // 2, (hh % 2) * 64
                    o_, to_, _ = po[hh]
                    mm(o_[:, chc], SPL[:, ch, 0, hh, :], QS[:, 1, c, chc], True, False, [tSPL, tQS], [to_])
                    ck(42)
                    mm(o_[:, chc], SPL[:, ch, 1, hh, :], QS[:, 2, c, chc], False, False, [tSPL, tQS], [to_])
                    mm(o_[:, chc], VT[:, ch, hh * C:(hh + 1) * C], PM[:, pmi, hh, :], False, True, [tVT, tPM[pmi]], [to_])
                    ck(43)
            for hh in range(4):
                o_, to_, _ = po[hh]
                rs, trs = rstd_of(o_[:].unsqueeze(1), 1, 512, C, [to_])
                tm, ttm = tmp()
                tt(tm[:], o_[:], rs[:], ALU.mult, [to_, trs], [ttm])
                stt(H[:, hh, cols], tm[:], math.sqrt(float(C)), SG[:, hh, :], ALU.mult, ALU.mult, [ttm, tSG], [hT[s]])
            ps_release(po)
            ck(5)
            pg_ = ps_reserve(4)
            for ch in range(4):
                chc = slice(ch * C, (ch + 1) * C)
                for g_ in range(4):
                    o_, to_, _ = pg_[g_]
                    mm(o_[:, chc], VN[:, ch, g_ * C:(g_ + 1) * C], WST[:, g_, :], True, True, [tVN, tWST], [to_])
            for g_ in range(4):
                o_, to_, _ = pg_[g_]
                tm, ttm = tmp()
                tt(tm[:].rearrange("p (a j) -> p a j", a=4), o_[:].rearrange("p (a j) -> p a j", a=4),
                   SGB[:, g_, :].unsqueeze(1).to_broadcast([C, 4, C]), ALU.add, [to_, tWST], [ttm])
                tt(H[:, 4 + g_, cols], tm[:], GU[:, g_, :], ALU.mult, [ttm, tGU], [hT[s]])
            ps_release(pg_)
            ck(6)
        cv.done()
        wo, two, _ = w_next("evo")
        for oc in range(KC):
            for s in range(nsub):
                cols = slice(s * 512, (s + 1) * 512)
                pp, tpp = ps()
                for kc in range(KC):
                    mm(pp[:], wo[:, kc, oc * C:(oc + 1) * C], H[:, kc, cols], kc == 0, kc == KC - 1, [two, hT[s]], [tpp])
                stt(X[:, oc, cols], pp[:], Gvec(0, ci, 1, oc), X[:, oc, cols], ALU.mult, ALU.add, [tpp, tAV, xT[s]], [xT[s]])
        ck(7)
        norm_tile(0, ci, 2, nsub)
        ffn(0, 1, ci, nsub)
        ck(8)
        norm_tile(1, ci, 0, nsub)
        ffn(1, 0, ci, nsub)
        for s in range(nsub):
            cols = slice(s * 512, (s + 1) * 512)
            P.dma("sp", g["X4"][ti, :, :, cols], X[:, :, cols], reads=[xT[s]], writes=[g["tX4"][ti]])
        norm_tile(1, ci, 1, nsub)
        ck(9)
        koff = 0
        cv = Carver()
        R32b = [cv.take([C, 512], F32) for _ in range(2)]
        KST, tKST0 = cv.take([C, KC, 512], BF16)
        QST, tQST = cv.take([C, KC, 2, 512], BF16)
        tKST = [tKST0]
        for blk_i, tag in enumerate(("od1", "od2")):
            wb, twb, _ = w_next(tag)
            for s in range(nsub):
                cols = slice(s * 512, (s + 1) * 512)
                tok0 = t0 + s * 512
                load_rope(g, tok0)
                ks = tKST[s % 2] if False else tKST[0]
                for hh in range(8):
                    pq, tpq = ps()
                    for kc in range(KC):
                        mm(pq[:], wb[:, kc, hh * C:(hh + 1) * C], H[:, kc, cols], kc == 0, kc == KC - 1, [twb, hT[s]], [tpq])
                    r32, tr32 = R32b[hh % 2]

                    def post(hh=hh, r32=r32, tr32=tr32, blk_i=blk_i, ks=ks):
                        if blk_i == 0:
                            for j in range(2):
                                ts(QST[:, hh, j, :], r32[:], MCOL[:, 2 + j:3 + j], None, ALU.mult, None, [tr32, tMCOL], [tQST])
                        else:
                            cp("act", KST[:, hh, :], r32[:], [tr32], [ks])

                    rope(pq[:], tpq, tok0, r32[:], tr32, g, post)
                rope_flush()
                if blk_i == 0:
                    P.dma("sp", g["QFM"][ti, :, :, :, cols], QST[:], reads=[tQST], writes=[g["tQ"][ti]])
                else:
                    P.dma("sp", g["KFM"][:, :, koff + tok0:koff + tok0 + 512].rearrange("h p n -> p h n"), KST[:],
                          reads=[ks], writes=[g["tK"][hh][ti] for hh in range(8)])
                    if not lat:
                        for ch in range(4):
                            tcols = slice(s * 512 + ch * C, s * 512 + (ch + 1) * C)
                            i = ost_rr[0]; ost_rr[0] ^= 1
                            for half in range(2):
                                pk, tpk = ps()
                                for kc in range(KC):
                                    mm(pk[:], H[:, kc, tcols], wb[:, kc, half * 512:(half + 1) * 512], kc == 0, kc == KC - 1, [twb, hT[s]], [tpk])
                                cpa(OST[i][:, half * 512:(half + 1) * 512], pk[:], [tpk], [tOST[i]])
                            P.dma("sp", o_k[tok0 + ch * C:tok0 + (ch + 1) * C, :], OST[i][:], reads=[tOST[i]], writes=[tOUT])
        ck(10)
        wb, twb, _ = w_next("od3")
        for s in range(nsub):
            tok0 = t0 + s * 512
            for ch in range(4):
                tcols = slice(s * 512 + ch * C, s * 512 + (ch + 1) * C)
                i = ost_rr[0]; ost_rr[0] ^= 1
                iv = ost_rr[1]; ost_rr[1] ^= 1
                for half in range(2):
                    pv, tpv = ps()
                    for kc in range(KC):
                        mm(pv[:], H[:, kc, tcols], wb[:, kc, half * 512:(half + 1) * 512], kc == 0, kc == KC - 1, [twb, hT[s]], [tpv])
                    if not lat:
                        cp("dve", OST[i][:, half * 512:(half + 1) * 512], pv[:], [tpv], [tOST[i]])
                        cp("act", VST[iv][:, half * 512:(half + 1) * 512], OST[i][:, half * 512:(half + 1) * 512], [tOST[i]], [tVST[iv]])
                    else:
                        cp("act", VST[iv][:, half * 512:(half + 1) * 512], pv[:], [tpv], [tVST[iv]])
                kch = (koff + tok0) // C + ch
                P.dma("sp", g["VTM"][kch], VST[iv][:], reads=[tVST[iv]], writes=[g["tV"][kch]])
                if not lat:
                    P.dma("sp", o_v[tok0 + ch * C:tok0 + (ch + 1) * C, :], OST[i][:], reads=[tOST[i]], writes=[tOUT])
        cv.done()
        ck(11)

    def ctx_kv(g):
        cv = Carver()
        KST, tKST0 = cv.take([C, KC, 512], BF16)
        tKST = [tKST0]
        for j in range(PAST // C):
            i = xst_rr[0]; xst_rr[0] ^= 1
            P.dma("sp", XST[i][:], cache_k[j * C:(j + 1) * C, :], writes=[tXST[i]])
            for half in range(2):
                pt, tp = ps()
                for q in range(4):
                    hh = half * 4 + q
                    tr(pt[:, q * C:(q + 1) * C], XST[i][:, hh * C:(hh + 1) * C], [tXST[i]], [tp])
                cpa(KST[:, half * 4:(half + 1) * 4, 0:C], pt[:].rearrange("p (a b) -> p a b", a=4), [tp], [tKST[0]])
            P.dma("sp", KCTX[:, :, j * C:(j + 1) * C].rearrange("h p n -> p h n"), KST[:, :, 0:C],
                  reads=[tKST[0]], writes=[tKCTX])
            i = xst_rr[0]; xst_rr[0] ^= 1
            iv = ost_rr[1]; ost_rr[1] ^= 1
            P.dma("sp", XST[i][:], cache_v[j * C:(j + 1) * C, :], writes=[tXST[i]])
            cpa(VST[iv][:], XST[i][:], [tXST[i]], [tVST[iv]])
            P.dma("sp", VCTX[j], VST[iv][:], reads=[tVST[iv]], writes=[tVCTX])
        cv.done()

    def kv_gather(g):
        for i in range(NSPL):
            P.cc("AllGather", ALU.bypass, PAIRS, g["KFM"][i * HPS:(i + 1) * HPS].rearrange("h p n -> (h p) n"), KALL[i],
                 reads=[t for l in g["tK"] for t in l], writes=[tKALL])
            P.cc("AllGather", ALU.bypass, PAIRS, g["VTM"][i * CPS:(i + 1) * CPS].rearrange("k p e -> (k p) e"), VALL[i],
                 reads=g["tV"], writes=[tVALL])

    EB = [sb(f"EB{i}", [C, 512], BF16) for i in range(4)]; tEB = [T(None) for _ in range(4)]
    eb_rr = [0]

    def pass_c(g, ti):
        t0, nsub = g["tiles"][ti]
        ci = g["ci"]
        lat = g["latent"]
        ntile = len(g["tiles"])
        for s in range(nsub):
            cols = slice(s * 512, (s + 1) * 512)
            P.dma("sp", X[:, :, cols], g["X4"][ti, :, :, cols], reads=[g["tX4"][ti]], writes=[xT[s]])
        cv = Carver()
        QTb = [cv.take([C, 2, 1024], BF16) for _ in range(2)]
        nk = (PAST + 2 * LH) if lat else nsub * 512
        k0 = 0 if lat else t0
        nkc = nk // C
        KH = []; VH = []
        for i in range(2):
            a, ta = cv.take([C, nk], BF16)
            b, tb = cv.take([C, nkc, C], BF16)
            KH.append((a, ta)); VH.append((b, tb))
        A32, tA32 = cv.take([C, 512], F32)
        KALLv = [k_.rearrange("(r h p) n -> r h p n", r=2, h=HPS) for k_ in KALL]
        VALLv = [v_.rearrange("(r k p) e -> r k p e", r=2, p=C) for v_ in VALL]

        def load_kv(hh):
            (a, ta), (b, tb) = KH[hh % 2], VH[hh % 2]
            P.dma("sp", QTb[hh % 2][0][:, :, 0:nsub * 512], g["QFM"][ti, :, hh, :, 0:nsub * 512], reads=[g["tQ"][ti]], writes=[QTb[hh % 2][1]])
            if lat:
                pc = PAST // C
                lc = LH // C
                P.dma("sp", a[:, 0:PAST], KCTX[hh], reads=[tKCTX], writes=[ta])
                P.dma("sp", b[:, 0:pc, :], VCTX[:, :, hh * C:(hh + 1) * C].rearrange("k p e -> p k e"), reads=[tVCTX], writes=[tb])
                for r in range(2):
                    P.dma("sp", a[:, PAST + r * LH:PAST + (r + 1) * LH], KALLv[hh // HPS][r, hh % HPS], reads=[tKALL], writes=[ta])
                    for i in range(NSPL):
                        c0 = pc + r * lc + i * CPS
                        P.dma("sp", b[:, c0:c0 + CPS, :], VALLv[i][r, :, :, hh * C:(hh + 1) * C].rearrange("k p e -> p k e"),
                              reads=[tVALL], writes=[tb])
            else:
                P.dma("sp", a[:], g["KFM"][hh, :, k0:k0 + nk], reads=[g["tK"][hh][ti]], writes=[ta])
                P.dma("sp", b[:], g["VTM"][k0 // C:k0 // C + nkc, :, hh * C:(hh + 1) * C].rearrange("k p e -> p k e"),
                      reads=[g["tV"][k0 // C + j] for j in range(nkc)], writes=[tb])

        load_kv(0)
        for hh in range(8):
            if hh + 1 < 8:
                load_kv(hh + 1)
            (kh, tkh), (vh, tvh) = KH[hh % 2], VH[hh % 2]
            QT, tQT = QTb[hh % 2]
            if lat:
                qblocks = [(s * 512, 512, list(range(nkc))) for s in range(nsub)]
            else:
                qblocks = [(q * 256, 256, [2 * q, 2 * q + 1]) for q in range(nsub * 2)]
            for (q0, N, chunks) in qblocks:
                s = q0 // 512
                acc = ps_reserve(4)
                scs = {}

                def issue_sc(i):
                    kc_ = chunks[i]
                    lst = []
                    for j in range(2):
                        sc, tsc = ps()
                        mm(sc[:, 0:N], kh[:, kc_ * C:(kc_ + 1) * C], QT[:, j, q0:q0 + N], True, True, [tkh, tQT], [tsc])
                        lst.append((sc, tsc))
                    scs[i] = lst

                issue_sc(0)
                nchk = len(chunks)
                for i, kc_ in enumerate(chunks):
                    if i + 1 < nchk:
                        issue_sc(i + 1)
                    for j in range(2):
                        sc, tsc = scs[i][j]
                        e_i = eb_rr[0]; eb_rr[0] = (e_i + 1) % 4
                        act(EB[e_i][:, 0:N], sc[:, 0:N], AF.Exp, [tsc], [tEB[e_i]], scale=0.125)
                        mm(acc[j][0][:, 0:N], vh[:, kc_, :], EB[e_i][:, 0:N], i == 0, i == nchk - 1, [tvh, tEB[e_i]], [acc[j][1]])
                        mm(acc[2 + j][0][:, 0:N], ONES[:], EB[e_i][:, 0:N], i == 0, i == nchk - 1, [tONES, tEB[e_i]], [acc[2 + j][1]])
                    del scs[i]
                r1, tr1 = tmp()
                act(r1[:, 0:N], acc[2][0][:, 0:N], AF.Ln, [acc[2][1]], [tr1])
                act(r1[:, 0:N], r1[:, 0:N], AF.Exp, [tr1], [tr1], scale=-1.0)
                t1, tt1 = tmp()
                tt(t1[:, 0:N], acc[0][0][:, 0:N], r1[:, 0:N], ALU.mult, [acc[0][1], tr1], [tt1])
                r2, tr2 = tmp()
                act(r2[:, 0:N], acc[3][0][:, 0:N], AF.Ln, [acc[3][1]], [tr2])
                act(r2[:, 0:N], r2[:, 0:N], AF.Exp, [tr2], [tr2], scale=-1.0)
                t2, tt2 = tmp()
                tt(t2[:, 0:N], acc[1][0][:, 0:N], r2[:, 0:N], ALU.mult, [acc[1][1], tr2], [tt2])
                stt(A32[:, 0:N], t2[:, 0:N], NEGLAM, t1[:, 0:N], ALU.mult, ALU.add, [tt2, tt1, tLAM], [tA32])
                ps_release(acc)
                rs, trs = rstd_of(A32[:, 0:N].unsqueeze(1), 1, N, C, [tA32])
                stt(H[:, hh, q0:q0 + N], A32[:, 0:N], SUBG[:, 0:1], rs[:, 0:N], ALU.mult, ALU.mult, [tA32, trs, tLAM], [hT[s]])
        wb, twb, _ = w_next("odo")
        for oc in range(KC):
            for s in range(nsub):
                cols = slice(s * 512, (s + 1) * 512)
                pp, tpp = ps()
                for kc in range(KC):
                    mm(pp[:], wb[:, kc, oc * C:(oc + 1) * C], H[:, kc, cols], kc == 0, kc == KC - 1, [twb, hT[s]], [tpp])
                stt(X[:, oc, cols], pp[:], Gvec(1, ci, 1, oc), X[:, oc, cols], ALU.mult, ALU.add, [tpp, tAV, xT[s]], [xT[s]])
        cv.done()
        norm_tile(1, ci, 2, nsub)
        ffn(1, 1, ci, nsub)
        cv = Carver()
        YF, tYF = cv.take([C, KC, 512], F32)
        for s in range(nsub):
            cols = slice(s * 512, (s + 1) * 512)
            rs, trs = rstd_of(X[:, :, cols], KC, 512, D, [xT[s]])
            for kc in range(KC):
                stt(YF[:, kc, :], X[:, kc, cols], FNA[:, kc:kc + 1], rs[:], ALU.mult, ALU.mult, [xT[s], trs, tAV], [tYF])
            for ch in range(4):
                i = ost_rr[0]; ost_rr[0] ^= 1
                for half in range(2):
                    pt, tp = ps()
                    for q in range(4):
                        kc = half * 4 + q
                        tr(pt[:, q * C:(q + 1) * C], YF[:, kc, ch * C:(ch + 1) * C], [tYF], [tp])
                    cpa(OST[i][:, half * 512:(half + 1) * 512], pt[:], [tp], [tOST[i]])
                r0 = t0 + s * 512 + ch * C
                P.dma("sp", g["yout"][r0:r0 + C, :], OST[i][:], reads=[tOST[i]], writes=[tOUT])
        cv.done()

    try:
        if "B" in passes:
            for i_, (g, ti) in enumerate(all_tiles):
                if g["latent"] and ti == 0:
                    sample_scans()
                    ctx_kv(groups[1])
                pass_b(g, ti)
        if "C" in passes:
            w_prefetch()
            kv_gather(groups[1])
            for g, ti in all_tiles:
                pass_c(g, ti)
    except StopBuild:
        pass

    P.fence("sp", [tOUT, tKALL, tVALL, tCALL, tKCTX, tVCTX, tCSEND] + wbT + xT + hT + psT + [arena_hist[0]] + [t for g in groups for t in g["tX1"] + g["tX4"] + g["tQ"] + g["tV"] + g["tSPF"] + g["tSPB"] + g["tUPDB"] + g["tUPDF"] + [x for l in g["tK"] for x in l]])
    P.emit()
    st.close()
    return nc, P


def host_consts(LS):
    f32 = np.float32
    ident = np.eye(C, dtype=f32)
    pm = np.zeros((C, C), f32)
    for base in range(0, C, 32):
        for i in range(16):
            pm[base + i, base + i + 16] = -1.0
            pm[base + 16 + i, base + i] = 1.0
    prot = np.ascontiguousarray(pm.T)
    t = np.arange(LS)
    row = (t // 64).astype(f32)
    col = (t % 64).astype(f32)
    freq = (f32(10000.0) ** (-(np.arange(16, dtype=f32)) / f32(16))).astype(f32)
    cos = np.zeros((C, LS), f32)
    sin = np.zeros((C, LS), f32)
    for f in range(C):
        fh = f % 64
        pos = row if (fh // 32) == 0 else col
        ang = (pos * freq[fh % 16]).astype(f32)
        cos[f] = np.cos(ang).astype(f32)
        sin[f] = np.sin(ang).astype(f32)
    m = np.arange(C)[:, None].astype(f32)
    j = np.arange(C)[None, :].astype(f32)
    tabs = np.zeros((C, 5, C), f32)
    tabs[:, 0, :] = np.maximum(j - m, 0)
    tabs[:, 1, :] = np.maximum(m - j, 0)
    tabs[:, 2, :] = 1.0 + np.eye(C, dtype=f32)
    tabs[:, 3, :] = j + 1.0
    tabs[:, 4, :] = C - j
    mcol = np.zeros((C, 4), f32)
    mcol[:, 0] = C - 1 - np.arange(C)
    mcol[:, 1] = np.arange(C)
    mcol[:64, 2] = 1.0
    mcol[64:, 3] = 1.0
    return dict(c_ident=ident, c_prot=prot, c_cos=cos, c_sin=sin, c_tabs=tabs, c_mcol=mcol)


def make_in_maps(inp, NPS, LS, PAST, ncores=8):
    cst = host_consts(LS)
    LH = LS // 2
    maps = []
    a = lambda x: np.ascontiguousarray(np.asarray(x, dtype=np.float32))
    shared = dict(
        ada_w=a(inp["ada_w"]), ada_b=a(inp["ada_b"]), norm_g=a(inp["norm_g"]).reshape(6, 1024),
        ffn_w_in=a(inp["ffn_w_in"]), ffn_w_out=a(inp["ffn_w_out"]),
        even_w_in=a(inp["even_w_in"])[0], even_w_out=a(inp["even_w_out"])[0],
        ret_decay_fwd=a(inp["ret_decay_fwd"]), ret_decay_bwd=a(inp["ret_decay_bwd"]),
        sgu_w=a(inp["sgu_w"])[0], sgu_b=a(inp["sgu_b"])[0],
        odd_w_in=a(inp["odd_w_in"])[0], odd_w_out=a(inp["odd_w_out"])[0],
        diff_lambda=a(inp["diff_lambda"])[0], diff_subln=a(inp["diff_subln"]),
        final_norm=a(inp["final_norm"]).reshape(1, 1024), **cst)
    xp = a(inp["x_prompt"]); xs = a(inp["x_sample"])
    nb = xs.shape[0]
    for c in range(ncores):
        b, r = c // 2, c % 2
        m = dict(shared)
        m["c_cos"] = np.ascontiguousarray(cst["c_cos"][:, r * LH:(r + 1) * LH])
        m["c_sin"] = np.ascontiguousarray(cst["c_sin"][:, r * LH:(r + 1) * LH])
        fl = np.zeros((C, 4), np.float32)
        fl[:, 0] = 1.0 - r
        fl[:, 1] = float(r)
        fl[:, 2] = 1.0 - fl[:, 0]
        fl[:, 3] = 1.0 - fl[:, 1]
        m["c_flag"] = fl
        m["x_prompt"] = np.ascontiguousarray(xp[c * NPS:(c + 1) * NPS].reshape(NPS * 256, 1024))
        m["x_sample"] = np.ascontiguousarray(xs[b, r * LH:(r + 1) * LH])
        m["state_f"] = a(inp["state_ret_fwd"])[b, 0]
        m["state_b"] = a(inp["state_ret_bwd"])[b, 0]
        m["cache_k"] = a(inp["cache_k"])[b, 0].reshape(PAST, 1024)
        m["cache_v"] = a(inp["cache_v"])[b, 0].reshape(PAST, 1024)
        m["cond"] = np.ascontiguousarray(np.stack([a(inp["c_ctx"]), a(inp["c"])[b]]))
        maps.append(m)
    return maps


def gather(results, NPS, LS, nb=4):
    yp = np.stack([r["y_prompt"].reshape(NPS, 256, 1024) for r in results]).reshape(-1, 256, 1024)
    ys = np.stack([np.concatenate([results[2 * b]["y_sample"], results[2 * b + 1]["y_sample"]]) for b in range(nb)])
    rf = np.concatenate([r["new_ret_f"] for r in results])[:, None]
    rb = np.concatenate([r["new_ret_b"] for r in results])[:, None]
    nk = np.concatenate([r["new_k"].reshape(NPS, 256, 8, 128) for r in results])[:, None]
    nv = np.concatenate([r["new_v"].reshape(NPS, 256, 8, 128) for r in results])[:, None]
    return (yp, ys, rf, rb, nk, nv)


_NPS, _LS, _PAST = 4, 4096, 256


def kernel(**inputs):
    nc, P = build(_NPS, _LS // 2, _PAST)
    maps = make_in_maps(inputs, _NPS, _LS, _PAST)
    res = run_bass_kernel_spmd(nc, maps, core_ids=list(range(8)))
    outs = gather(res.results, _NPS, _LS)
    return tuple(np.ascontiguousarray(o, dtype=np.float32) for o in outs)
```

```python
import math
import contextlib
import numpy as np
from concourse.bass_utils import run_bass_kernel_spmd
import concourse.bass as bass
import concourse.mybir as mybir

F32 = mybir.dt.float32
BF16 = mybir.dt.bfloat16
AF = mybir.ActivationFunctionType
ALU = mybir.AluOpType
AX = mybir.AxisListType

EPOCH = 30000
COMPUTE = ("pe", "act", "dve", "pool")
NSLOT = {"sp": 12, "pool": 12, "act": 4}


class T:
    __slots__ = ("ap", "wr", "rd", "name")

    def __init__(self, ap, name=""):
        self.ap = ap
        self.wr = {}
        self.rd = {}
        self.name = name

    def inherit(self, *others):
        for o in others:
            for k, v in o.wr.items():
                if k not in self.rd or self.rd[k].seq < v.seq:
                    self.rd[k] = v
            for k, v in o.rd.items():
                if k not in self.rd or self.rd[k].seq < v.seq:
                    self.rd[k] = v
        return self


class Op:
    __slots__ = ("eng", "fn", "deps", "signal", "val", "stream", "seq", "is_dma", "slot", "inc")

    def __init__(self, eng, fn, stream, is_dma, slot=None):
        self.eng = eng
        self.fn = fn
        self.deps = []
        self.signal = False
        self.val = None
        self.stream = stream
        self.is_dma = is_dma
        self.slot = slot
        self.seq = 0
        self.inc = 16


class Prog:
    def __init__(self, nc):
        self.nc = nc
        self.ops = {e: [] for e in ("pe", "act", "dve", "pool", "sp")}
        self.seq = 0
        self.slot_rr = {q: 0 for q in NSLOT}
        self.slot_last = {}
        self.all_ops = []

    def _deps(self, op, reads, writes):
        deps = {}
        for t in reads:
            for k, v in t.wr.items():
                deps[id(v)] = v
        for t in writes:
            for k, v in t.wr.items():
                deps[id(v)] = v
            for k, v in t.rd.items():
                deps[id(v)] = v
        for v in deps.values():
            if v is op:
                continue
            if (not v.is_dma) and (not op.is_dma) and v.stream == op.stream and op.eng == "pe":
                continue
            op.deps.append(v)
            v.signal = True
        for t in reads:
            t.rd[op.stream] = op
        for t in writes:
            t.wr = {op.stream: op}
            t.rd = {}

    def op(self, eng, fn, reads=(), writes=()):
        o = Op(eng, fn, eng, False)
        self.seq += 1
        o.seq = self.seq
        self._deps(o, reads, writes)
        self.ops[eng].append(o)
        self.all_ops.append(o)
        return o

    def dma(self, q, out_ap, in_ap, reads=(), writes=(), **kw):
        slot = self.slot_rr[q]
        self.slot_rr[q] = (slot + 1) % NSLOT[q]
        stream = ("dma", q, slot)

        def fn(e, out_ap=out_ap, in_ap=in_ap, kw=kw):
            return e.dma_start(out=out_ap, in_=in_ap, allow_slow_non_contiguous=True, **kw)

        o = Op(q, fn, stream, True, slot)
        o.signal = True
        self.seq += 1
        o.seq = self.seq
        prev = self.slot_last.get(stream)
        if prev is not None:
            o.deps.append(prev)
        self.slot_last[stream] = o
        self._deps(o, reads, writes)
        self.ops[q].append(o)
        self.all_ops.append(o)
        return o

    def cc(self, kind, op, groups, in_ap, out_ap, reads=(), writes=()):
        stream = ("dma", "cc", 0)

        def fn(e):
            return e.collective_compute(kind, op, replica_groups=groups, ins=[in_ap], outs=[out_ap])

        o = Op("pool", fn, stream, True, 0)
        o.inc = 1
        o.signal = True
        self.seq += 1
        o.seq = self.seq
        prev = self.slot_last.get(stream)
        if prev is not None:
            o.deps.append(prev)
        self.slot_last[stream] = o
        self._deps(o, reads, writes)
        self.ops["pool"].append(o)
        self.all_ops.append(o)
        return o

    def fence(self, eng, ts):
        def fn(e):
            return None
        o = Op(eng, fn, eng, False)
        self.seq += 1
        o.seq = self.seq
        seen = {}
        for t in ts:
            for v in list(t.wr.values()) + list(t.rd.values()):
                seen[id(v)] = v
        for v in seen.values():
            o.deps.append(v)
            v.signal = True
        self.ops[eng].append(o)
        self.all_ops.append(o)
        return o

    def emit(self):
        import contextlib
        nc = self.nc
        cnt = {e: 0 for e in COMPUTE}
        dcnt = {}
        for o in self.all_ops:
            if o.is_dma:
                dcnt[o.stream] = dcnt.get(o.stream, 0) + 1
                o.val = dcnt[o.stream]
            elif o.signal:
                cnt[o.eng] += 1
                o.val = cnt[o.eng]
        DEP = EPOCH // 16
        stack = contextlib.ExitStack()
        sems = {}
        for e in COMPUTE:
            for ep in range((max(cnt[e], 1) - 1) // EPOCH + 1):
                sems[(e, ep)] = stack.enter_context(nc.semaphore(f"s_{e}_{ep}"))
        for st in dcnt:
            for ep in range((dcnt[st] - 1) // DEP + 1):
                sems[(st, ep)] = stack.enter_context(nc.semaphore(f"d_{st[1]}_{st[2]}_{ep}"))
        self.stats = {e: len(self.ops[e]) for e in self.ops}
        self.stats["sems"] = len(sems)
        self.stats["signals"] = dict(cnt)

        def semof(o):
            n = o.val - 1
            if o.is_dma:
                ep = n // DEP
                return sems[(o.stream, ep)], (n % DEP + 1) * o.inc
            ep = n // EPOCH
            return sems[(o.stream, ep)], n % EPOCH + 1

        block = stack.enter_context(nc.Block())
        ops = self.ops

        def run(engname):
            def body(e):
                waited = {}
                nwait = 0
                for o in ops[engname]:
                    for d in o.deps:
                        key = d.stream
                        if waited.get(key, 0) >= d.val:
                            continue
                        waited[key] = d.val
                        s, v = semof(d)
                        e.wait_ge(s, v)
                        nwait += 1
                    ins = o.fn(e)
                    if ins is None:
                        continue
                    if o.is_dma:
                        s, v = semof(o)
                        ins.then_inc(s, o.inc)
                    elif o.signal:
                        s, v = semof(o)
                        ins.then_inc(s, 1)
                self.stats["waits_" + engname] = nwait
            return body

        block.tensor(run("pe"))
        block.scalar(run("act"))
        block.vector(run("dve"))
        block.gpsimd(run("pool"))
        block.sync(run("sp"))
        stack.close()


C = 128
EPS = 1e-6
D = 1024
KC = 8
DFF = 2816
HC = 22
ARENA_B = 47104
LAMBDA_INIT = 0.8 - 0.6 * math.exp(-0.3 * 1)


class StopBuild(Exception):
    pass


def build(NPS, LS, PAST, stop_after=None, ck_stop=None):
    def ck(i):
        if ck_stop is not None and i == ck_stop:
            raise StopBuild()

    nc = bass.Bass("TRN2", target_bir_lowering=False)
    st = contextlib.ExitStack()
    P = Prog(nc)
    NPT = NPS * 256

    def din(name, shape, dt=F32):
        return nc.dram_tensor(name, list(shape), dt, kind="ExternalInput").ap()

    def dout(name, shape, dt=F32):
        return nc.dram_tensor(name, list(shape), dt, kind="ExternalOutput").ap()

    def dscr(name, shape, dt):
        return nc.dram_tensor(name, list(shape), dt).ap()

    def sb(name, shape, dt=F32):
        return st.enter_context(nc.sbuf_tensor(name, list(shape), dt))

    xp = din("x_prompt", [NPT, D])
    xs_ = din("x_sample", [LS, D])
    PAIRS = [[0, 1], [2, 3], [4, 5], [6, 7]]
    s0f = din("state_f", [4, 64, 128])
    s0b = din("state_b", [4, 64, 128])
    cache_k = din("cache_k", [PAST, D])
    cache_v = din("cache_v", [PAST, D])
    cond = din("cond", [2, D])
    ada_w = din("ada_w", [2, D, 9 * D])
    ada_b = din("ada_b", [2, 9 * D])
    norm_g = din("norm_g", [6, D])
    ffn_w_in = din("ffn_w_in", [2, 2, D, 2 * DFF])
    ffn_w_out = din("ffn_w_out", [2, 2, DFF, D])
    even_w_in = din("even_w_in", [D, 2560])
    even_w_out = din("even_w_out", [D, D])
    rdf = din("ret_decay_fwd", [1, 4])
    rdb = din("ret_decay_bwd", [1, 4])
    sgu_w = din("sgu_w", [4, C, C])
    sgu_b = din("sgu_b", [4, C])
    odd_w_in = din("odd_w_in", [D, 3072])
    odd_w_out = din("odd_w_out", [D, D])
    diff_lambda = din("diff_lambda", [4, 64])
    diff_subln = din("diff_subln", [1, C])
    final_norm = din("final_norm", [1, D])
    c_ident = din("c_ident", [C, C])
    c_prot = din("c_prot", [C, C])
    c_cos = din("c_cos", [C, LS])
    c_sin = din("c_sin", [C, LS])
    c_tabs = din("c_tabs", [C, 5, C])
    c_mcol = din("c_mcol", [C, 4])
    c_flag = din("c_flag", [C, 4])

    y_p = dout("y_prompt", [NPT, D])
    y_s = dout("y_sample", [LS, D])
    o_rf = dout("new_ret_f", [NPS, 4, 64, 128])
    o_rb = dout("new_ret_b", [NPS, 4, 64, 128])
    o_k = dout("new_k", [NPT, D])
    o_v = dout("new_v", [NPT, D])

    groups = []
    for gi, (nm, latent, ntok, seqlen, xin, yout) in enumerate(
            [("p", False, NPT, 256, xp, y_p), ("s", True, LS, LS, xs_, y_s)]):
        g = dict(name=nm, latent=latent, ntok=ntok, seqlen=seqlen, xin=xin, yout=yout, ci=gi)
        tiles = []
        t0 = 0
        while t0 < ntok:
            n = min(1024, ntok - t0)
            tiles.append((t0, n // 512))
            t0 += n
        g["tiles"] = tiles
        nt = len(tiles)
        nch = ntok // C
        g["nch"] = nch
        nk = ntok
        g["nk"] = nk
        g["X1"] = dscr(f"X1_{nm}", [nt, C, KC, 1024], F32)
        g["X4"] = dscr(f"X4_{nm}", [nt, C, KC, 1024], F32)
        g["QFM"] = dscr(f"QFM_{nm}", [nt, C, KC, 2, 1024], BF16)
        g["KFM"] = dscr(f"KFM_{nm}", [8, C, nk], BF16)
        g["VTM"] = dscr(f"VTM_{nm}", [nk // C, C, D], BF16)
        g["SPF"] = dscr(f"SPF_{nm}", [nch, 64, 512], BF16)
        g["SPB"] = dscr(f"SPB_{nm}", [nch, 64, 512], BF16)
        g["UPDB"] = dscr(f"UPDB_{nm}", [nch, 64, 512], F32)
        g["UPDF"] = dscr(f"UPDF_{nm}", [nch, 64, 512], F32)
        g["tUPDF"] = [T(None) for _ in range(nch)]
        g["tX1"] = [T(None) for _ in range(nt)]
        g["tX4"] = [T(None) for _ in range(nt)]
        g["tQ"] = [T(None) for _ in range(nt)]
        g["tK"] = [[T(None) for _ in range(nt)] for _ in range(8)]
        g["tV"] = [T(None) for _ in range(nk // C)]
        g["tSPF"] = [T(None) for _ in range(nch)]
        g["tSPB"] = [T(None) for _ in range(nch)]
        g["tUPDB"] = [T(None) for _ in range(nch)]
        groups.append(g)

    LH = LS
    KCTX = dscr("KCTX", [8, C, PAST], BF16); tKCTX = T(None)
    VCTX = dscr("VCTX", [PAST // C, C, D], BF16); tVCTX = T(None)
    NSPL = max(1, (8 * C * LH * 2) // (2 << 20))
    HPS = 8 // NSPL
    CPS = (LH // C) // NSPL
    KALL = [dscr(f"KALL{i}", [2 * HPS * C, LH], BF16) for i in range(NSPL)]; tKALL = T(None)
    VALL = [dscr(f"VALL{i}", [2 * CPS * C, D], BF16) for i in range(NSPL)]; tVALL = T(None)
    CSEND = dscr("CSEND", [128, 512], F32); tCSEND = T(None)
    CALL = dscr("CALL", [256, 512], F32); tCALL = T(None)

    X = sb("X", [C, KC, 1024], F32)
    H = sb("H", [C, KC, 1024], BF16)
    ARENA = sb("ARENA", [C, ARENA_B], mybir.dt.uint8)
    xT = [T(None, "x0"), T(None, "x1")]
    hT = [T(None, "h0"), T(None, "h1")]
    WB = [sb(f"WB{i}", [C, 8192], BF16) for i in range(3)]
    wbT = [T(None, f"wb{i}") for i in range(3)]
    PS = [st.enter_context(nc.psum_tensor(f"PS{i}", [C, 512], F32)) for i in range(8)]
    psT = [T(None, f"ps{i}") for i in range(8)]
    ps_rr = [0]

    ps_res = set()

    def ps():
        while True:
            i = ps_rr[0]
            ps_rr[0] = (i + 1) % 8
            if i not in ps_res:
                return PS[i], psT[i]

    def ps_reserve(n):
        out = []
        for _ in range(n):
            while True:
                i = ps_rr[0]
                ps_rr[0] = (i + 1) % 8
                if i not in ps_res:
                    break
            ps_res.add(i)
            out.append((PS[i], psT[i], i))
        return out

    def ps_release(lst):
        for _, _, i in lst:
            ps_res.discard(i)

    XST = [sb(f"XST{i}", [C, D], F32) for i in range(2)]; tXST = [T(None), T(None)]
    ONES = sb("ONES", [C, C], BF16); tONES = T(None)
    IDENT = sb("IDENT", [C, C], F32); tIDENT = T(None)
    PROT = sb("PROT", [C, C], F32); tPROT = T(None)
    TABS = sb("TABS", [C, 5, C], F32); tTABS = T(None)
    MCOL = sb("MCOL", [C, 4], F32); tMCOL = T(None)
    P.op("dve", lambda e: e.memset(ONES[:], 1.0), [], [tONES])
    P.dma("sp", IDENT[:], c_ident, writes=[tIDENT])
    P.dma("sp", PROT[:], c_prot, writes=[tPROT])
    P.dma("sp", TABS[:], c_tabs, writes=[tTABS])
    P.dma("sp", MCOL[:], c_mcol, writes=[tMCOL])
    FLAG = sb("FLAG", [C, 4], F32)
    P.dma("sp", FLAG[:], c_flag, writes=[tMCOL])

    CST = sb("CST", [C, 4], F32); tCST = T(None)
    P.op("dve", lambda e: e.memset(CST[:, 0:1], float(D * EPS)), [], [tCST])
    P.op("dve", lambda e: e.memset(CST[:, 1:2], float(C * EPS)), [], [tCST])
    P.op("dve", lambda e: e.memset(CST[:, 2:3], 1.0), [], [tCST])
    P.op("dve", lambda e: e.memset(CST[:, 3:4], float(EPS)), [], [tCST])

    def mm(out, lhsT, rhs, start, stop, reads, writes):
        return P.op("pe", lambda e: e.matmul(out, lhsT=lhsT, rhs=rhs, start=start, stop=stop), reads, writes)

    def tr(out, in_, reads, writes):
        return P.op("pe", lambda e: e.transpose(out, in_, IDENT[:]), list(reads) + [tIDENT], writes)

    def act(out, in_, func, reads, writes, bias=None, scale=None):
        kw = {}
        if bias is not None:
            kw["bias"] = bias
        if scale is not None:
            kw["scale"] = scale
        return P.op("act", lambda e: e.activation(out=out, in_=in_, func=func, **kw), reads, writes)

    def tt(out, in0, in1, op, reads, writes):
        return P.op("dve", lambda e: e.tensor_tensor(out=out, in0=in0, in1=in1, op=op), reads, writes)

    def ts(out, in0, s1, s2, op0, op1, reads, writes):
        if s2 is None:
            return P.op("dve", lambda e: e.tensor_scalar(out=out, in0=in0, scalar1=s1, scalar2=None, op0=op0), reads, writes)
        return P.op("dve", lambda e: e.tensor_scalar(out=out, in0=in0, scalar1=s1, scalar2=s2, op0=op0, op1=op1), reads, writes)

    def stt(out, in0, scalar, in1, op0, op1, reads, writes):
        return P.op("dve", lambda e: e.scalar_tensor_tensor(out=out, in0=in0, scalar=scalar, in1=in1, op0=op0, op1=op1), reads, writes)

    def recip(out, in_, reads, writes):
        return P.op("dve", lambda e: e.reciprocal(out=out, in_=in_), reads, writes)

    def cp(eng, out, in_, reads, writes):
        if eng == "act":
            return P.op("act", lambda e: e.copy(out=out, in_=in_), reads, writes)
        return P.op("dve", lambda e: e.tensor_copy(out=out, in_=in_), reads, writes)

    cp_rr = [0]

    def cpa(out, in_, reads, writes):
        cp_rr[0] ^= 1
        return cp("act" if cp_rr[0] else "dve", out, in_, reads, writes)

    ncdma = lambda: nc.allow_non_contiguous_dma(reason="small param layout")

    CONDT = sb("CONDT", [C, KC, 2], F32); tCOND = T(None)
    with ncdma():
        for c in range(2):
            P.dma("sp", CONDT[:, :, c], cond[c, :].rearrange("(k p) -> p k", p=C), writes=[tCOND])
    act(CONDT[:], CONDT[:], AF.Silu, [tCOND], [tCOND])
    MODT = sb("MODT", [C, 2, 72, 2], F32); tMOD = T(None)
    ADAB = sb("ADAB", [C, 2, 72], F32); tADAB = T(None)
    GT = sb("GT", [C, 6, KC], F32); tGT = T(None)
    FNG = sb("FNG", [C, KC], F32); tFNG = T(None)
    with ncdma():
        for l in range(2):
            P.dma("sp", ADAB[:, l, :], ada_b[l, :].rearrange("(o p) -> p o", p=C), writes=[tADAB])
        for i in range(6):
            P.dma("sp", GT[:, i, :], norm_g[i, :].rearrange("(k p) -> p k", p=C), writes=[tGT])
        P.dma("sp", FNG[:], final_norm[0, :].rearrange("(k p) -> p k", p=C), writes=[tFNG])
    AW = [ARENA[:, i * 16384:(i + 1) * 16384].bitcast(F32).rearrange("p (k m) -> p k m", k=KC) for i in range(2)]
    tAW = [T(None), T(None)]
    AV = sb("AV", [C, 2, 2, 3, KC], F32)
    GV = sb("GV", [C, 2, 2, 3, KC], F32)
    tAV = T(None)
    ada_blk = [0]

    def ada_layer(l, q="sp"):
        for b in range(18):
            ada_block(l, b, q)
            yield
        ada_finish(l)

    def ada_block(l, b, q):
        if True:
            blk = ada_blk[0]
            ada_blk[0] += 1
            buf, tb = AW[blk % 2], tAW[blk % 2]
            P.dma(q, buf, ada_w[l, :, b * 512:(b + 1) * 512].rearrange("(k p) m -> p k m", p=C), writes=[tb])
            pt, tp = ps()
            for oc in range(4):
                for kc in range(KC):
                    mm(pt[:, oc * 2:oc * 2 + 2], buf[:, kc, oc * 128:(oc + 1) * 128], CONDT[:, kc, :],
                       kc == 0, kc == KC - 1, [tb, tCOND], [tp])
            tt(MODT[:, l, b * 4:(b + 1) * 4, :], pt[:, 0:8].rearrange("p (a c) -> p a c", c=2),
               ADAB[:, l, b * 4:(b + 1) * 4].unsqueeze(2).to_broadcast([C, 4, 2]), ALU.add, [tp, tADAB], [tMOD])

    def ada_finish(l):
        for c in range(2):
            for n in range(3):
                ts(AV[:, l, c, n, :], MODT[:, l, (3 * n + 1) * 8:(3 * n + 2) * 8, c], 1.0, 32.0, ALU.add, ALU.mult, [tMOD], [tAV])
                tt(AV[:, l, c, n, :], AV[:, l, c, n, :], GT[:, l * 3 + n, :], ALU.mult, [tAV, tGT], [tAV])
                ts(GV[:, l, c, n, :], MODT[:, l, (3 * n + 2) * 8:(3 * n + 3) * 8, c], 0.5 if n != 1 else 1.0, None, ALU.mult, None, [tMOD], [tAV])

    for _ in ada_layer(0):
        pass
    FNA = sb("FNA", [C, KC], F32)
    ts(FNA[:], FNG[:], 32.0, None, ALU.mult, None, [tFNG], [tAV])

    def Avec(l, c, n, kc):
        return AV[:, l, c, n, kc:kc + 1]

    def Bvec(l, c, n, kc):
        return MODT[:, l, (3 * n) * 8 + kc, c:c + 1]

    def Gvec(l, c, n, kc):
        return GV[:, l, c, n, kc:kc + 1]

    LGR = sb("LGR", [C, 2, 4], F32); tLG = T(None)
    LGQ = sb("LGQ", [C, 2, 2], F32)
    with ncdma():
        for d_, src in enumerate((rdf, rdb)):
            P.dma("sp", LGR[:, d_, :], src[0, :].partition_broadcast(C), writes=[tLG])
            for c in range(2):
                for hh in range(2):
                    P.dma("sp", LGQ[hh * 64:(hh + 1) * 64, d_, c:c + 1],
                          src[0, 2 * c + hh:2 * c + hh + 1].partition_broadcast(64), writes=[tLG])
    for tbuf in (LGR, LGQ):
        act(tbuf[:], tbuf[:], AF.Exp, [tLG], [tLG], scale=-1.0)
        act(tbuf[:], tbuf[:], AF.Ln, [tLG, tCST], [tLG], bias=CST[:, 2:3])
        ts(tbuf[:], tbuf[:], -1.0, None, ALU.mult, None, [tLG], [tLG])
    DT = sb("DT", [C, 4, C], BF16); tDT = T(None)
    DTMP = sb("DTMP", [C, C], F32)
    for hh in range(4):
        ts(DTMP[:], TABS[:, 0, :], LGR[:, 0, hh:hh + 1], None, ALU.mult, None, [tTABS, tLG], [tDT])
        stt(DTMP[:], TABS[:, 1, :], LGR[:, 1, hh:hh + 1], DTMP[:], ALU.mult, ALU.add, [tTABS, tLG, tDT], [tDT])
        act(DTMP[:], DTMP[:], AF.Exp, [tDT], [tDT])
        tt(DT[:, hh, :], DTMP[:], TABS[:, 2, :], ALU.mult, [tDT, tTABS], [tDT])
    DQ = sb("DQ", [C, 2, 2, C], F32)
    for d_ in range(2):
        for c in range(2):
            act(DQ[:, d_, c, :], TABS[:, 3 + d_, :], AF.Exp, [tTABS, tLG, tDT], [tDT], scale=LGQ[:, d_, c:c + 1])
    ts(DQ[:], DQ[:], 0.125, None, ALU.mult, None, [tDT], [tDT])
    DK = sb("DK", [C, 2, 4], F32)
    CD = sb("CD", [C, 2, 4], F32)
    for d_ in range(2):
        act(DK[:, d_, :], LGR[:, d_, :], AF.Exp, [tLG, tMCOL, tDT], [tDT], scale=MCOL[:, d_:d_ + 1])
        act(CD[:, d_, :], LGR[:, d_, :], AF.Exp, [tLG, tDT], [tDT], scale=float(C))
    WST = sb("WST", [C, 4, C], BF16); tWST = T(None)
    SGB = sb("SGB", [C, 4, C], F32)
    WSL = XST[0][:, 0:512].rearrange("p (g m) -> p g m", g=4)
    P.dma("sp", WSL, sgu_w.rearrange("g p m -> p g m"), writes=[tXST[0]])
    with ncdma():
        for g_ in range(4):
            P.dma("sp", SGB[:, g_, :], sgu_b[g_, :].partition_broadcast(C), writes=[tWST])
    pt, tp = ps()
    for g_ in range(4):
        tr(pt[:, g_ * C:(g_ + 1) * C], WSL[:, g_, :], [tXST[0]], [tp])
    cp("dve", WST[:], pt[:].rearrange("p (g m) -> p g m", g=4), [tp], [tWST])
    LAM = sb("LAM", [C, 4, 64], F32); tLAM = T(None)
    LAMV = sb("LAMV", [C, 4], F32)
    SUBG = sb("SUBG", [C, 1], F32)
    with ncdma():
        P.dma("sp", LAM[:].rearrange("p a b -> p (a b)"), diff_lambda.rearrange("a b -> (a b)").partition_broadcast(C), writes=[tLAM])
        P.dma("sp", SUBG[:], diff_subln[0, :].rearrange("(p o) -> p o", o=1), writes=[tLAM])
    tt(LAM[:, 0, :], LAM[:, 0, :], LAM[:, 1, :], ALU.mult, [tLAM], [tLAM])
    tt(LAM[:, 2, :], LAM[:, 2, :], LAM[:, 3, :], ALU.mult, [tLAM], [tLAM])
    P.op("dve", lambda e: e.reduce_sum(out=LAMV[:, 0:1], in_=LAM[:, 0, :], axis=AX.X), [tLAM], [tLAM])
    P.op("dve", lambda e: e.reduce_sum(out=LAMV[:, 1:2], in_=LAM[:, 2, :], axis=AX.X), [tLAM], [tLAM])
    act(LAMV[:, 0:2], LAMV[:, 0:2], AF.Exp, [tLAM], [tLAM])
    tt(LAMV[:, 2:3], LAMV[:, 1:2], LAMV[:, 0:1], ALU.subtract, [tLAM], [tLAM])
    ts(LAMV[:, 2:3], LAMV[:, 2:3], -LAMBDA_INIT, None, ALU.add, None, [tLAM], [tLAM])
    ts(SUBG[:], SUBG[:], (1.0 - LAMBDA_INIT) * math.sqrt(128.0), None, ALU.mult, None, [tLAM], [tLAM])
    NEGLAM = LAMV[:, 2:3]

    def wspec_ffn_in(l, f):
        W = ffn_w_in[l, f]
        out = []
        for i0 in range(0, HC, 4):
            n = min(4, HC - i0)
            out.append(("ffi", l, f, i0, n, KC, 2 * n * C,
                        [(W[:, i0 * C:(i0 + n) * C], 0), (W[:, DFF + i0 * C:DFF + (i0 + n) * C], n * C)]))
        return out

    def wspec_ffn_out(l, f):
        W = ffn_w_out[l, f]
        return [("ffo", l, f, b, 0, HC, 256, [(W[:, b * 256:(b + 1) * 256], 0)]) for b in range(4)]

    def wspec_cols(tag, W, c0, n):
        return [(tag, c0, n, 0, 0, KC, n, [(W[:, c0:c0 + n], 0)])]

    planA = wspec_ffn_in(0, 0) + wspec_ffn_out(0, 0) + wspec_cols("evkv", even_w_in, 256, 768)
    planB = (wspec_cols("ev1", even_w_in, 0, 1024) + wspec_cols("ev2", even_w_in, 1024, 1024)
             + wspec_cols("ev3", even_w_in, 2048, 512) + wspec_cols("evo", even_w_out, 0, 1024)
             + wspec_ffn_in(0, 1) + wspec_ffn_out(0, 1) + wspec_ffn_in(1, 0) + wspec_ffn_out(1, 0)
             + wspec_cols("od1", odd_w_in, 0, 1024) + wspec_cols("od2", odd_w_in, 1024, 1024)
             + wspec_cols("od3", odd_w_in, 2048, 1024))
    planC = wspec_cols("odo", odd_w_out, 0, 1024) + wspec_ffn_in(1, 1) + wspec_ffn_out(1, 1)
    all_tiles = [(g, ti) for g in groups for ti in range(len(g["tiles"]))]
    passes = "ABC" if stop_after is None else "ABC"[:"ABC".index(stop_after) + 1]
    plan = []
    for ps_ in passes:
        for _ in all_tiles:
            plan += {"A": planA, "B": planB, "C": planC}[ps_]
    wstate = dict(issued=0, used=0)

    def w_issue():
        i = wstate["issued"]
        if i >= len(plan):
            return
        spec = plan[i]
        kcn, ncols = spec[5], spec[6]
        buf = WB[i % 3][:, 0:kcn * ncols].rearrange("p (k m) -> p k m", k=kcn)
        for src, off in spec[7]:
            n = src.shape[1]
            P.dma("pool", buf[:, :, off:off + n], src.rearrange("(k p) m -> p k m", p=C), writes=[wbT[i % 3]])
        wstate["issued"] = i + 1

    def w_prefetch():
        while wstate["issued"] < min(wstate["used"] + 3, len(plan)):
            w_issue()

    def w_next(tag, ahead=2):
        i = wstate["used"]
        spec = plan[i]
        assert spec[0] == tag, (spec[0], tag)
        while wstate["issued"] < min(i + 1 + ahead, len(plan)):
            w_issue()
        wstate["used"] = i + 1
        kcn, ncols = spec[5], spec[6]
        return WB[i % 3][:, 0:kcn * ncols].rearrange("p (k m) -> p k m", k=kcn), wbT[i % 3], spec

    arena_hist = [T(None)]

    class Carver:
        def __init__(self):
            self.off = 0
            self.ts = []

        def take(self, shape, dt):
            nbytes = int(np.prod(shape[1:])) * (2 if dt == BF16 else 4)
            a = ARENA[:, self.off:self.off + nbytes].bitcast(dt)
            self.off += (nbytes + 63) // 64 * 64
            assert self.off <= ARENA_B, self.off
            if len(shape) > 2:
                names = "abcde"[:len(shape) - 1]
                kw = {names[i]: shape[1 + i] for i in range(len(shape) - 2)}
                a = a.rearrange("p (" + " ".join(names) + ") -> p " + " ".join(names), **kw)
            t = T(None).inherit(arena_hist[0])
            self.ts.append(t)
            return a, t

        def done(self):
            arena_hist[0] = T(None).inherit(arena_hist[0], *self.ts)

    arena_hist[0].inherit(*tAW)

    def ada_layer_late(l):
        for t_ in tAW:
            t_.inherit(arena_hist[0])
        for _ in ada_layer(l, "pool"):
            yield
        arena_hist[0] = T(None).inherit(arena_hist[0], *tAW)

    SQ = sb("SQ", [C, KC, 512], BF16); tSQ = T(None)
    RS = [sb(f"RS{i}", [C, 512], F32) for i in range(2)]; tRS = [T(None), T(None)]
    TMP = [sb(f"TMP{i}", [C, 512], F32) for i in range(4)]; tTMP = [T(None) for _ in range(4)]
    tmp_rr = [0]

    def tmp():
        i = tmp_rr[0]
        tmp_rr[0] = (i + 1) % 4
        return TMP[i], tTMP[i]

    rs_rr = [0]

    def rstd_of(src_ap, n, width, nfeat, reads):
        i = rs_rr[0]
        rs_rr[0] ^= 1
        act(SQ[:, 0:n, 0:width], src_ap, AF.Square, reads, [tSQ])
        pt, tp = ps()
        for kc in range(n):
            mm(pt[:, 0:width], ONES[:], SQ[:, kc, 0:width], kc == 0, kc == n - 1, [tONES, tSQ], [tp])
        act(RS[i][:, 0:width], pt[:, 0:width], AF.Ln, [tp, tCST], [tRS[i]], bias=CST[:, 0:1] if nfeat == D else CST[:, 1:2])
        act(RS[i][:, 0:width], RS[i][:, 0:width], AF.Exp, [tRS[i]], [tRS[i]], scale=-0.5)
        return RS[i], tRS[i]

    def norm_tile(l, c, n, nsub):
        pts = []
        for s in range(nsub):
            cols = slice(s * 512, (s + 1) * 512)
            act(H[:, :, cols], X[:, :, cols], AF.Square, [xT[s]], [hT[s]])
        for s in range(nsub):
            cols = slice(s * 512, (s + 1) * 512)
            pt, tp = ps()
            for kc in range(KC):
                mm(pt[:], ONES[:], H[:, kc, cols], kc == 0, kc == KC - 1, [tONES, hT[s]], [tp])
            pts.append((pt, tp))
        for s in range(nsub):
            pt, tp = pts[s]
            act(RS[s][:], pt[:], AF.Ln, [tp, tCST], [tRS[s]], bias=CST[:, 0:1])
            act(RS[s][:], RS[s][:], AF.Exp, [tRS[s]], [tRS[s]], scale=-0.5)
        for s in range(nsub):
            cols = slice(s * 512, (s + 1) * 512)
            for kc in range(KC):
                tm, ttm = tmp()
                stt(tm[:], X[:, kc, cols], Avec(l, c, n, kc), RS[s][:], ALU.mult, ALU.mult, [xT[s], tRS[s], tAV], [ttm])
                act(H[:, kc, cols], tm[:], AF.Identity, [ttm, tMOD], [hT[s]], bias=Bvec(l, c, n, kc))

    def norm_mod(l, c, n, s, dst, tdst):
        cols = slice(s * 512, (s + 1) * 512)
        rs, trs = rstd_of(X[:, :, cols], KC, 512, D, [xT[s]])
        for kc in range(KC):
            tm, ttm = tmp()
            stt(tm[:], X[:, kc, cols], Avec(l, c, n, kc), rs[:], ALU.mult, ALU.mult, [xT[s], trs, tAV], [ttm])
            act(dst[:, kc, cols], tm[:], AF.Identity, [ttm, tMOD], [tdst], bias=Bvec(l, c, n, kc))

    def ffn(l, f, c, nsub):
        cv = Carver()
        HID, _ = cv.take([C, HC, 1024], BF16)
        tHID = [T(None).inherit(arena_hist[0]) for _ in range(2)]
        cv.ts += tHID
        for (i0_) in range(0, HC, 4):
            wb, twb, spec = w_next("ffi")
            n = spec[4]
            for s in range(nsub):
                for j in range(n):
                    cols = slice(s * 512, (s + 1) * 512)
                    pg, tpg = ps()
                    pu, tpu = ps()
                    for kc in range(KC):
                        mm(pg[:], wb[:, kc, j * C:(j + 1) * C], H[:, kc, cols], kc == 0, kc == KC - 1, [twb, hT[s]], [tpg])
                    for kc in range(KC):
                        mm(pu[:], wb[:, kc, (n + j) * C:(n + j + 1) * C], H[:, kc, cols], kc == 0, kc == KC - 1, [twb, hT[s]], [tpu])
                    tm, ttm = tmp()
                    act(tm[:], pg[:], AF.Silu, [tpg], [ttm])
                    tt(HID[:, i0_ + j, cols], tm[:], pu[:], ALU.mult, [ttm, tpu], [tHID[s]])
        for b in range(4):
            wb, twb, spec = w_next("ffo")
            for o2 in range(2):
                oc = 2 * b + o2
                for s in range(nsub):
                    cols = slice(s * 512, (s + 1) * 512)
                    po, tpo = ps()
                    for i in range(HC):
                        mm(po[:], wb[:, i, o2 * C:(o2 + 1) * C], HID[:, i, cols], i == 0, i == HC - 1, [twb, tHID[s]], [tpo])
                    stt(X[:, oc, cols], po[:], Gvec(l, c, 0 if f == 0 else 2, oc), X[:, oc, cols], ALU.mult, ALU.add,
                        [tpo, tAV, xT[s]], [xT[s]])
        cv.done()

    xst_rr = [0]

    def load_x_tm(g, t0, nsub):
        for ch in range(nsub * 4):
            i = xst_rr[0]
            xst_rr[0] ^= 1
            s = ch // 4
            P.dma("sp", XST[i][:], g["xin"][t0 + ch * C:t0 + (ch + 1) * C, :], writes=[tXST[i]])
            for half in range(2):
                pt, tp = ps()
                for q in range(4):
                    kc = half * 4 + q
                    tr(pt[:, q * C:(q + 1) * C], XST[i][:, kc * C:(kc + 1) * C], [tXST[i]], [tp])
                cpa(X[:, half * 4:(half + 1) * 4, ch * C:(ch + 1) * C], pt[:].rearrange("p (a b) -> p a b", a=4), [tp], [xT[s]])

    rope_pending = []

    def rope_flush():
        while rope_pending:
            rope_pending.pop(0)()

    def rope(src_ps, tsrc, s_tok0, dst32, tdst, g, post=None):
        if not g["latent"]:
            cpa(dst32, src_ps, [tsrc], [tdst])
            if post is not None:
                post()
            return
        tm, ttm = tmp()
        cp("act", tm[:], src_ps, [tsrc], [ttm])

        def stage2():
            pr, tpr = ps()
            mm(pr[:], PROT[:], tm[:], True, True, [tPROT, ttm], [tpr])
            tt(dst32, tm[:], ROPE[:, 0, :], ALU.mult, [ttm, tROPE], [tdst])
            tm2, ttm2 = tmp()
            tt(tm2[:], pr[:], ROPE[:, 1, :], ALU.mult, [tpr, tROPE], [ttm2])
            tt(dst32, dst32, tm2[:], ALU.add, [ttm2, tdst], [tdst])
            if post is not None:
                post()

        rope_flush()
        rope_pending.append(stage2)

    ROPE = sb("ROPE", [C, 2, 512], F32); tROPE = T(None)

    def load_rope(g, tok0):
        if g["latent"]:
            P.dma("sp", ROPE[:, 0, :], c_cos[:, tok0:tok0 + 512], writes=[tROPE])
            P.dma("sp", ROPE[:, 1, :], c_sin[:, tok0:tok0 + 512], writes=[tROPE])

    SF = sb("SF", [64, 4, C], F32); tSF = T(None)
    SBK = sb("SBK", [64, 4, C], F32); tSBK = T(None)
    STG16 = [sb(f"STG16_{i}", [64, 512], BF16) for i in range(2)]; tSTG16 = [T(None), T(None)]
    STG32 = [sb(f"STG32_{i}", [64, 512], F32) for i in range(2)]; tSTG32 = [T(None), T(None)]
    stg_rr = [0, 0]

    def pass_a(g, ti):
        t0, nsub = g["tiles"][ti]
        ci = g["ci"]
        load_x_tm(g, t0, nsub)
        norm_tile(0, ci, 0, nsub)
        ffn(0, 0, ci, nsub)
        for s in range(nsub):
            P.dma("sp", g["X1"][ti, :, :, s * 512:(s + 1) * 512], X[:, :, s * 512:(s + 1) * 512], reads=[xT[s]], writes=[g["tX1"][ti]])
        norm_tile(0, ci, 1, nsub)
        wb, twb, spec = w_next("evkv")
        cv = Carver()
        KR, tKR = cv.take([C, 2, 512], F32)
        KF, tKF = cv.take([C, 2, 4, 64], BF16)
        VT, tVT = cv.take([C, 512], BF16)
        for s in range(nsub):
            cols = slice(s * 512, (s + 1) * 512)
            load_rope(g, t0 + s * 512)
            for c in range(2):
                pk, tpk = ps()
                for kc in range(KC):
                    mm(pk[:], wb[:, kc, c * C:(c + 1) * C], H[:, kc, cols], kc == 0, kc == KC - 1, [twb, hT[s]], [tpk])
                rope(pk[:], tpk, t0 + s * 512, KR[:, c, :], tKR, g)
            rope_flush()
            for ch in range(4):
                n = (t0 + s * 512) // C + ch
                seq_chunks = g["seqlen"] // C
                first = (n % seq_chunks == 0)
                last = (n % seq_chunks == seq_chunks - 1)
                tcols = slice(ch * C, (ch + 1) * C)
                pkt, tpkt = ps()
                for c in range(2):
                    tr(pkt[:, c * C:(c + 1) * C], KR[:, c, tcols], [tKR], [tpkt])
                for d_ in range(2):
                    tt(KF[:, d_, :, :], pkt[:, 0:256].rearrange("p (h d) -> p h d", h=4),
                       DK[:, d_, :].unsqueeze(2).to_broadcast([C, 4, 64]), ALU.mult, [tpkt, tDT], [tKF])
                pv, tpv = ps()
                for kc in range(KC):
                    mm(pv[:], H[:, kc, s * 512 + ch * C:s * 512 + (ch + 1) * C], wb[:, kc, 256:768], kc == 0, kc == KC - 1, [twb, hT[s]], [tpv])
                cp("act", VT[:], pv[:], [tpv], [tVT])
                pu_ = []
                for d_ in range(2):
                    pu, tpu = ps()
                    for hh in range(4):
                        mm(pu[0:64, hh * C:(hh + 1) * C], KF[:, d_, hh, :], VT[:, hh * C:(hh + 1) * C], True, True, [tKF, tVT], [tpu])
                    pu_.append((pu, tpu))
                for d_, (UPD, tUPD) in enumerate(((g["UPDF"], g["tUPDF"]), (g["UPDB"], g["tUPDB"]))):
                    i32 = stg_rr[1]; stg_rr[1] ^= 1
                    cp("dve", STG32[i32][:], pu_[d_][0][0:64, :], [pu_[d_][1]], [tSTG32[i32]])
                    P.dma("sp", UPD[n], STG32[i32][:], reads=[tSTG32[i32]], writes=[tUPD[n]])
        cv.done()

    tOUT = T(None)

    scan_hook = [None, 0]

    def scan(g, d_, init, store, final):
        S, tS = (SF, tSF) if d_ == 0 else (SBK, tSBK)
        UPD, tUPD = (g["UPDF"], g["tUPDF"]) if d_ == 0 else (g["UPDB"], g["tUPDB"])
        SP, tSP = (g["SPF"], g["tSPF"]) if d_ == 0 else (g["SPB"], g["tSPB"])
        seq_chunks = g["seqlen"] // C
        order = range(g["nch"]) if d_ == 0 else range(g["nch"] - 1, -1, -1)
        Sf = S[:].rearrange("d h e -> d (h e)")
        for n in order:
            pos = n % seq_chunks
            first = (pos == 0) if d_ == 0 else (pos == seq_chunks - 1)
            last = (pos == seq_chunks - 1) if d_ == 0 else (pos == 0)
            if first:
                init(S, tS, d_)
            if store:
                i16 = stg_rr[0]; stg_rr[0] ^= 1
                cp("act", STG16[i16][:], Sf, [tS], [tSTG16[i16]])
                P.dma("sp", SP[n], STG16[i16][:], reads=[tSTG16[i16]], writes=[tSP[n]])
            ub, tub = tmp()
            P.dma("pool", ub[0:64, :], UPD[n], reads=[tUPD[n]], writes=[tub])
            tt(S[:], S[:], CD[0:64, d_, :].unsqueeze(2).to_broadcast([64, 4, C]), ALU.mult, [tS, tDT], [tS])
            tt(Sf, Sf, ub[0:64, :], ALU.add, [tS, tub], [tS])
            if last and final is not None:
                final(n // seq_chunks, S, tS, d_)
            if scan_hook[0] is not None:
                scan_hook[1] += 1
                if scan_hook[1] % 2 == 0:
                    try:
                        next(scan_hook[0])
                    except StopIteration:
                        scan_hook[0] = None

    def init_zero(S, tS, d_):
        P.op("dve", lambda e: e.memset(S[:], 0.0), [], [tS])

    def init_flag(S, tS, d_):
        P.dma("sp", S[:], (s0f, s0b)[d_].rearrange("h d e -> d h e"), writes=[tS])
        ts(S[:], S[:], FLAG[0:64, d_:d_ + 1], None, ALU.mult, None, [tS, tMCOL], [tS])

    def init_mix(S, tS, d_):
        init_flag(S, tS, d_)
        src = CALL[0:64, :] if d_ == 0 else CALL[192:256, :]
        CARRY, tCARRY = tmp()
        P.dma("sp", CARRY[0:64, :], src, reads=[tCALL], writes=[tCARRY])
        Sf = S[:].rearrange("d h e -> d (h e)")
        stt(Sf, CARRY[0:64, :], FLAG[0:64, 2 + d_:3 + d_], Sf, ALU.mult, ALU.add, [tCARRY, tS, tMCOL], [tS])

    def final_out(q, S, tS, d_):
        P.dma("sp", (o_rf, o_rb)[d_][q].rearrange("h d e -> d h e"), S[:], reads=[tS], writes=[tOUT])

    def final_carry(q, S, tS, d_):
        P.dma("sp", CSEND[d_ * 64:(d_ + 1) * 64, :], S[:].rearrange("d h e -> d (h e)"), reads=[tS], writes=[tCSEND])

    for g, ti in all_tiles:
        pass_a(g, ti)
    gp, gs = groups
    if "B" in passes:
        w_prefetch()
    scan_hook[0] = ada_layer_late(1)
    scan(gp, 0, init_zero, True, final_out)
    scan(gp, 1, init_zero, True, final_out)
    scan(gs, 0, init_flag, False, final_carry)
    scan(gs, 1, init_flag, False, final_carry)
    P.cc("AllGather", ALU.bypass, PAIRS, CSEND, CALL, reads=[tCSEND], writes=[tCALL])
    if scan_hook[0] is not None:
        for _ in scan_hook[0]:
            pass
        scan_hook[0] = None

    def sample_scans():
        scan(gs, 0, init_mix, True, None)
        scan(gs, 1, init_mix, True, None)

    def gelu_to(dst, tdst, src_ps, tsrc, width=512):
        act(dst, src_ps, AF.Gelu_apprx_tanh, [tsrc], [tdst])

    SS4 = sb("SS4", [C, 8], F32); tSS4 = T(None)
    OST = XST; tOST = tXST
    VST = [sb(f"VST{i}", [C, D], BF16) for i in range(2)]; tVST = [T(None), T(None)]
    ost_rr = [0, 0]
    ost_rr = xst_rr + [0]

    def pass_b(g, ti):
        t0, nsub = g["tiles"][ti]
        ci = g["ci"]
        lat = g["latent"]
        for s in range(nsub):
            cols = slice(s * 512, (s + 1) * 512)
            P.dma("sp", X[:, :, cols], g["X1"][ti, :, :, cols], reads=[g["tX1"][ti]], writes=[xT[s]])
        norm_tile(0, ci, 1, nsub)
        w1, tw1, _ = w_next("ev1", 2)
        w2, tw2, _ = w_next("ev2", 0)
        w3, tw3, _ = w_next("ev3", 0)
        cv = Carver()
        QR, tQR = cv.take([C, 2, 512], F32)
        KR, tKR = cv.take([C, 2, 512], F32)
        QS, tQS = cv.take([C, 3, 2, 512], BF16)
        KM, tKB = cv.take([C, 4, 512], BF16)
        VT, tVT = cv.take([C, 4, 512], BF16)
        SG, tSG = cv.take([C, 4, 512], BF16)
        GU, tGU = cv.take([C, 4, 512], BF16)
        VN, tVN = cv.take([C, 4, 512], BF16)
        PM, tPM0 = cv.take([C, 2, 4, C], BF16)
        tPM = [tPM0, T(None).inherit(arena_hist[0])]
        cv.ts.append(tPM[1])
        SPL, tSPL = cv.take([C, 4, 2, 4, C], BF16)
        P.op("dve", lambda e: e.memset(SPL[:], 0.0), [], [tSPL])
        for s in range(nsub):
            cols = slice(s * 512, (s + 1) * 512)
            tok0 = t0 + s * 512
            load_rope(g, tok0)
            for ch in range(4):
                n = tok0 // C + ch
                for d_, (SP, tSP) in enumerate(((g["SPF"], g["tSPF"]), (g["SPB"], g["tSPB"]))):
                    for r in range(2):
                        P.dma("sp", SPL[r * 64:(r + 1) * 64, ch, d_, :, :].rearrange("p (c r) e -> p c r e", r=2)[:, :, r, :],
                              SP[n].rearrange("d (c r e) -> d c r e", c=2, r=2)[:, :, r, :], reads=[tSP[n]], writes=[tSPL])
            ck(1)
            for c in range(2):
                pq, tpq = ps()
                for kc in range(KC):
                    mm(pq[:], w1[:, kc, c * C:(c + 1) * C], H[:, kc, cols], kc == 0, kc == KC - 1, [tw1, hT[s]], [tpq])
                rope(pq[:], tpq, tok0, QR[:, c, :], tQR, g)
                pk, tpk = ps()
                for kc in range(KC):
                    mm(pk[:], w1[:, kc, 256 + c * C:256 + (c + 1) * C], H[:, kc, cols], kc == 0, kc == KC - 1, [tw1, hT[s]], [tpk])
                rope(pk[:], tpk, tok0, KR[:, c, :], tKR, g)
            rope_flush()
            for hh in range(4):
                ts(KM[:, hh, :], KR[:, hh // 2, :], MCOL[:, 2 + hh % 2:3 + hh % 2], None, ALU.mult, None, [tKR, tMCOL], [tKB])
            ts(QS[:, 0, :, :], QR[:], 0.125, None, ALU.mult, None, [tQR], [tQS])
            for d_ in range(2):
                for c in range(2):
                    tt(QS[:, 1 + d_, c, :].rearrange("p (a j) -> p a j", a=4), QR[:, c, :].rearrange("p (a j) -> p a j", a=4),
                       DQ[:, d_, c, :].unsqueeze(1).to_broadcast([C, 4, C]), ALU.mult, [tQR, tDT], [tQS])
            ck(2)
            for ch in range(4):
                tcols = slice(s * 512 + ch * C, s * 512 + (ch + 1) * C)
                pv, tpv = ps()
                for kc in range(KC):
                    mm(pv[:], H[:, kc, tcols], w1[:, kc, 512:1024], kc == 0, kc == KC - 1, [tw1, hT[s]], [tpv])
                cp("act", VT[:, ch, :], pv[:], [tpv], [tVT])
                pvs, tpvs = ps()
                for kc in range(KC):
                    mm(pvs[:], H[:, kc, tcols], w3[:, kc, 0:512], kc == 0, kc == KC - 1, [tw3, hT[s]], [tpvs])
                gv, tgv = tmp()
                gelu_to(gv[:], tgv, pvs[:], tpvs)
                sq, tsq = tmp()
                tt(sq[:], gv[:], gv[:], ALU.mult, [tgv], [tsq])
                P.op("dve", lambda e, sq=sq: e.reduce_sum(out=SS4[:, 0:4], in_=sq[:].rearrange("p (g c) -> p g c", g=4), axis=AX.X), [tsq], [tSS4])
                act(SS4[:, 0:4], SS4[:, 0:4], AF.Sqrt, [tSS4, tCST], [tSS4], bias=CST[:, 3:4], scale=1.0 / C)
                recip(SS4[:, 0:4], SS4[:, 0:4], [tSS4], [tSS4])
                tt(VN[:, ch, :].rearrange("p (g c) -> p g c", g=4), gv[:].rearrange("p (g c) -> p g c", g=4),
                   SS4[:, 0:4].unsqueeze(2).to_broadcast([C, 4, C]), ALU.mult, [tgv, tSS4], [tVN])
            ck(3)
            for c4 in range(4):
                pg, tpg = ps()
                for kc in range(KC):
                    mm(pg[:], w2[:, kc, c4 * C:(c4 + 1) * C], H[:, kc, cols], kc == 0, kc == KC - 1, [tw2, hT[s]], [tpg])
                act(SG[:, c4, :], pg[:], AF.Silu, [tpg], [tSG])
                pu, tpu = ps()
                for kc in range(KC):
                    mm(pu[:], w2[:, kc, 512 + c4 * C:512 + (c4 + 1) * C], H[:, kc, cols], kc == 0, kc == KC - 1, [tw2, hT[s]], [tpu])
                gelu_to(GU[:, c4, :], tGU, pu[:], tpu)
            ck(4)
            po = ps_reserve(4)
            for ch in range(4):
                chc = slice(ch * C, (ch + 1) * C)
                sc, tsc = ps()
                for hh in range(4):
                    c, base = hh // 2, (hh % 2) * 64
                    mm(sc[:, hh * C:(hh + 1) * C], KM[:, hh, chc], QS[:, 0, c, chc], True, True, [tKB, tQS], [tsc])
                pmi = ch % 2
                tt(PM[:, pmi, :, :], sc[:].rearrange("p (h j) -> p h j", h=4), DT[:], ALU.mult, [tsc, tDT], [tPM[pmi]])
                ck(41)
                for hh in range(4):
                    c, base = hh // 2, (hh % 2) * 64
                    o_, to_, _ = po[hh]
                    mm(o_[:, chc], SPL[:, ch, 0, hh, :], QS[:, 1, c, chc], True, False, [tSPL, tQS], [to_])
                    ck(42)
                    mm(o_[:, chc], SPL[:, ch, 1, hh, :], QS[:, 2, c, chc], False, False, [tSPL, tQS], [to_])
                    mm(o_[:, chc], VT[:, ch, hh * C:(hh + 1) * C], PM[:, pmi, hh, :], False, True, [tVT, tPM[pmi]], [to_])
                    ck(43)
            for hh in range(4):
                o_, to_, _ = po[hh]
                rs, trs = rstd_of(o_[:].unsqueeze(1), 1, 512, C, [to_])
                tm, ttm = tmp()
                tt(tm[:], o_[:], rs[:], ALU.mult, [to_, trs], [ttm])
                stt(H[:, hh, cols], tm[:], math.sqrt(float(C)), SG[:, hh, :], ALU.mult, ALU.mult, [ttm, tSG], [hT[s]])
            ps_release(po)
            ck(5)
            pg_ = ps_reserve(4)
            for ch in range(4):
                chc = slice(ch * C, (ch + 1) * C)
                for g_ in range(4):
                    o_, to_, _ = pg_[g_]
                    mm(o_[:, chc], VN[:, ch, g_ * C:(g_ + 1) * C], WST[:, g_, :], True, True, [tVN, tWST], [to_])
            for g_ in range(4):
                o_, to_, _ = pg_[g_]
                tm, ttm = tmp()
                tt(tm[:].rearrange("p (a j) -> p a j", a=4), o_[:].rearrange("p (a j) -> p a j", a=4),
                   SGB[:, g_, :].unsqueeze(1).to_broadcast([C, 4, C]), ALU.add, [to_, tWST], [ttm])
                tt(H[:, 4 + g_, cols], tm[:], GU[:, g_, :], ALU.mult, [ttm, tGU], [hT[s]])
            ps_release(pg_)
            ck(6)
        cv.done()
        wo, two, _ = w_next("evo")
        for oc in range(KC):
            for s in range(nsub):
                cols = slice(s * 512, (s + 1) * 512)
                pp, tpp = ps()
                for kc in range(KC):
                    mm(pp[:], wo[:, kc, oc * C:(oc + 1) * C], H[:, kc, cols], kc == 0, kc == KC - 1, [two, hT[s]], [tpp])
                stt(X[:, oc, cols], pp[:], Gvec(0, ci, 1, oc), X[:, oc, cols], ALU.mult, ALU.add, [tpp, tAV, xT[s]], [xT[s]])
        ck(7)
        norm_tile(0, ci, 2, nsub)
        ffn(0, 1, ci, nsub)
        ck(8)
        norm_tile(1, ci, 0, nsub)
        ffn(1, 0, ci, nsub)
        for s in range(nsub):
            cols = slice(s * 512, (s + 1) * 512)
            P.dma("sp", g["X4"][ti, :, :, cols], X[:, :, cols], reads=[xT[s]], writes=[g["tX4"][ti]])
        norm_tile(1, ci, 1, nsub)
        ck(9)
        koff = 0
        cv = Carver()
        R32b = [cv.take([C, 512], F32) for _ in range(2)]
        KST, tKST0 = cv.take([C, KC, 512], BF16)
        QST, tQST = cv.take([C, KC, 2, 512], BF16)
        tKST = [tKST0]
        for blk_i, tag in enumerate(("od1", "od2")):
            wb, twb, _ = w_next(tag)
            for s in range(nsub):
                cols = slice(s * 512, (s + 1) * 512)
                tok0 = t0 + s * 512
                load_rope(g, tok0)
                ks = tKST[s % 2] if False else tKST[0]
                for hh in range(8):
                    pq, tpq = ps()
                    for kc in range(KC):
                        mm(pq[:], wb[:, kc, hh * C:(hh + 1) * C], H[:, kc, cols], kc == 0, kc == KC - 1, [twb, hT[s]], [tpq])
                    r32, tr32 = R32b[hh % 2]

                    def post(hh=hh, r32=r32, tr32=tr32, blk_i=blk_i, ks=ks):
                        if blk_i == 0:
                            for j in range(2):
                                ts(QST[:, hh, j, :], r32[:], MCOL[:, 2 + j:3 + j], None, ALU.mult, None, [tr32, tMCOL], [tQST])
                        else:
                            cp("act", KST[:, hh, :], r32[:], [tr32], [ks])

                    rope(pq[:], tpq, tok0, r32[:], tr32, g, post)
                rope_flush()
                if blk_i == 0:
                    P.dma("sp", g["QFM"][ti, :, :, :, cols], QST[:], reads=[tQST], writes=[g["tQ"][ti]])
                else:
                    P.dma("sp", g["KFM"][:, :, koff + tok0:koff + tok0 + 512].rearrange("h p n -> p h n"), KST[:],
                          reads=[ks], writes=[g["tK"][hh][ti] for hh in range(8)])
                    if not lat:
                        for ch in range(4):
                            tcols = slice(s * 512 + ch * C, s * 512 + (ch + 1) * C)
                            i = ost_rr[0]; ost_rr[0] ^= 1
                            for half in range(2):
                                pk, tpk = ps()
                                for kc in range(KC):
                                    mm(pk[:], H[:, kc, tcols], wb[:, kc, half * 512:(half + 1) * 512], kc == 0, kc == KC - 1, [twb, hT[s]], [tpk])
                                cpa(OST[i][:, half * 512:(half + 1) * 512], pk[:], [tpk], [tOST[i]])
                            P.dma("sp", o_k[tok0 + ch * C:tok0 + (ch + 1) * C, :], OST[i][:], reads=[tOST[i]], writes=[tOUT])
        ck(10)
        wb, twb, _ = w_next("od3")
        for s in range(nsub):
            tok0 = t0 + s * 512
            for ch in range(4):
                tcols = slice(s * 512 + ch * C, s * 512 + (ch + 1) * C)
                i = ost_rr[0]; ost_rr[0] ^= 1
                iv = ost_rr[1]; ost_rr[1] ^= 1
                for half in range(2):
                    pv, tpv = ps()
                    for kc in range(KC):
                        mm(pv[:], H[:, kc, tcols], wb[:, kc, half * 512:(half + 1) * 512], kc == 0, kc == KC - 1, [twb, hT[s]], [tpv])
                    if not lat:
                        cp("dve", OST[i][:, half * 512:(half + 1) * 512], pv[:], [tpv], [tOST[i]])
                        cp("act", VST[iv][:, half * 512:(half + 1) * 512], OST[i][:, half * 512:(half + 1) * 512], [tOST[i]], [tVST[iv]])
                    else:
                        cp("act", VST[iv][:, half * 512:(half + 1) * 512], pv[:], [tpv], [tVST[iv]])
                kch = (koff + tok0) // C + ch
                P.dma("sp", g["VTM"][kch], VST[iv][:], reads=[tVST[iv]], writes=[g["tV"][kch]])
                if not lat:
                    P.dma("sp", o_v[tok0 + ch * C:tok0 + (ch + 1) * C, :], OST[i][:], reads=[tOST[i]], writes=[tOUT])
        cv.done()
        ck(11)

    def ctx_kv(g):
        cv = Carver()
        KST, tKST0 = cv.take([C, KC, 512], BF16)
        tKST = [tKST0]
        for j in range(PAST // C):
            i = xst_rr[0]; xst_rr[0] ^= 1
            P.dma("sp", XST[i][:], cache_k[j * C:(j + 1) * C, :], writes=[tXST[i]])
            for half in range(2):
                pt, tp = ps()
                for q in range(4):
                    hh = half * 4 + q
                    tr(pt[:, q * C:(q + 1) * C], XST[i][:, hh * C:(hh + 1) * C], [tXST[i]], [tp])
                cpa(KST[:, half * 4:(half + 1) * 4, 0:C], pt[:].rearrange("p (a b) -> p a b", a=4), [tp], [tKST[0]])
            P.dma("sp", KCTX[:, :, j * C:(j + 1) * C].rearrange("h p n -> p h n"), KST[:, :, 0:C],
                  reads=[tKST[0]], writes=[tKCTX])
            i = xst_rr[0]; xst_rr[0] ^= 1
            iv = ost_rr[1]; ost_rr[1] ^= 1
            P.dma("sp", XST[i][:], cache_v[j * C:(j + 1) * C, :], writes=[tXST[i]])
            cpa(VST[iv][:], XST[i][:], [tXST[i]], [tVST[iv]])
            P.dma("sp", VCTX[j], VST[iv][:], reads=[tVST[iv]], writes=[tVCTX])
        cv.done()

    def kv_gather(g):
        for i in range(NSPL):
            P.cc("AllGather", ALU.bypass, PAIRS, g["KFM"][i * HPS:(i + 1) * HPS].rearrange("h p n -> (h p) n"), KALL[i],
                 reads=[t for l in g["tK"] for t in l], writes=[tKALL])
            P.cc("AllGather", ALU.bypass, PAIRS, g["VTM"][i * CPS:(i + 1) * CPS].rearrange("k p e -> (k p) e"), VALL[i],
                 reads=g["tV"], writes=[tVALL])

    EB = [sb(f"EB{i}", [C, 512], BF16) for i in range(4)]; tEB = [T(None) for _ in range(4)]
    eb_rr = [0]

    def pass_c(g, ti):
        t0, nsub = g["tiles"][ti]
        ci = g["ci"]
        lat = g["latent"]
        ntile = len(g["tiles"])
        for s in range(nsub):
            cols = slice(s * 512, (s + 1) * 512)
            P.dma("sp", X[:, :, cols], g["X4"][ti, :, :, cols], reads=[g["tX4"][ti]], writes=[xT[s]])
        cv = Carver()
        QTb = [cv.take([C, 2, 1024], BF16) for _ in range(2)]
        nk = (PAST + 2 * LH) if lat else nsub * 512
        k0 = 0 if lat else t0
        nkc = nk // C
        KH = []; VH = []
        for i in range(2):
            a, ta = cv.take([C, nk], BF16)
            b, tb = cv.take([C, nkc, C], BF16)
            KH.append((a, ta)); VH.append((b, tb))
        A32, tA32 = cv.take([C, 512], F32)
        KALLv = [k_.rearrange("(r h p) n -> r h p n", r=2, h=HPS) for k_ in KALL]
        VALLv = [v_.rearrange("(r k p) e -> r k p e", r=2, p=C) for v_ in VALL]

        def load_kv(hh):
            (a, ta), (b, tb) = KH[hh % 2], VH[hh % 2]
            P.dma("sp", QTb[hh % 2][0][:, :, 0:nsub * 512], g["QFM"][ti, :, hh, :, 0:nsub * 512], reads=[g["tQ"][ti]], writes=[QTb[hh % 2][1]])
            if lat:
                pc = PAST // C
                lc = LH // C
                P.dma("sp", a[:, 0:PAST], KCTX[hh], reads=[tKCTX], writes=[ta])
                P.dma("sp", b[:, 0:pc, :], VCTX[:, :, hh * C:(hh + 1) * C].rearrange("k p e -> p k e"), reads=[tVCTX], writes=[tb])
                for r in range(2):
                    P.dma("sp", a[:, PAST + r * LH:PAST + (r + 1) * LH], KALLv[hh // HPS][r, hh % HPS], reads=[tKALL], writes=[ta])
                    for i in range(NSPL):
                        c0 = pc + r * lc + i * CPS
                        P.dma("sp", b[:, c0:c0 + CPS, :], VALLv[i][r, :, :, hh * C:(hh + 1) * C].rearrange("k p e -> p k e"),
                              reads=[tVALL], writes=[tb])
            else:
                P.dma("sp", a[:], g["KFM"][hh, :, k0:k0 + nk], reads=[g["tK"][hh][ti]], writes=[ta])
                P.dma("sp", b[:], g["VTM"][k0 // C:k0 // C + nkc, :, hh * C:(hh + 1) * C].rearrange("k p e -> p k e"),
                      reads=[g["tV"][k0 // C + j] for j in range(nkc)], writes=[tb])

        load_kv(0)
        for hh in range(8):
            if hh + 1 < 8:
                load_kv(hh + 1)
            (kh, tkh), (vh, tvh) = KH[hh % 2], VH[hh % 2]
            QT, tQT = QTb[hh % 2]
            if lat:
                qblocks = [(s * 512, 512, [(0, 512, list(range(nkc)))]) for s in range(nsub)]
            else:
                qblocks = [(s * 512, 512, [(0, 256, [4 * s, 4 * s + 1]), (256, 256, [4 * s + 2, 4 * s + 3])]) for s in range(nsub)]
            for (q0, N, parts) in qblocks:
                s = q0 // 512
                acc = ps_reserve(4)
                scs = {}
                items = []
                for (qa, n_, chunks) in parts:
                    for i, kc_ in enumerate(chunks):
                        items.append((qa, n_, kc_, i == 0, i == len(chunks) - 1))

                def issue_sc(idx):
                    qa, n_, kc_, _, _ = items[idx]
                    lst = []
                    for j in range(2):
                        sc, tsc = ps()
                        mm(sc[:, 0:n_], kh[:, kc_ * C:(kc_ + 1) * C], QT[:, j, q0 + qa:q0 + qa + n_], True, True, [tkh, tQT], [tsc])
                        lst.append((sc, tsc))
                    scs[idx] = lst

                issue_sc(0)
                for idx, (qa, n_, kc_, first, last) in enumerate(items):
                    if idx + 1 < len(items):
                        issue_sc(idx + 1)
                    for j in range(2):
                        sc, tsc = scs[idx][j]
                        e_i = eb_rr[0]; eb_rr[0] = (e_i + 1) % 4
                        act(EB[e_i][:, 0:n_], sc[:, 0:n_], AF.Exp, [tsc], [tEB[e_i]], scale=0.125)
                        mm(acc[j][0][:, qa:qa + n_], vh[:, kc_, :], EB[e_i][:, 0:n_], first, last, [tvh, tEB[e_i]], [acc[j][1]])
                        mm(acc[2 + j][0][:, qa:qa + n_], ONES[:], EB[e_i][:, 0:n_], first, last, [tONES, tEB[e_i]], [acc[2 + j][1]])
                    del scs[idx]
                r1, tr1 = tmp()
                act(r1[:, 0:N], acc[2][0][:, 0:N], AF.Ln, [acc[2][1]], [tr1])
                act(r1[:, 0:N], r1[:, 0:N], AF.Exp, [tr1], [tr1], scale=-1.0)
                t1, tt1 = tmp()
                tt(t1[:, 0:N], acc[0][0][:, 0:N], r1[:, 0:N], ALU.mult, [acc[0][1], tr1], [tt1])
                r2, tr2 = tmp()
                act(r2[:, 0:N], acc[3][0][:, 0:N], AF.Ln, [acc[3][1]], [tr2])
                act(r2[:, 0:N], r2[:, 0:N], AF.Exp, [tr2], [tr2], scale=-1.0)
                t2, tt2 = tmp()
                tt(t2[:, 0:N], acc[1][0][:, 0:N], r2[:, 0:N], ALU.mult, [acc[1][1], tr2], [tt2])
                stt(A32[:, 0:N], t2[:, 0:N], NEGLAM, t1[:, 0:N], ALU.mult, ALU.add, [tt2, tt1, tLAM], [tA32])
                ps_release(acc)
                rs, trs = rstd_of(A32[:, 0:N].unsqueeze(1), 1, N, C, [tA32])
                stt(H[:, hh, q0:q0 + N], A32[:, 0:N], SUBG[:, 0:1], rs[:, 0:N], ALU.mult, ALU.mult, [tA32, trs, tLAM], [hT[s]])
        wb, twb, _ = w_next("odo")
        for oc in range(KC):
            for s in range(nsub):
                cols = slice(s * 512, (s + 1) * 512)
                pp, tpp = ps()
                for kc in range(KC):
                    mm(pp[:], wb[:, kc, oc * C:(oc + 1) * C], H[:, kc, cols], kc == 0, kc == KC - 1, [twb, hT[s]], [tpp])
                stt(X[:, oc, cols], pp[:], Gvec(1, ci, 1, oc), X[:, oc, cols], ALU.mult, ALU.add, [tpp, tAV, xT[s]], [xT[s]])
        cv.done()
        norm_tile(1, ci, 2, nsub)
        ffn(1, 1, ci, nsub)
        cv = Carver()
        YF, tYF = cv.take([C, KC, 512], F32)
        for s in range(nsub):
            cols = slice(s * 512, (s + 1) * 512)
            rs, trs = rstd_of(X[:, :, cols], KC, 512, D, [xT[s]])
            for kc in range(KC):
                stt(YF[:, kc, :], X[:, kc, cols], FNA[:, kc:kc + 1], rs[:], ALU.mult, ALU.mult, [xT[s], trs, tAV], [tYF])
            for ch in range(4):
                i = ost_rr[0]; ost_rr[0] ^= 1
                for half in range(2):
                    pt, tp = ps()
                    for q in range(4):
                        kc = half * 4 + q
                        tr(pt[:, q * C:(q + 1) * C], YF[:, kc, ch * C:(ch + 1) * C], [tYF], [tp])
                    cpa(OST[i][:, half * 512:(half + 1) * 512], pt[:], [tp], [tOST[i]])
                r0 = t0 + s * 512 + ch * C
                P.dma("sp", g["yout"][r0:r0 + C, :], OST[i][:], reads=[tOST[i]], writes=[tOUT])
        cv.done()

    try:
        if "B" in passes:
            for i_, (g, ti) in enumerate(all_tiles):
                if g["latent"] and ti == 0:
                    sample_scans()
                    ctx_kv(groups[1])
                pass_b(g, ti)
        if "C" in passes:
            w_prefetch()
            kv_gather(groups[1])
            for g, ti in all_tiles:
                pass_c(g, ti)
    except StopBuild:
        pass

    P.fence("sp", [tOUT, tKALL, tVALL, tCALL, tKCTX, tVCTX, tCSEND] + wbT + xT + hT + psT + [arena_hist[0]] + [t for g in groups for t in g["tX1"] + g["tX4"] + g["tQ"] + g["tV"] + g["tSPF"] + g["tSPB"] + g["tUPDB"] + g["tUPDF"] + [x for l in g["tK"] for x in l]])
    P.emit()
    st.close()
    return nc, P


def host_consts(LS):
    f32 = np.float32
    ident = np.eye(C, dtype=f32)
    pm = np.zeros((C, C), f32)
    for base in range(0, C, 32):
        for i in range(16):
            pm[base + i, base + i + 16] = -1.0
            pm[base + 16 + i, base + i] = 1.0
    prot = np.ascontiguousarray(pm.T)
    t = np.arange(LS)
    row = (t // 64).astype(f32)
    col = (t % 64).astype(f32)
    freq = (f32(10000.0) ** (-(np.arange(16, dtype=f32)) / f32(16))).astype(f32)
    cos = np.zeros((C, LS), f32)
    sin = np.zeros((C, LS), f32)
    for f in range(C):
        fh = f % 64
        pos = row if (fh // 32) == 0 else col
        ang = (pos * freq[fh % 16]).astype(f32)
        cos[f] = np.cos(ang).astype(f32)
        sin[f] = np.sin(ang).astype(f32)
    m = np.arange(C)[:, None].astype(f32)
    j = np.arange(C)[None, :].astype(f32)
    tabs = np.zeros((C, 5, C), f32)
    tabs[:, 0, :] = np.maximum(j - m, 0)
    tabs[:, 1, :] = np.maximum(m - j, 0)
    tabs[:, 2, :] = 1.0 + np.eye(C, dtype=f32)
    tabs[:, 3, :] = j + 1.0
    tabs[:, 4, :] = C - j
    mcol = np.zeros((C, 4), f32)
    mcol[:, 0] = C - 1 - np.arange(C)
    mcol[:, 1] = np.arange(C)
    mcol[:64, 2] = 1.0
    mcol[64:, 3] = 1.0
    return dict(c_ident=ident, c_prot=prot, c_cos=cos, c_sin=sin, c_tabs=tabs, c_mcol=mcol)


def make_in_maps(inp, NPS, LS, PAST, ncores=8):
    cst = host_consts(LS)
    LH = LS // 2
    maps = []
    a = lambda x: np.ascontiguousarray(np.asarray(x, dtype=np.float32))
    shared = dict(
        ada_w=a(inp["ada_w"]), ada_b=a(inp["ada_b"]), norm_g=a(inp["norm_g"]).reshape(6, 1024),
        ffn_w_in=a(inp["ffn_w_in"]), ffn_w_out=a(inp["ffn_w_out"]),
        even_w_in=a(inp["even_w_in"])[0], even_w_out=a(inp["even_w_out"])[0],
        ret_decay_fwd=a(inp["ret_decay_fwd"]), ret_decay_bwd=a(inp["ret_decay_bwd"]),
        sgu_w=a(inp["sgu_w"])[0], sgu_b=a(inp["sgu_b"])[0],
        odd_w_in=a(inp["odd_w_in"])[0], odd_w_out=a(inp["odd_w_out"])[0],
        diff_lambda=a(inp["diff_lambda"])[0], diff_subln=a(inp["diff_subln"]),
        final_norm=a(inp["final_norm"]).reshape(1, 1024), **cst)
    xp = a(inp["x_prompt"]); xs = a(inp["x_sample"])
    nb = xs.shape[0]
    for c in range(ncores):
        b, r = c // 2, c % 2
        m = dict(shared)
        m["c_cos"] = np.ascontiguousarray(cst["c_cos"][:, r * LH:(r + 1) * LH])
        m["c_sin"] = np.ascontiguousarray(cst["c_sin"][:, r * LH:(r + 1) * LH])
        fl = np.zeros((C, 4), np.float32)
        fl[:, 0] = 1.0 - r
        fl[:, 1] = float(r)
        fl[:, 2] = 1.0 - fl[:, 0]
        fl[:, 3] = 1.0 - fl[:, 1]
        m["c_flag"] = fl
        m["x_prompt"] = np.ascontiguousarray(xp[c * NPS:(c + 1) * NPS].reshape(NPS * 256, 1024))
        m["x_sample"] = np.ascontiguousarray(xs[b, r * LH:(r + 1) * LH])
        m["state_f"] = a(inp["state_ret_fwd"])[b, 0]
        m["state_b"] = a(inp["state_ret_bwd"])[b, 0]
        m["cache_k"] = a(inp["cache_k"])[b, 0].reshape(PAST, 1024)
        m["cache_v"] = a(inp["cache_v"])[b, 0].reshape(PAST, 1024)
        m["cond"] = np.ascontiguousarray(np.stack([a(inp["c_ctx"]), a(inp["c"])[b]]))
        maps.append(m)
    return maps


def gather(results, NPS, LS, nb=4):
    yp = np.stack([r["y_prompt"].reshape(NPS, 256, 1024) for r in results]).reshape(-1, 256, 1024)
    ys = np.stack([np.concatenate([results[2 * b]["y_sample"], results[2 * b + 1]["y_sample"]]) for b in range(nb)])
    rf = np.concatenate([r["new_ret_f"] for r in results])[:, None]
    rb = np.concatenate([r["new_ret_b"] for r in results])[:, None]
    nk = np.concatenate([r["new_k"].reshape(NPS, 256, 8, 128) for r in results])[:, None]
    nv = np.concatenate([r["new_v"].reshape(NPS, 256, 8, 128) for r in results])[:, None]
    return (yp, ys, rf, rb, nk, nv)


_NPS, _LS, _PAST = 4, 4096, 256


def kernel(**inputs):
    nc, P = build(_NPS, _LS // 2, _PAST)
    maps = make_in_maps(inputs, _NPS, _LS, _PAST)
    res = run_bass_kernel_spmd(nc, maps, core_ids=list(range(8)))
    outs = gather(res.results, _NPS, _LS)
    return tuple(np.ascontiguousarray(o, dtype=np.float32) for o in outs)
```

```python
import math
import contextlib
import numpy as np
from concourse.bass_utils import run_bass_kernel_spmd
import concourse.bass as bass
import concourse.mybir as mybir

F32 = mybir.dt.float32
BF16 = mybir.dt.bfloat16
AF = mybir.ActivationFunctionType
ALU = mybir.AluOpType
AX = mybir.AxisListType

EPOCH = 30000
COMPUTE = ("pe", "act", "dve", "pool")
NSLOT = {"sp": 12, "pool": 12, "act": 4}


class T:
    __slots__ = ("ap", "wr", "rd", "name")

    def __init__(self, ap, name=""):
        self.ap = ap
        self.wr = {}
        self.rd = {}
        self.name = name

    def inherit(self, *others):
        for o in others:
            for k, v in o.wr.items():
                if k not in self.rd or self.rd[k].seq < v.seq:
                    self.rd[k] = v
            for k, v in o.rd.items():
                if k not in self.rd or self.rd[k].seq < v.seq:
                    self.rd[k] = v
        return self


class Op:
    __slots__ = ("eng", "fn", "deps", "signal", "val", "stream", "seq", "is_dma", "slot", "inc")

    def __init__(self, eng, fn, stream, is_dma, slot=None):
        self.eng = eng
        self.fn = fn
        self.deps = []
        self.signal = False
        self.val = None
        self.stream = stream
        self.is_dma = is_dma
        self.slot = slot
        self.seq = 0
        self.inc = 16


class Prog:
    def __init__(self, nc):
        self.nc = nc
        self.ops = {e: [] for e in ("pe", "act", "dve", "pool", "sp")}
        self.seq = 0
        self.slot_rr = {q: 0 for q in NSLOT}
        self.slot_last = {}
        self.all_ops = []

    def _deps(self, op, reads, writes):
        deps = {}
        for t in reads:
            for k, v in t.wr.items():
                deps[id(v)] = v
        for t in writes:
            for k, v in t.wr.items():
                deps[id(v)] = v
            for k, v in t.rd.items():
                deps[id(v)] = v
        for v in deps.values():
            if v is op:
                continue
            if (not v.is_dma) and (not op.is_dma) and v.stream == op.stream and op.eng == "pe":
                continue
            op.deps.append(v)
            v.signal = True
        for t in reads:
            t.rd[op.stream] = op
        for t in writes:
            t.wr = {op.stream: op}
            t.rd = {}

    def op(self, eng, fn, reads=(), writes=()):
        o = Op(eng, fn, eng, False)
        self.seq += 1
        o.seq = self.seq
        self._deps(o, reads, writes)
        self.ops[eng].append(o)
        self.all_ops.append(o)
        return o

    def dma(self, q, out_ap, in_ap, reads=(), writes=(), **kw):
        slot = self.slot_rr[q]
        self.slot_rr[q] = (slot + 1) % NSLOT[q]
        stream = ("dma", q, slot)

        def fn(e, out_ap=out_ap, in_ap=in_ap, kw=kw):
            return e.dma_start(out=out_ap, in_=in_ap, allow_slow_non_contiguous=True, **kw)

        o = Op(q, fn, stream, True, slot)
        o.signal = True
        self.seq += 1
        o.seq = self.seq
        prev = self.slot_last.get(stream)
        if prev is not None:
            o.deps.append(prev)
        self.slot_last[stream] = o
        self._deps(o, reads, writes)
        self.ops[q].append(o)
        self.all_ops.append(o)
        return o

    def cc(self, kind, op, groups, in_ap, out_ap, reads=(), writes=()):
        stream = ("dma", "cc", 0)

        def fn(e):
            return e.collective_compute(kind, op, replica_groups=groups, ins=[in_ap], outs=[out_ap])

        o = Op("pool", fn, stream, True, 0)
        o.inc = 1
        o.signal = True
        self.seq += 1
        o.seq = self.seq
        prev = self.slot_last.get(stream)
        if prev is not None:
            o.deps.append(prev)
        self.slot_last[stream] = o
        self._deps(o, reads, writes)
        self.ops["pool"].append(o)
        self.all_ops.append(o)
        return o

    def fence(self, eng, ts):
        def fn(e):
            return None
        o = Op(eng, fn, eng, False)
        self.seq += 1
        o.seq = self.seq
        seen = {}
        for t in ts:
            for v in list(t.wr.values()) + list(t.rd.values()):
                seen[id(v)] = v
        for v in seen.values():
            o.deps.append(v)
            v.signal = True
        self.ops[eng].append(o)
        self.all_ops.append(o)
        return o

    def emit(self):
        import contextlib
        nc = self.nc
        cnt = {e: 0 for e in COMPUTE}
        dcnt = {}
        for o in self.all_ops:
            if o.is_dma:
                dcnt[o.stream] = dcnt.get(o.stream, 0) + 1
                o.val = dcnt[o.stream]
            elif o.signal:
                cnt[o.eng] += 1
                o.val = cnt[o.eng]
        DEP = EPOCH // 16
        stack = contextlib.ExitStack()
        sems = {}
        for e in COMPUTE:
            for ep in range((max(cnt[e], 1) - 1) // EPOCH + 1):
                sems[(e, ep)] = stack.enter_context(nc.semaphore(f"s_{e}_{ep}"))
        for st in dcnt:
            for ep in range((dcnt[st] - 1) // DEP + 1):
                sems[(st, ep)] = stack.enter_context(nc.semaphore(f"d_{st[1]}_{st[2]}_{ep}"))
        self.stats = {e: len(self.ops[e]) for e in self.ops}
        self.stats["sems"] = len(sems)
        self.stats["signals"] = dict(cnt)

        def semof(o):
            n = o.val - 1
            if o.is_dma:
                ep = n // DEP
                return sems[(o.stream, ep)], (n % DEP + 1) * o.inc
            ep = n // EPOCH
            return sems[(o.stream, ep)], n % EPOCH + 1

        block = stack.enter_context(nc.Block())
        ops = self.ops

        def run(engname):
            def body(e):
                waited = {}
                nwait = 0
                for o in ops[engname]:
                    for d in o.deps:
                        key = d.stream
                        if waited.get(key, 0) >= d.val:
                            continue
                        waited[key] = d.val
                        s, v = semof(d)
                        e.wait_ge(s, v)
                        nwait += 1
                    ins = o.fn(e)
                    if ins is None:
                        continue
                    if o.is_dma:
                        s, v = semof(o)
                        ins.then_inc(s, o.inc)
                    elif o.signal:
                        s, v = semof(o)
                        ins.then_inc(s, 1)
                self.stats["waits_" + engname] = nwait
            return body

        block.tensor(run("pe"))
        block.scalar(run("act"))
        block.vector(run("dve"))
        block.gpsimd(run("pool"))
        block.sync(run("sp"))
        stack.close()


C = 128
EPS = 1e-6
D = 1024
KC = 8
DFF = 2816
HC = 22
ARENA_B = 47104
LAMBDA_INIT = 0.8 - 0.6 * math.exp(-0.3 * 1)


class StopBuild(Exception):
    pass


def build(NPS, LS, PAST, stop_after=None, ck_stop=None):
    def ck(i):
        if ck_stop is not None and i == ck_stop:
            raise StopBuild()

    nc = bass.Bass("TRN2", target_bir_lowering=False)
    st = contextlib.ExitStack()
    P = Prog(nc)
    NPT = NPS * 256

    def din(name, shape, dt=F32):
        return nc.dram_tensor(name, list(shape), dt, kind="ExternalInput").ap()

    def dout(name, shape, dt=F32):
        return nc.dram_tensor(name, list(shape), dt, kind="ExternalOutput").ap()

    def dscr(name, shape, dt):
        return nc.dram_tensor(name, list(shape), dt).ap()

    def sb(name, shape, dt=F32):
        return st.enter_context(nc.sbuf_tensor(name, list(shape), dt))

    xp = din("x_prompt", [NPT, D])
    xs_ = din("x_sample", [LS, D])
    PAIRS = [[0, 1], [2, 3], [4, 5], [6, 7]]
    s0f = din("state_f", [4, 64, 128])
    s0b = din("state_b", [4, 64, 128])
    cache_k = din("cache_k", [PAST, D])
    cache_v = din("cache_v", [PAST, D])
    cond = din("cond", [2, D])
    ada_w = din("ada_w", [2, D, 9 * D])
    ada_b = din("ada_b", [2, 9 * D])
    norm_g = din("norm_g", [6, D])
    ffn_w_in = din("ffn_w_in", [2, 2, D, 2 * DFF])
    ffn_w_out = din("ffn_w_out", [2, 2, DFF, D])
    even_w_in = din("even_w_in", [D, 2560])
    even_w_out = din("even_w_out", [D, D])
    rdf = din("ret_decay_fwd", [1, 4])
    rdb = din("ret_decay_bwd", [1, 4])
    sgu_w = din("sgu_w", [4, C, C])
    sgu_b = din("sgu_b", [4, C])
    odd_w_in = din("odd_w_in", [D, 3072])
    odd_w_out = din("odd_w_out", [D, D])
    diff_lambda = din("diff_lambda", [4, 64])
    diff_subln = din("diff_subln", [1, C])
    final_norm = din("final_norm", [1, D])
    c_ident = din("c_ident", [C, C])
    c_prot = din("c_prot", [C, C])
    c_cos = din("c_cos", [C, LS])
    c_sin = din("c_sin", [C, LS])
    c_tabs = din("c_tabs", [C, 5, C])
    c_mcol = din("c_mcol", [C, 4])
    c_flag = din("c_flag", [C, 4])

    y_p = dout("y_prompt", [NPT, D])
    y_s = dout("y_sample", [LS, D])
    o_rf = dout("new_ret_f", [NPS, 4, 64, 128])
    o_rb = dout("new_ret_b", [NPS, 4, 64, 128])
    o_k = dout("new_k", [NPT, D])
    o_v = dout("new_v", [NPT, D])

    groups = []
    for gi, (nm, latent, ntok, seqlen, xin, yout) in enumerate(
            [("p", False, NPT, 256, xp, y_p), ("s", True, LS, LS, xs_, y_s)]):
        g = dict(name=nm, latent=latent, ntok=ntok, seqlen=seqlen, xin=xin, yout=yout, ci=gi)
        tiles = []
        t0 = 0
        while t0 < ntok:
            n = min(1024, ntok - t0)
            tiles.append((t0, n // 512))
            t0 += n
        g["tiles"] = tiles
        nt = len(tiles)
        nch = ntok // C
        g["nch"] = nch
        nk = ntok
        g["nk"] = nk
        g["X1"] = dscr(f"X1_{nm}", [nt, C, KC, 1024], F32)
        g["X4"] = dscr(f"X4_{nm}", [nt, C, KC, 1024], F32)
        g["QFM"] = dscr(f"QFM_{nm}", [nt, C, KC, 2, 1024], BF16)
        g["KFM"] = dscr(f"KFM_{nm}", [8, C, nk], BF16)
        g["VTM"] = dscr(f"VTM_{nm}", [nk // C, C, D], BF16)
        g["SPF"] = dscr(f"SPF_{nm}", [nch, 64, 512], BF16)
        g["SPB"] = dscr(f"SPB_{nm}", [nch, 64, 512], BF16)
        g["UPDB"] = dscr(f"UPDB_{nm}", [nch, 64, 512], F32)
        g["UPDF"] = dscr(f"UPDF_{nm}", [nch, 64, 512], F32)
        g["tUPDF"] = [T(None) for _ in range(nch)]
        g["tX1"] = [T(None) for _ in range(nt)]
        g["tX4"] = [T(None) for _ in range(nt)]
        g["tQ"] = [T(None) for _ in range(nt)]
        g["tK"] = [[T(None) for _ in range(nt)] for _ in range(8)]
        g["tV"] = [T(None) for _ in range(nk // C)]
        g["tSPF"] = [T(None) for _ in range(nch)]
        g["tSPB"] = [T(None) for _ in range(nch)]
        g["tUPDB"] = [T(None) for _ in range(nch)]
        groups.append(g)

    LH = LS
    KCTX = dscr("KCTX", [8, C, PAST], BF16); tKCTX = T(None)
    VCTX = dscr("VCTX", [PAST // C, C, D], BF16); tVCTX = T(None)
    NSPL = max(1, (8 * C * LH * 2) // (2 << 20))
    HPS = 8 // NSPL
    CPS = (LH // C) // NSPL
    KALL = [dscr(f"KALL{i}", [2 * HPS * C, LH], BF16) for i in range(NSPL)]; tKALL = T(None)
    VALL = [dscr(f"VALL{i}", [2 * CPS * C, D], BF16) for i in range(NSPL)]; tVALL = T(None)
    CSEND = dscr("CSEND", [128, 512], F32); tCSEND = T(None)
    CALL = dscr("CALL", [256, 512], F32); tCALL = T(None)

    X = sb("X", [C, KC, 1024], F32)
    H = sb("H", [C, KC, 1024], BF16)
    ARENA = sb("ARENA", [C, ARENA_B], mybir.dt.uint8)
    xT = [T(None, "x0"), T(None, "x1")]
    hT = [T(None, "h0"), T(None, "h1")]
    WB = [sb(f"WB{i}", [C, 8192], BF16) for i in range(3)]
    wbT = [T(None, f"wb{i}") for i in range(3)]
    PS = [st.enter_context(nc.psum_tensor(f"PS{i}", [C, 512], F32)) for i in range(8)]
    psT = [T(None, f"ps{i}") for i in range(8)]
    ps_rr = [0]

    ps_res = set()

    def ps():
        while True:
            i = ps_rr[0]
            ps_rr[0] = (i + 1) % 8
            if i not in ps_res:
                return PS[i], psT[i]

    def ps_reserve(n):
        out = []
        for _ in range(n):
            while True:
                i = ps_rr[0]
                ps_rr[0] = (i + 1) % 8
                if i not in ps_res:
                    break
            ps_res.add(i)
            out.append((PS[i], psT[i], i))
        return out

    def ps_release(lst):
        for _, _, i in lst:
            ps_res.discard(i)

    XST = [sb(f"XST{i}", [C, D], F32) for i in range(2)]; tXST = [T(None), T(None)]
    ONES = sb("ONES", [C, C], BF16); tONES = T(None)
    IDENT = sb("IDENT", [C, C], F32); tIDENT = T(None)
    PROT = sb("PROT", [C, C], F32); tPROT = T(None)
    TABS = sb("TABS", [C, 5, C], F32); tTABS = T(None)
    MCOL = sb("MCOL", [C, 4], F32); tMCOL = T(None)
    P.op("dve", lambda e: e.memset(ONES[:], 1.0), [], [tONES])
    P.dma("sp", IDENT[:], c_ident, writes=[tIDENT])
    P.dma("sp", PROT[:], c_prot, writes=[tPROT])
    P.dma("sp", TABS[:], c_tabs, writes=[tTABS])
    P.dma("sp", MCOL[:], c_mcol, writes=[tMCOL])
    FLAG = sb("FLAG", [C, 4], F32)
    P.dma("sp", FLAG[:], c_flag, writes=[tMCOL])

    CST = sb("CST", [C, 4], F32); tCST = T(None)
    P.op("dve", lambda e: e.memset(CST[:, 0:1], float(D * EPS)), [], [tCST])
    P.op("dve", lambda e: e.memset(CST[:, 1:2], float(C * EPS)), [], [tCST])
    P.op("dve", lambda e: e.memset(CST[:, 2:3], 1.0), [], [tCST])
    P.op("dve", lambda e: e.memset(CST[:, 3:4], float(EPS)), [], [tCST])

    def mm(out, lhsT, rhs, start, stop, reads, writes):
        return P.op("pe", lambda e: e.matmul(out, lhsT=lhsT, rhs=rhs, start=start, stop=stop), reads, writes)

    def tr(out, in_, reads, writes):
        return P.op("pe", lambda e: e.transpose(out, in_, IDENT[:]), list(reads) + [tIDENT], writes)

    def act(out, in_, func, reads, writes, bias=None, scale=None):
        kw = {}
        if bias is not None:
            kw["bias"] = bias
        if scale is not None:
            kw["scale"] = scale
        return P.op("act", lambda e: e.activation(out=out, in_=in_, func=func, **kw), reads, writes)

    def tt(out, in0, in1, op, reads, writes):
        return P.op("dve", lambda e: e.tensor_tensor(out=out, in0=in0, in1=in1, op=op), reads, writes)

    def ts(out, in0, s1, s2, op0, op1, reads, writes):
        if s2 is None:
            return P.op("dve", lambda e: e.tensor_scalar(out=out, in0=in0, scalar1=s1, scalar2=None, op0=op0), reads, writes)
        return P.op("dve", lambda e: e.tensor_scalar(out=out, in0=in0, scalar1=s1, scalar2=s2, op0=op0, op1=op1), reads, writes)

    def stt(out, in0, scalar, in1, op0, op1, reads, writes):
        return P.op("dve", lambda e: e.scalar_tensor_tensor(out=out, in0=in0, scalar=scalar, in1=in1, op0=op0, op1=op1), reads, writes)

    def recip(out, in_, reads, writes):
        return P.op("dve", lambda e: e.reciprocal(out=out, in_=in_), reads, writes)

    def cp(eng, out, in_, reads, writes):
        if eng == "act":
            return P.op("act", lambda e: e.copy(out=out, in_=in_), reads, writes)
        return P.op("dve", lambda e: e.tensor_copy(out=out, in_=in_), reads, writes)

    cp_rr = [0]

    def cpa(out, in_, reads, writes):
        cp_rr[0] ^= 1
        return cp("act" if cp_rr[0] else "dve", out, in_, reads, writes)

    ncdma = lambda: nc.allow_non_contiguous_dma(reason="small param layout")

    CONDT = sb("CONDT", [C, KC, 2], F32); tCOND = T(None)
    with ncdma():
        for c in range(2):
            P.dma("sp", CONDT[:, :, c], cond[c, :].rearrange("(k p) -> p k", p=C), writes=[tCOND])
    act(CONDT[:], CONDT[:], AF.Silu, [tCOND], [tCOND])
    MODT = sb("MODT", [C, 2, 72, 2], F32); tMOD = T(None)
    ADAB = sb("ADAB", [C, 2, 72], F32); tADAB = T(None)
    GT = sb("GT", [C, 6, KC], F32); tGT = T(None)
    FNG = sb("FNG", [C, KC], F32); tFNG = T(None)
    with ncdma():
        for l in range(2):
            P.dma("sp", ADAB[:, l, :], ada_b[l, :].rearrange("(o p) -> p o", p=C), writes=[tADAB])
        for i in range(6):
            P.dma("sp", GT[:, i, :], norm_g[i, :].rearrange("(k p) -> p k", p=C), writes=[tGT])
        P.dma("sp", FNG[:], final_norm[0, :].rearrange("(k p) -> p k", p=C), writes=[tFNG])
    AW = [ARENA[:, i * 16384:(i + 1) * 16384].bitcast(F32).rearrange("p (k m) -> p k m", k=KC) for i in range(2)]
    tAW = [T(None), T(None)]
    AV = sb("AV", [C, 2, 2, 3, KC], F32)
    GV = sb("GV", [C, 2, 2, 3, KC], F32)
    tAV = T(None)
    ada_blk = [0]

    def ada_layer(l, q="sp"):
        for b in range(18):
            ada_block(l, b, q)
            yield
        ada_finish(l)

    def ada_block(l, b, q):
        if True:
            blk = ada_blk[0]
            ada_blk[0] += 1
            buf, tb = AW[blk % 2], tAW[blk % 2]
            P.dma(q, buf, ada_w[l, :, b * 512:(b + 1) * 512].rearrange("(k p) m -> p k m", p=C), writes=[tb])
            pt, tp = ps()
            for oc in range(4):
                for kc in range(KC):
                    mm(pt[:, oc * 2:oc * 2 + 2], buf[:, kc, oc * 128:(oc + 1) * 128], CONDT[:, kc, :],
                       kc == 0, kc == KC - 1, [tb, tCOND], [tp])
            tt(MODT[:, l, b * 4:(b + 1) * 4, :], pt[:, 0:8].rearrange("p (a c) -> p a c", c=2),
               ADAB[:, l, b * 4:(b + 1) * 4].unsqueeze(2).to_broadcast([C, 4, 2]), ALU.add, [tp, tADAB], [tMOD])

    def ada_finish(l):
        for c in range(2):
            for n in range(3):
                ts(AV[:, l, c, n, :], MODT[:, l, (3 * n + 1) * 8:(3 * n + 2) * 8, c], 1.0, 32.0, ALU.add, ALU.mult, [tMOD], [tAV])
                tt(AV[:, l, c, n, :], AV[:, l, c, n, :], GT[:, l * 3 + n, :], ALU.mult, [tAV, tGT], [tAV])
                ts(GV[:, l, c, n, :], MODT[:, l, (3 * n + 2) * 8:(3 * n + 3) * 8, c], 0.5 if n != 1 else 1.0, None, ALU.mult, None, [tMOD], [tAV])

    for _ in ada_layer(0):
        pass
    FNA = sb("FNA", [C, KC], F32)
    ts(FNA[:], FNG[:], 32.0, None, ALU.mult, None, [tFNG], [tAV])

    def Avec(l, c, n, kc):
        return AV[:, l, c, n, kc:kc + 1]

    def Bvec(l, c, n, kc):
        return MODT[:, l, (3 * n) * 8 + kc, c:c + 1]

    def Gvec(l, c, n, kc):
        return GV[:, l, c, n, kc:kc + 1]

    LGR = sb("LGR", [C, 2, 4], F32); tLG = T(None)
    LGQ = sb("LGQ", [C, 2, 2], F32)
    with ncdma():
        for d_, src in enumerate((rdf, rdb)):
            P.dma("sp", LGR[:, d_, :], src[0, :].partition_broadcast(C), writes=[tLG])
            for c in range(2):
                for hh in range(2):
                    P.dma("sp", LGQ[hh * 64:(hh + 1) * 64, d_, c:c + 1],
                          src[0, 2 * c + hh:2 * c + hh + 1].partition_broadcast(64), writes=[tLG])
    for tbuf in (LGR, LGQ):
        act(tbuf[:], tbuf[:], AF.Exp, [tLG], [tLG], scale=-1.0)
        act(tbuf[:], tbuf[:], AF.Ln, [tLG, tCST], [tLG], bias=CST[:, 2:3])
        ts(tbuf[:], tbuf[:], -1.0, None, ALU.mult, None, [tLG], [tLG])
    DT = sb("DT", [C, 4, C], BF16); tDT = T(None)
    DTMP = sb("DTMP", [C, C], F32)
    for hh in range(4):
        ts(DTMP[:], TABS[:, 0, :], LGR[:, 0, hh:hh + 1], None, ALU.mult, None, [tTABS, tLG], [tDT])
        stt(DTMP[:], TABS[:, 1, :], LGR[:, 1, hh:hh + 1], DTMP[:], ALU.mult, ALU.add, [tTABS, tLG, tDT], [tDT])
        act(DTMP[:], DTMP[:], AF.Exp, [tDT], [tDT])
        tt(DT[:, hh, :], DTMP[:], TABS[:, 2, :], ALU.mult, [tDT, tTABS], [tDT])
    DQ = sb("DQ", [C, 2, 2, C], F32)
    for d_ in range(2):
        for c in range(2):
            act(DQ[:, d_, c, :], TABS[:, 3 + d_, :], AF.Exp, [tTABS, tLG, tDT], [tDT], scale=LGQ[:, d_, c:c + 1])
    ts(DQ[:], DQ[:], 0.125, None, ALU.mult, None, [tDT], [tDT])
    DK = sb("DK", [C, 2, 4], F32)
    CD = sb("CD", [C, 2, 4], F32)
    for d_ in range(2):
        act(DK[:, d_, :], LGR[:, d_, :], AF.Exp, [tLG, tMCOL, tDT], [tDT], scale=MCOL[:, d_:d_ + 1])
        act(CD[:, d_, :], LGR[:, d_, :], AF.Exp, [tLG, tDT], [tDT], scale=float(C))
    WST = sb("WST", [C, 4, C], BF16); tWST = T(None)
    SGB = sb("SGB", [C, 4, C], F32)
    WSL = XST[0][:, 0:512].rearrange("p (g m) -> p g m", g=4)
    P.dma("sp", WSL, sgu_w.rearrange("g p m -> p g m"), writes=[tXST[0]])
    with ncdma():
        for g_ in range(4):
            P.dma("sp", SGB[:, g_, :], sgu_b[g_, :].partition_broadcast(C), writes=[tWST])
    pt, tp = ps()
    for g_ in range(4):
        tr(pt[:, g_ * C:(g_ + 1) * C], WSL[:, g_, :], [tXST[0]], [tp])
    cp("dve", WST[:], pt[:].rearrange("p (g m) -> p g m", g=4), [tp], [tWST])
    LAM = sb("LAM", [C, 4, 64], F32); tLAM = T(None)
    LAMV = sb("LAMV", [C, 4], F32)
    SUBG = sb("SUBG", [C, 1], F32)
    with ncdma():
        P.dma("sp", LAM[:].rearrange("p a b -> p (a b)"), diff_lambda.rearrange("a b -> (a b)").partition_broadcast(C), writes=[tLAM])
        P.dma("sp", SUBG[:], diff_subln[0, :].rearrange("(p o) -> p o", o=1), writes=[tLAM])
    tt(LAM[:, 0, :], LAM[:, 0, :], LAM[:, 1, :], ALU.mult, [tLAM], [tLAM])
    tt(LAM[:, 2, :], LAM[:, 2, :], LAM[:, 3, :], ALU.mult, [tLAM], [tLAM])
    P.op("dve", lambda e: e.reduce_sum(out=LAMV[:, 0:1], in_=LAM[:, 0, :], axis=AX.X), [tLAM], [tLAM])
    P.op("dve", lambda e: e.reduce_sum(out=LAMV[:, 1:2], in_=LAM[:, 2, :], axis=AX.X), [tLAM], [tLAM])
    act(LAMV[:, 0:2], LAMV[:, 0:2], AF.Exp, [tLAM], [tLAM])
    tt(LAMV[:, 2:3], LAMV[:, 1:2], LAMV[:, 0:1], ALU.subtract, [tLAM], [tLAM])
    ts(LAMV[:, 2:3], LAMV[:, 2:3], -LAMBDA_INIT, None, ALU.add, None, [tLAM], [tLAM])
    ts(SUBG[:], SUBG[:], (1.0 - LAMBDA_INIT) * math.sqrt(128.0), None, ALU.mult, None, [tLAM], [tLAM])
    NEGLAM = LAMV[:, 2:3]

    def wspec_ffn_in(l, f):
        W = ffn_w_in[l, f]
        out = []
        for i0 in range(0, HC, 4):
            n = min(4, HC - i0)
            out.append(("ffi", l, f, i0, n, KC, 2 * n * C,
                        [(W[:, i0 * C:(i0 + n) * C], 0), (W[:, DFF + i0 * C:DFF + (i0 + n) * C], n * C)]))
        return out

    def wspec_ffn_out(l, f):
        W = ffn_w_out[l, f]
        return [("ffo", l, f, b, 0, HC, 256, [(W[:, b * 256:(b + 1) * 256], 0)]) for b in range(4)]

    def wspec_cols(tag, W, c0, n):
        return [(tag, c0, n, 0, 0, KC, n, [(W[:, c0:c0 + n], 0)])]

    planA = wspec_ffn_in(0, 0) + wspec_ffn_out(0, 0) + wspec_cols("evkv", even_w_in, 256, 768)
    planB = (wspec_cols("ev1", even_w_in, 0, 1024) + wspec_cols("ev2", even_w_in, 1024, 1024)
             + wspec_cols("ev3", even_w_in, 2048, 512) + wspec_cols("evo", even_w_out, 0, 1024)
             + wspec_ffn_in(0, 1) + wspec_ffn_out(0, 1) + wspec_ffn_in(1, 0) + wspec_ffn_out(1, 0)
             + wspec_cols("od1", odd_w_in, 0, 1024) + wspec_cols("od2", odd_w_in, 1024, 1024)
             + wspec_cols("od3", odd_w_in, 2048, 1024))
    planC = wspec_cols("odo", odd_w_out, 0, 1024) + wspec_ffn_in(1, 1) + wspec_ffn_out(1, 1)
    all_tiles = [(g, ti) for g in groups for ti in range(len(g["tiles"]))]
    passes = "ABC" if stop_after is None else "ABC"[:"ABC".index(stop_after) + 1]
    plan = []
    for ps_ in passes:
        for _ in all_tiles:
            plan += {"A": planA, "B": planB, "C": planC}[ps_]
    wstate = dict(issued=0, used=0)

    def w_issue():
        i = wstate["issued"]
        if i >= len(plan):
            return
        spec = plan[i]
        kcn, ncols = spec[5], spec[6]
        buf = WB[i % 3][:, 0:kcn * ncols].rearrange("p (k m) -> p k m", k=kcn)
        for src, off in spec[7]:
            n = src.shape[1]
            P.dma("pool", buf[:, :, off:off + n], src.rearrange("(k p) m -> p k m", p=C), writes=[wbT[i % 3]])
        wstate["issued"] = i + 1

    def w_prefetch():
        while wstate["issued"] < min(wstate["used"] + 3, len(plan)):
            w_issue()

    def w_next(tag, ahead=2):
        i = wstate["used"]
        spec = plan[i]
        assert spec[0] == tag, (spec[0], tag)
        while wstate["issued"] < min(i + 1 + ahead, len(plan)):
            w_issue()
        wstate["used"] = i + 1
        kcn, ncols = spec[5], spec[6]
        return WB[i % 3][:, 0:kcn * ncols].rearrange("p (k m) -> p k m", k=kcn), wbT[i % 3], spec

    arena_hist = [T(None)]

    class Carver:
        def __init__(self):
            self.off = 0
            self.ts = []

        def take(self, shape, dt):
            nbytes = int(np.prod(shape[1:])) * (2 if dt == BF16 else 4)
            a = ARENA[:, self.off:self.off + nbytes].bitcast(dt)
            self.off += (nbytes + 63) // 64 * 64
            assert self.off <= ARENA_B, self.off
            if len(shape) > 2:
                names = "abcde"[:len(shape) - 1]
                kw = {names[i]: shape[1 + i] for i in range(len(shape) - 2)}
                a = a.rearrange("p (" + " ".join(names) + ") -> p " + " ".join(names), **kw)
            t = T(None).inherit(arena_hist[0])
            self.ts.append(t)
            return a, t

        def done(self):
            arena_hist[0] = T(None).inherit(arena_hist[0], *self.ts)

    arena_hist[0].inherit(*tAW)

    def ada_layer_late(l):
        for t_ in tAW:
            t_.inherit(arena_hist[0])
        for _ in ada_layer(l, "pool"):
            yield
        arena_hist[0] = T(None).inherit(arena_hist[0], *tAW)

    SQ = sb("SQ", [C, KC, 512], BF16); tSQ = T(None)
    RS = [sb(f"RS{i}", [C, 512], F32) for i in range(2)]; tRS = [T(None), T(None)]
    TMP = [sb(f"TMP{i}", [C, 512], F32) for i in range(4)]; tTMP = [T(None) for _ in range(4)]
    tmp_rr = [0]

    def tmp():
        i = tmp_rr[0]
        tmp_rr[0] = (i + 1) % 4
        return TMP[i], tTMP[i]

    rs_rr = [0]

    def rstd_of(src_ap, n, width, nfeat, reads):
        i = rs_rr[0]
        rs_rr[0] ^= 1
        act(SQ[:, 0:n, 0:width], src_ap, AF.Square, reads, [tSQ])
        pt, tp = ps()
        for kc in range(n):
            mm(pt[:, 0:width], ONES[:], SQ[:, kc, 0:width], kc == 0, kc == n - 1, [tONES, tSQ], [tp])
        act(RS[i][:, 0:width], pt[:, 0:width], AF.Ln, [tp, tCST], [tRS[i]], bias=CST[:, 0:1] if nfeat == D else CST[:, 1:2])
        act(RS[i][:, 0:width], RS[i][:, 0:width], AF.Exp, [tRS[i]], [tRS[i]], scale=-0.5)
        return RS[i], tRS[i]

    def norm_tile(l, c, n, nsub):
        pts = []
        for s in range(nsub):
            cols = slice(s * 512, (s + 1) * 512)
            act(H[:, :, cols], X[:, :, cols], AF.Square, [xT[s]], [hT[s]])
        for s in range(nsub):
            cols = slice(s * 512, (s + 1) * 512)
            pt, tp = ps()
            for kc in range(KC):
                mm(pt[:], ONES[:], H[:, kc, cols], kc == 0, kc == KC - 1, [tONES, hT[s]], [tp])
            pts.append((pt, tp))
        for s in range(nsub):
            pt, tp = pts[s]
            act(RS[s][:], pt[:], AF.Ln, [tp, tCST], [tRS[s]], bias=CST[:, 0:1])
            act(RS[s][:], RS[s][:], AF.Exp, [tRS[s]], [tRS[s]], scale=-0.5)
        for s in range(nsub):
            cols = slice(s * 512, (s + 1) * 512)
            for kc in range(KC):
                tm, ttm = tmp()
                stt(tm[:], X[:, kc, cols], Avec(l, c, n, kc), RS[s][:], ALU.mult, ALU.mult, [xT[s], tRS[s], tAV], [ttm])
                act(H[:, kc, cols], tm[:], AF.Identity, [ttm, tMOD], [hT[s]], bias=Bvec(l, c, n, kc))

    def norm_mod(l, c, n, s, dst, tdst):
        cols = slice(s * 512, (s + 1) * 512)
        rs, trs = rstd_of(X[:, :, cols], KC, 512, D, [xT[s]])
        for kc in range(KC):
            tm, ttm = tmp()
            stt(tm[:], X[:, kc, cols], Avec(l, c, n, kc), rs[:], ALU.mult, ALU.mult, [xT[s], trs, tAV], [ttm])
            act(dst[:, kc, cols], tm[:], AF.Identity, [ttm, tMOD], [tdst], bias=Bvec(l, c, n, kc))

    def ffn(l, f, c, nsub):
        cv = Carver()
        HID, _ = cv.take([C, HC, 1024], BF16)
        tHID = [T(None).inherit(arena_hist[0]) for _ in range(2)]
        cv.ts += tHID
        for (i0_) in range(0, HC, 4):
            wb, twb, spec = w_next("ffi")
            n = spec[4]
            for s in range(nsub):
                for j in range(n):
                    cols = slice(s * 512, (s + 1) * 512)
                    pg, tpg = ps()
                    pu, tpu = ps()
                    for kc in range(KC):
                        mm(pg[:], wb[:, kc, j * C:(j + 1) * C], H[:, kc, cols], kc == 0, kc == KC - 1, [twb, hT[s]], [tpg])
                    for kc in range(KC):
                        mm(pu[:], wb[:, kc, (n + j) * C:(n + j + 1) * C], H[:, kc, cols], kc == 0, kc == KC - 1, [twb, hT[s]], [tpu])
                    tm, ttm = tmp()
                    act(tm[:], pg[:], AF.Silu, [tpg], [ttm])
                    tt(HID[:, i0_ + j, cols], tm[:], pu[:], ALU.mult, [ttm, tpu], [tHID[s]])
        for b in range(4):
            wb, twb, spec = w_next("ffo")
            for o2 in range(2):
                oc = 2 * b + o2
                for s in range(nsub):
                    cols = slice(s * 512, (s + 1) * 512)
                    po, tpo = ps()
                    for i in range(HC):
                        mm(po[:], wb[:, i, o2 * C:(o2 + 1) * C], HID[:, i, cols], i == 0, i == HC - 1, [twb, tHID[s]], [tpo])
                    stt(X[:, oc, cols], po[:], Gvec(l, c, 0 if f == 0 else 2, oc), X[:, oc, cols], ALU.mult, ALU.add,
                        [tpo, tAV, xT[s]], [xT[s]])
        cv.done()

    xst_rr = [0]

    def load_x_tm(g, t0, nsub):
        for ch in range(nsub * 4):
            i = xst_rr[0]
            xst_rr[0] ^= 1
            s = ch // 4
            P.dma("sp", XST[i][:], g["xin"][t0 + ch * C:t0 + (ch + 1) * C, :], writes=[tXST[i]])
            for half in range(2):
                pt, tp = ps()
                for q in range(4):
                    kc = half * 4 + q
                    tr(pt[:, q * C:(q + 1) * C], XST[i][:, kc * C:(kc + 1) * C], [tXST[i]], [tp])
                cpa(X[:, half * 4:(half + 1) * 4, ch * C:(ch + 1) * C], pt[:].rearrange("p (a b) -> p a b", a=4), [tp], [xT[s]])

    rope_pending = []

    def rope_flush():
        while rope_pending:
            rope_pending.pop(0)()

    def rope(src_ps, tsrc, s_tok0, dst32, tdst, g, post=None):
        if not g["latent"]:
            cpa(dst32, src_ps, [tsrc], [tdst])
            if post is not None:
                post()
            return
        tm, ttm = tmp()
        cp("act", tm[:], src_ps, [tsrc], [ttm])

        def stage2():
            pr, tpr = ps()
            mm(pr[:], PROT[:], tm[:], True, True, [tPROT, ttm], [tpr])
            tt(dst32, tm[:], ROPE[:, 0, :], ALU.mult, [ttm, tROPE], [tdst])
            tm2, ttm2 = tmp()
            tt(tm2[:], pr[:], ROPE[:, 1, :], ALU.mult, [tpr, tROPE], [ttm2])
            tt(dst32, dst32, tm2[:], ALU.add, [ttm2, tdst], [tdst])
            if post is not None:
                post()

        rope_flush()
        rope_pending.append(stage2)

    ROPE = sb("ROPE", [C, 2, 512], F32); tROPE = T(None)

    def load_rope(g, tok0):
        if g["latent"]:
            P.dma("sp", ROPE[:, 0, :], c_cos[:, tok0:tok0 + 512], writes=[tROPE])
            P.dma("sp", ROPE[:, 1, :], c_sin[:, tok0:tok0 + 512], writes=[tROPE])

    SF = sb("SF", [64, 4, C], F32); tSF = T(None)
    SBK = sb("SBK", [64, 4, C], F32); tSBK = T(None)
    STG16 = [sb(f"STG16_{i}", [64, 512], BF16) for i in range(2)]; tSTG16 = [T(None), T(None)]
    STG32 = [sb(f"STG32_{i}", [64, 512], F32) for i in range(2)]; tSTG32 = [T(None), T(None)]
    stg_rr = [0, 0]

    def pass_a(g, ti):
        t0, nsub = g["tiles"][ti]
        ci = g["ci"]
        load_x_tm(g, t0, nsub)
        norm_tile(0, ci, 0, nsub)
        ffn(0, 0, ci, nsub)
        for s in range(nsub):
            P.dma("sp", g["X1"][ti, :, :, s * 512:(s + 1) * 512], X[:, :, s * 512:(s + 1) * 512], reads=[xT[s]], writes=[g["tX1"][ti]])
        norm_tile(0, ci, 1, nsub)
        wb, twb, spec = w_next("evkv")
        cv = Carver()
        KR, tKR = cv.take([C, 2, 512], F32)
        KF, tKF = cv.take([C, 2, 4, 64], BF16)
        VT, tVT = cv.take([C, 512], BF16)
        for s in range(nsub):
            cols = slice(s * 512, (s + 1) * 512)
            load_rope(g, t0 + s * 512)
            for c in range(2):
                pk, tpk = ps()
                for kc in range(KC):
                    mm(pk[:], wb[:, kc, c * C:(c + 1) * C], H[:, kc, cols], kc == 0, kc == KC - 1, [twb, hT[s]], [tpk])
                rope(pk[:], tpk, t0 + s * 512, KR[:, c, :], tKR, g)
            rope_flush()
            for ch in range(4):
                n = (t0 + s * 512) // C + ch
                seq_chunks = g["seqlen"] // C
                first = (n % seq_chunks == 0)
                last = (n % seq_chunks == seq_chunks - 1)
                tcols = slice(ch * C, (ch + 1) * C)
                pkt, tpkt = ps()
                for c in range(2):
                    tr(pkt[:, c * C:(c + 1) * C], KR[:, c, tcols], [tKR], [tpkt])
                for d_ in range(2):
                    tt(KF[:, d_, :, :], pkt[:, 0:256].rearrange("p (h d) -> p h d", h=4),
                       DK[:, d_, :].unsqueeze(2).to_broadcast([C, 4, 64]), ALU.mult, [tpkt, tDT], [tKF])
                pv, tpv = ps()
                for kc in range(KC):
                    mm(pv[:], H[:, kc, s * 512 + ch * C:s * 512 + (ch + 1) * C], wb[:, kc, 256:768], kc == 0, kc == KC - 1, [twb, hT[s]], [tpv])
                cp("act", VT[:], pv[:], [tpv], [tVT])
                pu_ = []
                for d_ in range(2):
                    pu, tpu = ps()
                    for hh in range(4):
                        mm(pu[0:64, hh * C:(hh + 1) * C], KF[:, d_, hh, :], VT[:, hh * C:(hh + 1) * C], True, True, [tKF, tVT], [tpu])
                    pu_.append((pu, tpu))
                for d_, (UPD, tUPD) in enumerate(((g["UPDF"], g["tUPDF"]), (g["UPDB"], g["tUPDB"]))):
                    i32 = stg_rr[1]; stg_rr[1] ^= 1
                    cp("dve", STG32[i32][:], pu_[d_][0][0:64, :], [pu_[d_][1]], [tSTG32[i32]])
                    P.dma("sp", UPD[n], STG32[i32][:], reads=[tSTG32[i32]], writes=[tUPD[n]])
        cv.done()

    tOUT = T(None)

    scan_hook = [None, 0]

    def scan(g, d_, init, store, final):
        S, tS = (SF, tSF) if d_ == 0 else (SBK, tSBK)
        UPD, tUPD = (g["UPDF"], g["tUPDF"]) if d_ == 0 else (g["UPDB"], g["tUPDB"])
        SP, tSP = (g["SPF"], g["tSPF"]) if d_ == 0 else (g["SPB"], g["tSPB"])
        seq_chunks = g["seqlen"] // C
        order = range(g["nch"]) if d_ == 0 else range(g["nch"] - 1, -1, -1)
        Sf = S[:].rearrange("d h e -> d (h e)")
        for n in order:
            pos = n % seq_chunks
            first = (pos == 0) if d_ == 0 else (pos == seq_chunks - 1)
            last = (pos == seq_chunks - 1) if d_ == 0 else (pos == 0)
            if first:
                init(S, tS, d_)
            if store:
                i16 = stg_rr[0]; stg_rr[0] ^= 1
                cp("act", STG16[i16][:], Sf, [tS], [tSTG16[i16]])
                P.dma("sp", SP[n], STG16[i16][:], reads=[tSTG16[i16]], writes=[tSP[n]])
            ub, tub = tmp()
            P.dma("pool", ub[0:64, :], UPD[n], reads=[tUPD[n]], writes=[tub])
            tt(S[:], S[:], CD[0:64, d_, :].unsqueeze(2).to_broadcast([64, 4, C]), ALU.mult, [tS, tDT], [tS])
            tt(Sf, Sf, ub[0:64, :], ALU.add, [tS, tub], [tS])
            if last and final is not None:
                final(n // seq_chunks, S, tS, d_)
            if scan_hook[0] is not None:
                scan_hook[1] += 1
                if scan_hook[1] % 2 == 0:
                    try:
                        next(scan_hook[0])
                    except StopIteration:
                        scan_hook[0] = None

    def init_zero(S, tS, d_):
        P.op("dve", lambda e: e.memset(S[:], 0.0), [], [tS])

    def init_flag(S, tS, d_):
        P.dma("sp", S[:], (s0f, s0b)[d_].rearrange("h d e -> d h e"), writes=[tS])
        ts(S[:], S[:], FLAG[0:64, d_:d_ + 1], None, ALU.mult, None, [tS, tMCOL], [tS])

    def init_mix(S, tS, d_):
        init_flag(S, tS, d_)
        src = CALL[0:64, :] if d_ == 0 else CALL[192:256, :]
        CARRY, tCARRY = tmp()
        P.dma("sp", CARRY[0:64, :], src, reads=[tCALL], writes=[tCARRY])
        Sf = S[:].rearrange("d h e -> d (h e)")
        stt(Sf, CARRY[0:64, :], FLAG[0:64, 2 + d_:3 + d_], Sf, ALU.mult, ALU.add, [tCARRY, tS, tMCOL], [tS])

    def final_out(q, S, tS, d_):
        P.dma("sp", (o_rf, o_rb)[d_][q].rearrange("h d e -> d h e"), S[:], reads=[tS], writes=[tOUT])

    def final_carry(q, S, tS, d_):
        P.dma("sp", CSEND[d_ * 64:(d_ + 1) * 64, :], S[:].rearrange("d h e -> d (h e)"), reads=[tS], writes=[tCSEND])

    for g, ti in all_tiles:
        pass_a(g, ti)
    gp, gs = groups
    if "B" in passes:
        w_prefetch()
    scan_hook[0] = ada_layer_late(1)
    scan(gp, 0, init_zero, True, final_out)
    scan(gp, 1, init_zero, True, final_out)
    scan(gs, 0, init_flag, False, final_carry)
    scan(gs, 1, init_flag, False, final_carry)
    P.cc("AllGather", ALU.bypass, PAIRS, CSEND, CALL, reads=[tCSEND], writes=[tCALL])
    if scan_hook[0] is not None:
        for _ in scan_hook[0]:
            pass
        scan_hook[0] = None

    def sample_scans():
        scan(gs, 0, init_mix, True, None)
        scan(gs, 1, init_mix, True, None)

    def gelu_to(dst, tdst, src_ps, tsrc, width=512):
        act(dst, src_ps, AF.Gelu_apprx_tanh, [tsrc], [tdst])

    SS4 = sb("SS4", [C, 8], F32); tSS4 = T(None)
    OST = XST; tOST = tXST
    VST = [sb(f"VST{i}", [C, D], BF16) for i in range(2)]; tVST = [T(None), T(None)]
    ost_rr = [0, 0]
    ost_rr = xst_rr + [0]

    def pass_b(g, ti):
        t0, nsub = g["tiles"][ti]
        ci = g["ci"]
        lat = g["latent"]
        for s in range(nsub):
            cols = slice(s * 512, (s + 1) * 512)
            P.dma("sp", X[:, :, cols], g["X1"][ti, :, :, cols], reads=[g["tX1"][ti]], writes=[xT[s]])
        norm_tile(0, ci, 1, nsub)
        w1, tw1, _ = w_next("ev1", 2)
        w2, tw2, _ = w_next("ev2", 0)
        w3, tw3, _ = w_next("ev3", 0)
        cv = Carver()
        QR, tQR = cv.take([C, 2, 512], F32)
        KR, tKR = cv.take([C, 2, 512], F32)
        QS, tQS = cv.take([C, 3, 2, 512], BF16)
        KM, tKB = cv.take([C, 4, 512], BF16)
        VT, tVT = cv.take([C, 4, 512], BF16)
        SG, tSG = cv.take([C, 4, 512], BF16)
        GU, tGU = cv.take([C, 4, 512], BF16)
        VN, tVN = cv.take([C, 4, 512], BF16)
        PM, tPM0 = cv.take([C, 2, 4, C], BF16)
        tPM = [tPM0, T(None).inherit(arena_hist[0])]
        cv.ts.append(tPM[1])
        SPL, tSPL = cv.take([C, 4, 2, 4, C], BF16)
        P.op("dve", lambda e: e.memset(SPL[:], 0.0), [], [tSPL])
        for s in range(nsub):
            cols = slice(s * 512, (s + 1) * 512)
            tok0 = t0 + s * 512
            load_rope(g, tok0)
            for ch in range(4):
                n = tok0 // C + ch
                for d_, (SP, tSP) in enumerate(((g["SPF"], g["tSPF"]), (g["SPB"], g["tSPB"]))):
                    for r in range(2):
                        P.dma("sp", SPL[r * 64:(r + 1) * 64, ch, d_, :, :].rearrange("p (c r) e -> p c r e", r=2)[:, :, r, :],
                              SP[n].rearrange("d (c r e) -> d c r e", c=2, r=2)[:, :, r, :], reads=[tSP[n]], writes=[tSPL])
            ck(1)
            for c in range(2):
                pq, tpq = ps()
                for kc in range(KC):
                    mm(pq[:], w1[:, kc, c * C:(c + 1) * C], H[:, kc, cols], kc == 0, kc == KC - 1, [tw1, hT[s]], [tpq])
                rope(pq[:], tpq, tok0, QR[:, c, :], tQR, g)
                pk, tpk = ps()
                for kc in range(KC):
                    mm(pk[:], w1[:, kc, 256 + c * C:256 + (c + 1) * C], H[:, kc, cols], kc == 0, kc == KC - 1, [tw1, hT[s]], [tpk])
                rope(pk[:], tpk, tok0, KR[:, c, :], tKR, g)
            rope_flush()
            for hh in range(4):
                ts(KM[:, hh, :], KR[:, hh // 2, :], MCOL[:, 2 + hh % 2:3 + hh % 2], None, ALU.mult, None, [tKR, tMCOL], [tKB])
            ts(QS[:, 0, :, :], QR[:], 0.125, None, ALU.mult, None, [tQR], [tQS])
            for d_ in range(2):
                for c in range(2):
                    tt(QS[:, 1 + d_, c, :].rearrange("p (a j) -> p a j", a=4), QR[:, c, :].rearrange("p (a j) -> p a j", a=4),
                       DQ[:, d_, c, :].unsqueeze(1).to_broadcast([C, 4, C]), ALU.mult, [tQR, tDT], [tQS])
            ck(2)
            for ch in range(4):
                tcols = slice(s * 512 + ch * C, s * 512 + (ch + 1) * C)
                pv, tpv = ps()
                for kc in range(KC):
                    mm(pv[:], H[:, kc, tcols], w1[:, kc, 512:1024], kc == 0, kc == KC - 1, [tw1, hT[s]], [tpv])
                cp("act", VT[:, ch, :], pv[:], [tpv], [tVT])
                pvs, tpvs = ps()
                for kc in range(KC):
                    mm(pvs[:], H[:, kc, tcols], w3[:, kc, 0:512], kc == 0, kc == KC - 1, [tw3, hT[s]], [tpvs])
                gv, tgv = tmp()
                gelu_to(gv[:], tgv, pvs[:], tpvs)
                sq, tsq = tmp()
                tt(sq[:], gv[:], gv[:], ALU.mult, [tgv], [tsq])
                P.op("dve", lambda e, sq=sq: e.reduce_sum(out=SS4[:, 0:4], in_=sq[:].rearrange("p (g c) -> p g c", g=4), axis=AX.X), [tsq], [tSS4])
                act(SS4[:, 0:4], SS4[:, 0:4], AF.Sqrt, [tSS4, tCST], [tSS4], bias=CST[:, 3:4], scale=1.0 / C)
                recip(SS4[:, 0:4], SS4[:, 0:4], [tSS4], [tSS4])
                tt(VN[:, ch, :].rearrange("p (g c) -> p g c", g=4), gv[:].rearrange("p (g c) -> p g c", g=4),
                   SS4[:, 0:4].unsqueeze(2).to_broadcast([C, 4, C]), ALU.mult, [tgv, tSS4], [tVN])
            ck(3)
            for c4 in range(4):
                pg, tpg = ps()
                for kc in range(KC):
                    mm(pg[:], w2[:, kc, c4 * C:(c4 + 1) * C], H[:, kc, cols], kc == 0, kc == KC - 1, [tw2, hT[s]], [tpg])
                act(SG[:, c4, :], pg[:], AF.Silu, [tpg], [tSG])
                pu, tpu = ps()
                for kc in range(KC):
                    mm(pu[:], w2[:, kc, 512 + c4 * C:512 + (c4 + 1) * C], H[:, kc, cols], kc == 0, kc == KC - 1, [tw2, hT[s]], [tpu])
                gelu_to(GU[:, c4, :], tGU, pu[:], tpu)
            ck(4)
            po = ps_reserve(4)
            for ch in range(4):
                chc = slice(ch * C, (ch + 1) * C)
                sc, tsc = ps()
                for hh in range(4):
                    c, base = hh // 2, (hh % 2) * 64
                    mm(sc[:, hh * C:(hh + 1) * C], KM[:, hh, chc], QS[:, 0, c, chc], True, True, [tKB, tQS], [tsc])
                pmi = ch % 2
                tt(PM[:, pmi, :, :], sc[:].rearrange("p (h j) -> p h j", h=4), DT[:], ALU.mult, [tsc, tDT], [tPM[pmi]])
                ck(41)
                for hh in range(4):
                    c, base = hh // 2, (hh % 2) * 64
                    o_, to_, _ = po[hh]
                    mm(o_[:, chc], SPL[:, ch, 0, hh, :], QS[:, 1, c, chc], True, False, [tSPL, tQS], [to_])
                    ck(42)
                    mm(o_[:, chc], SPL[:, ch, 1, hh, :], QS[:, 2, c, chc], False, False, [tSPL, tQS], [to_])
                    mm(o_[:, chc], VT[:, ch, hh * C:(hh + 1) * C], PM[:, pmi, hh, :], False, True, [tVT, tPM[pmi]], [to_])
                    ck(43)
            for hh in range(4):
                o_, to_, _ = po[hh]
                rs, trs = rstd_of(o_[:].unsqueeze(1), 1, 512, C, [to_])
                tm, ttm = tmp()
                tt(tm[:], o_[:], rs[:], ALU.mult, [to_, trs], [ttm])
                stt(H[:, hh, cols], tm[:], math.sqrt(float(C)), SG[:, hh, :], ALU.mult, ALU.mult, [ttm, tSG], [hT[s]])
            ps_release(po)
            ck(5)
            pg_ = ps_reserve(4)
            for ch in range(4):
                chc = slice(ch * C, (ch + 1) * C)
                for g_ in range(4):
                    o_, to_, _ = pg_[g_]
                    mm(o_[:, chc], VN[:, ch, g_ * C:(g_ + 1) * C], WST[:, g_, :], True, True, [tVN, tWST], [to_])
            for g_ in range(4):
                o_, to_, _ = pg_[g_]
                tm, ttm = tmp()
                tt(tm[:].rearrange("p (a j) -> p a j", a=4), o_[:].rearrange("p (a j) -> p a j", a=4),
                   SGB[:, g_, :].unsqueeze(1).to_broadcast([C, 4, C]), ALU.add, [to_, tWST], [ttm])
                tt(H[:, 4 + g_, cols], tm[:], GU[:, g_, :], ALU.mult, [ttm, tGU], [hT[s]])
            ps_release(pg_)
            ck(6)
        cv.done()
        wo, two, _ = w_next("evo")
        for oc in range(KC):
            for s in range(nsub):
                cols = slice(s * 512, (s + 1) * 512)
                pp, tpp = ps()
                for kc in range(KC):
                    mm(pp[:], wo[:, kc, oc * C:(oc + 1) * C], H[:, kc, cols], kc == 0, kc == KC - 1, [two, hT[s]], [tpp])
                stt(X[:, oc, cols], pp[:], Gvec(0, ci, 1, oc), X[:, oc, cols], ALU.mult, ALU.add, [tpp, tAV, xT[s]], [xT[s]])
        ck(7)
        norm_tile(0, ci, 2, nsub)
        ffn(0, 1, ci, nsub)
        ck(8)
        norm_tile(1, ci, 0, nsub)
        ffn(1, 0, ci, nsub)
        for s in range(nsub):
            cols = slice(s * 512, (s + 1) * 512)
            P.dma("sp", g["X4"][ti, :, :, cols], X[:, :, cols], reads=[xT[s]], writes=[g["tX4"][ti]])
        norm_tile(1, ci, 1, nsub)
        ck(9)
        koff = 0
        cv = Carver()
        R32b = [cv.take([C, 512], F32) for _ in range(2)]
        KST, tKST0 = cv.take([C, KC, 512], BF16)
        QST, tQST = cv.take([C, KC, 2, 512], BF16)
        tKST = [tKST0]
        for blk_i, tag in enumerate(("od1", "od2")):
            wb, twb, _ = w_next(tag)
            for s in range(nsub):
                cols = slice(s * 512, (s + 1) * 512)
                tok0 = t0 + s * 512
                load_rope(g, tok0)
                ks = tKST[s % 2] if False else tKST[0]
                for hh in range(8):
                    pq, tpq = ps()
                    for kc in range(KC):
                        mm(pq[:], wb[:, kc, hh * C:(hh + 1) * C], H[:, kc, cols], kc == 0, kc == KC - 1, [twb, hT[s]], [tpq])
                    r32, tr32 = R32b[hh % 2]

                    def post(hh=hh, r32=r32, tr32=tr32, blk_i=blk_i, ks=ks):
                        if blk_i == 0:
                            for j in range(2):
                                ts(QST[:, hh, j, :], r32[:], MCOL[:, 2 + j:3 + j], None, ALU.mult, None, [tr32, tMCOL], [tQST])
                        else:
                            cp("act", KST[:, hh, :], r32[:], [tr32], [ks])

                    rope(pq[:], tpq, tok0, r32[:], tr32, g, post)
                rope_flush()
                if blk_i == 0:
                    P.dma("sp", g["QFM"][ti, :, :, :, cols], QST[:], reads=[tQST], writes=[g["tQ"][ti]])
                else:
                    P.dma("sp", g["KFM"][:, :, koff + tok0:koff + tok0 + 512].rearrange("h p n -> p h n"), KST[:],
                          reads=[ks], writes=[g["tK"][hh][ti] for hh in range(8)])
                    if not lat:
                        for ch in range(4):
                            tcols = slice(s * 512 + ch * C, s * 512 + (ch + 1) * C)
                            i = ost_rr[0]; ost_rr[0] ^= 1
                            for half in range(2):
                                pk, tpk = ps()
                                for kc in range(KC):
                                    mm(pk[:], H[:, kc, tcols], wb[:, kc, half * 512:(half + 1) * 512], kc == 0, kc == KC - 1, [twb, hT[s]], [tpk])
                                cpa(OST[i][:, half * 512:(half + 1) * 512], pk[:], [tpk], [tOST[i]])
                            P.dma("sp", o_k[tok0 + ch * C:tok0 + (ch + 1) * C, :], OST[i][:], reads=[tOST[i]], writes=[tOUT])
        ck(10)
        wb, twb, _ = w_next("od3")
        for s in range(nsub):
            tok0 = t0 + s * 512
            for ch in range(4):
                tcols = slice(s * 512 + ch * C, s * 512 + (ch + 1) * C)
                i = ost_rr[0]; ost_rr[0] ^= 1
                iv = ost_rr[1]; ost_rr[1] ^= 1
                for half in range(2):
                    pv, tpv = ps()
                    for kc in range(KC):
                        mm(pv[:], H[:, kc, tcols], wb[:, kc, half * 512:(half + 1) * 512], kc == 0, kc == KC - 1, [twb, hT[s]], [tpv])
                    if not lat:
                        cp("dve", OST[i][:, half * 512:(half + 1) * 512], pv[:], [tpv], [tOST[i]])
                        cp("act", VST[iv][:, half * 512:(half + 1) * 512], OST[i][:, half * 512:(half + 1) * 512], [tOST[i]], [tVST[iv]])
                    else:
                        cp("act", VST[iv][:, half * 512:(half + 1) * 512], pv[:], [tpv], [tVST[iv]])
                kch = (koff + tok0) // C + ch
                P.dma("sp", g["VTM"][kch], VST[iv][:], reads=[tVST[iv]], writes=[g["tV"][kch]])
                if not lat:
                    P.dma("sp", o_v[tok0 + ch * C:tok0 + (ch + 1) * C, :], OST[i][:], reads=[tOST[i]], writes=[tOUT])
        cv.done()
        ck(11)

    def ctx_kv(g):
        cv = Carver()
        KST, tKST0 = cv.take([C, KC, 512], BF16)
        tKST = [tKST0]
        for j in range(PAST // C):
            i = xst_rr[0]; xst_rr[0] ^= 1
            P.dma("sp", XST[i][:], cache_k[j * C:(j + 1) * C, :], writes=[tXST[i]])
            for half in range(2):
                pt, tp = ps()
                for q in range(4):
                    hh = half * 4 + q
                    tr(pt[:, q * C:(q + 1) * C], XST[i][:, hh * C:(hh + 1) * C], [tXST[i]], [tp])
                cpa(KST[:, half * 4:(half + 1) * 4, 0:C], pt[:].rearrange("p (a b) -> p a b", a=4), [tp], [tKST[0]])
            P.dma("sp", KCTX[:, :, j * C:(j + 1) * C].rearrange("h p n -> p h n"), KST[:, :, 0:C],
                  reads=[tKST[0]], writes=[tKCTX])
            i = xst_rr[0]; xst_rr[0] ^= 1
            iv = ost_rr[1]; ost_rr[1] ^= 1
            P.dma("sp", XST[i][:], cache_v[j * C:(j + 1) * C, :], writes=[tXST[i]])
            cpa(VST[iv][:], XST[i][:], [tXST[i]], [tVST[iv]])
            P.dma("sp", VCTX[j], VST[iv][:], reads=[tVST[iv]], writes=[tVCTX])
        cv.done()

    def kv_gather(g):
        for i in range(NSPL):
            P.cc("AllGather", ALU.bypass, PAIRS, g["KFM"][i * HPS:(i + 1) * HPS].rearrange("h p n -> (h p) n"), KALL[i],
                 reads=[t for l in g["tK"] for t in l], writes=[tKALL])
            P.cc("AllGather", ALU.bypass, PAIRS, g["VTM"][i * CPS:(i + 1) * CPS].rearrange("k p e -> (k p) e"), VALL[i],
                 reads=g["tV"], writes=[tVALL])

    EB = [sb(f"EB{i}", [C, 512], BF16) for i in range(4)]; tEB = [T(None) for _ in range(4)]
    eb_rr = [0]

    def pass_c(g, ti):
        t0, nsub = g["tiles"][ti]
        ci = g["ci"]
        lat = g["latent"]
        ntile = len(g["tiles"])
        for s in range(nsub):
            cols = slice(s * 512, (s + 1) * 512)
            P.dma("sp", X[:, :, cols], g["X4"][ti, :, :, cols], reads=[g["tX4"][ti]], writes=[xT[s]])
        cv = Carver()
        QTb = [cv.take([C, 2, 1024], BF16) for _ in range(2)]
        nk = (PAST + 2 * LH) if lat else nsub * 512
        k0 = 0 if lat else t0
        nkc = nk // C
        KH = []; VH = []
        for i in range(2):
            a, ta = cv.take([C, nk], BF16)
            b, tb = cv.take([C, nkc, C], BF16)
            KH.append((a, ta)); VH.append((b, tb))
        A32, tA32 = cv.take([C, 512], F32)
        KALLv = [k_.rearrange("(r h p) n -> r h p n", r=2, h=HPS) for k_ in KALL]
        VALLv = [v_.rearrange("(r k p) e -> r k p e", r=2, p=C) for v_ in VALL]

        def load_kv(hh):
            (a, ta), (b, tb) = KH[hh % 2], VH[hh % 2]
            P.dma("sp", QTb[hh % 2][0][:, :, 0:nsub * 512], g["QFM"][ti, :, hh, :, 0:nsub * 512], reads=[g["tQ"][ti]], writes=[QTb[hh % 2][1]])
            if lat:
                pc = PAST // C
                lc = LH // C
                P.dma("sp", a[:, 0:PAST], KCTX[hh], reads=[tKCTX], writes=[ta])
                P.dma("sp", b[:, 0:pc, :], VCTX[:, :, hh * C:(hh + 1) * C].rearrange("k p e -> p k e"), reads=[tVCTX], writes=[tb])
                for r in range(2):
                    P.dma("sp", a[:, PAST + r * LH:PAST + (r + 1) * LH], KALLv[hh // HPS][r, hh % HPS], reads=[tKALL], writes=[ta])
                    for i in range(NSPL):
                        c0 = pc + r * lc + i * CPS
                        P.dma("sp", b[:, c0:c0 + CPS, :], VALLv[i][r, :, :, hh * C:(hh + 1) * C].rearrange("k p e -> p k e"),
                              reads=[tVALL], writes=[tb])
            else:
                P.dma("sp", a[:], g["KFM"][hh, :, k0:k0 + nk], reads=[g["tK"][hh][ti]], writes=[ta])
                P.dma("sp", b[:], g["VTM"][k0 // C:k0 // C + nkc, :, hh * C:(hh + 1) * C].rearrange("k p e -> p k e"),
                      reads=[g["tV"][k0 // C + j] for j in range(nkc)], writes=[tb])

        load_kv(0)
        for hh in range(8):
            if hh + 1 < 8:
                load_kv(hh + 1)
            (kh, tkh), (vh, tvh) = KH[hh % 2], VH[hh % 2]
            QT, tQT = QTb[hh % 2]
            if lat:
                qblocks = [(s * 512, 512, [(0, 512, list(range(nkc)))]) for s in range(nsub)]
            else:
                qblocks = [(s * 512, 512, [(0, 256, [4 * s, 4 * s + 1]), (256, 256, [4 * s + 2, 4 * s + 3])]) for s in range(nsub)]
            for (q0, N, parts) in qblocks:
                s = q0 // 512
                acc = ps_reserve(4)
                scs = {}
                items = []
                for (qa, n_, chunks) in parts:
                    for i, kc_ in enumerate(chunks):
                        items.append((qa, n_, kc_, i == 0, i == len(chunks) - 1))

                def issue_sc(idx):
                    qa, n_, kc_, _, _ = items[idx]
                    lst = []
                    for j in range(2):
                        sc, tsc = ps()
                        mm(sc[:, 0:n_], kh[:, kc_ * C:(kc_ + 1) * C], QT[:, j, q0 + qa:q0 + qa + n_], True, True, [tkh, tQT], [tsc])
                        lst.append((sc, tsc))
                    scs[idx] = lst

                issue_sc(0)
                for idx, (qa, n_, kc_, first, last) in enumerate(items):
                    if idx + 1 < len(items):
                        issue_sc(idx + 1)
                    for j in range(2):
                        sc, tsc = scs[idx][j]
                        e_i = eb_rr[0]; eb_rr[0] = (e_i + 1) % 4
                        act(EB[e_i][:, 0:n_], sc[:, 0:n_], AF.Exp, [tsc], [tEB[e_i]], scale=0.125)
                        mm(acc[j][0][:, qa:qa + n_], vh[:, kc_, :], EB[e_i][:, 0:n_], first, last, [tvh, tEB[e_i]], [acc[j][1]])
                        mm(acc[2 + j][0][:, qa:qa + n_], ONES[:], EB[e_i][:, 0:n_], first, last, [tONES, tEB[e_i]], [acc[2 + j][1]])
                    del scs[idx]
                r1, tr1 = tmp()
                act(r1[:, 0:N], acc[2][0][:, 0:N], AF.Ln, [acc[2][1]], [tr1])
                act(r1[:, 0:N], r1[:, 0:N], AF.Exp, [tr1], [tr1], scale=-1.0)
                t1, tt1 = tmp()
                tt(t1[:, 0:N], acc[0][0][:, 0:N], r1[:, 0:N], ALU.mult, [acc[0][1], tr1], [tt1])
                r2, tr2 = tmp()
                act(r2[:, 0:N], acc[3][0][:, 0:N], AF.Ln, [acc[3][1]], [tr2])
                act(r2[:, 0:N], r2[:, 0:N], AF.Exp, [tr2], [tr2], scale=-1.0)
                t2, tt2 = tmp()
                tt(t2[:, 0:N], acc[1][0][:, 0:N], r2[:, 0:N], ALU.mult, [acc[1][1], tr2], [tt2])
                stt(A32[:, 0:N], t2[:, 0:N], NEGLAM, t1[:, 0:N], ALU.mult, ALU.add, [tt2, tt1, tLAM], [tA32])
                ps_release(acc)
                rs, trs = rstd_of(A32[:, 0:N].unsqueeze(1), 1, N, C, [tA32])
                stt(H[:, hh, q0:q0 + N], A32[:, 0:N], SUBG[:, 0:1], rs[:, 0:N], ALU.mult, ALU.mult, [tA32, trs, tLAM], [hT[s]])
        wb, twb, _ = w_next("odo")
        for oc in range(KC):
            for s in range(nsub):
                cols = slice(s * 512, (s + 1) * 512)
                pp, tpp = ps()
                for kc in range(KC):
                    mm(pp[:], wb[:, kc, oc * C:(oc + 1) * C], H[:, kc, cols], kc == 0, kc == KC - 1, [twb, hT[s]], [tpp])
                stt(X[:, oc, cols], pp[:], Gvec(1, ci, 1, oc), X[:, oc, cols], ALU.mult, ALU.add, [tpp, tAV, xT[s]], [xT[s]])
        cv.done()
        norm_tile(1, ci, 2, nsub)
        ffn(1, 1, ci, nsub)
        cv = Carver()
        YF, tYF = cv.take([C, KC, 512], F32)
        for s in range(nsub):
            cols = slice(s * 512, (s + 1) * 512)
            rs, trs = rstd_of(X[:, :, cols], KC, 512, D, [xT[s]])
            for kc in range(KC):
                stt(YF[:, kc, :], X[:, kc, cols], FNA[:, kc:kc + 1], rs[:], ALU.mult, ALU.mult, [xT[s], trs, tAV], [tYF])
            for ch in range(4):
                i = ost_rr[0]; ost_rr[0] ^= 1
                for half in range(2):
                    pt, tp = ps()
                    for q in range(4):
                        kc = half * 4 + q
                        tr(pt[:, q * C:(q + 1) * C], YF[:, kc, ch * C:(ch + 1) * C], [tYF], [tp])
                    cpa(OST[i][:, half * 512:(half + 1) * 512], pt[:], [tp], [tOST[i]])
                r0 = t0 + s * 512 + ch * C
                P.dma("sp", g["yout"][r0:r0 + C, :], OST[i][:], reads=[tOST[i]], writes=[tOUT])
        cv.done()

    try:
        if "B" in passes:
            for i_, (g, ti) in enumerate(all_tiles):
                if g["latent"] and ti == 0:
                    sample_scans()
                    ctx_kv(groups[1])
                pass_b(g, ti)
        if "C" in passes:
            w_prefetch()
            kv_gather(groups[1])
            for g, ti in all_tiles:
                pass_c(g, ti)
    except StopBuild:
        pass

    P.fence("sp", [tOUT, tKALL, tVALL, tCALL, tKCTX, tVCTX, tCSEND] + wbT + xT + hT + psT + [arena_hist[0]] + [t for g in groups for t in g["tX1"] + g["tX4"] + g["tQ"] + g["tV"] + g["tSPF"] + g["tSPB"] + g["tUPDB"] + g["tUPDF"] + [x for l in g["tK"] for x in l]])
    P.emit()
    st.close()
    return nc, P


def host_consts(LS):
    f32 = np.float32
    ident = np.eye(C, dtype=f32)
    pm = np.zeros((C, C), f32)
    for base in range(0, C, 32):
        for i in range(16):
            pm[base + i, base + i + 16] = -1.0
            pm[base + 16 + i, base + i] = 1.0
    prot = np.ascontiguousarray(pm.T)
    t = np.arange(LS)
    row = (t // 64).astype(np.float64)
    col = (t % 64).astype(np.float64)
    freq = 10000.0 ** (-(np.arange(16, dtype=np.float64)) / 16.0)
    cos = np.zeros((C, LS), f32)
    sin = np.zeros((C, LS), f32)
    for f in range(C):
        fh = f % 64
        pos = row if (fh // 32) == 0 else col
        ang = pos * freq[fh % 16]
        cos[f] = np.cos(ang).astype(f32)
        sin[f] = np.sin(ang).astype(f32)
    m = np.arange(C)[:, None].astype(f32)
    j = np.arange(C)[None, :].astype(f32)
    tabs = np.zeros((C, 5, C), f32)
    tabs[:, 0, :] = np.maximum(j - m, 0)
    tabs[:, 1, :] = np.maximum(m - j, 0)
    tabs[:, 2, :] = 1.0 + np.eye(C, dtype=f32)
    tabs[:, 3, :] = j + 1.0
    tabs[:, 4, :] = C - j
    mcol = np.zeros((C, 4), f32)
    mcol[:, 0] = C - 1 - np.arange(C)
    mcol[:, 1] = np.arange(C)
    mcol[:64, 2] = 1.0
    mcol[64:, 3] = 1.0
    return dict(c_ident=ident, c_prot=prot, c_cos=cos, c_sin=sin, c_tabs=tabs, c_mcol=mcol)


def make_in_maps(inp, NPS, LS, PAST, ncores=8):
    cst = host_consts(LS)
    LH = LS // 2
    maps = []
    a = lambda x: np.ascontiguousarray(np.asarray(x, dtype=np.float32))
    shared = dict(
        ada_w=a(inp["ada_w"]), ada_b=a(inp["ada_b"]), norm_g=a(inp["norm_g"]).reshape(6, 1024),
        ffn_w_in=a(inp["ffn_w_in"]), ffn_w_out=a(inp["ffn_w_out"]),
        even_w_in=a(inp["even_w_in"])[0], even_w_out=a(inp["even_w_out"])[0],
        ret_decay_fwd=a(inp["ret_decay_fwd"]), ret_decay_bwd=a(inp["ret_decay_bwd"]),
        sgu_w=a(inp["sgu_w"])[0], sgu_b=a(inp["sgu_b"])[0],
        odd_w_in=a(inp["odd_w_in"])[0], odd_w_out=a(inp["odd_w_out"])[0],
        diff_lambda=a(inp["diff_lambda"])[0], diff_subln=a(inp["diff_subln"]),
        final_norm=a(inp["final_norm"]).reshape(1, 1024), **cst)
    xp = a(inp["x_prompt"]); xs = a(inp["x_sample"])
    nb = xs.shape[0]
    for c in range(ncores):
        b, r = c // 2, c % 2
        m = dict(shared)
        m["c_cos"] = np.ascontiguousarray(cst["c_cos"][:, r * LH:(r + 1) * LH])
        m["c_sin"] = np.ascontiguousarray(cst["c_sin"][:, r * LH:(r + 1) * LH])
        fl = np.zeros((C, 4), np.float32)
        fl[:, 0] = 1.0 - r
        fl[:, 1] = float(r)
        fl[:, 2] = 1.0 - fl[:, 0]
        fl[:, 3] = 1.0 - fl[:, 1]
        m["c_flag"] = fl
        m["x_prompt"] = np.ascontiguousarray(xp[c * NPS:(c + 1) * NPS].reshape(NPS * 256, 1024))
        m["x_sample"] = np.ascontiguousarray(xs[b, r * LH:(r + 1) * LH])
        m["state_f"] = a(inp["state_ret_fwd"])[b, 0]
        m["state_b"] = a(inp["state_ret_bwd"])[b, 0]
        m["cache_k"] = a(inp["cache_k"])[b, 0].reshape(PAST, 1024)
        m["cache_v"] = a(inp["cache_v"])[b, 0].reshape(PAST, 1024)
        m["cond"] = np.ascontiguousarray(np.stack([a(inp["c_ctx"]), a(inp["c"])[b]]))
        maps.append(m)
    return maps


def gather(results, NPS, LS, nb=4):
    yp = np.stack([r["y_prompt"].reshape(NPS, 256, 1024) for r in results]).reshape(-1, 256, 1024)
    ys = np.stack([np.concatenate([results[2 * b]["y_sample"], results[2 * b + 1]["y_sample"]]) for b in range(nb)])
    rf = np.concatenate([r["new_ret_f"] for r in results])[:, None]
    rb = np.concatenate([r["new_ret_b"] for r in results])[:, None]
    nk = np.concatenate([r["new_k"].reshape(NPS, 256, 8, 128) for r in results])[:, None]
    nv = np.concatenate([r["new_v"].reshape(NPS, 256, 8, 128) for r in results])[:, None]
    return (yp, ys, rf, rb, nk, nv)


_NPS, _LS, _PAST = 4, 4096, 256


def kernel(**inputs):
    nc, P = build(_NPS, _LS // 2, _PAST)
    maps = make_in_maps(inputs, _NPS, _LS, _PAST)
    res = run_bass_kernel_spmd(nc, maps, core_ids=list(range(8)))
    outs = gather(res.results, _NPS, _LS)
    return tuple(np.ascontiguousarray(o, dtype=np.float32) for o in outs)
```
